# Optimizing a Trainium2 kernel written in Bass

```python
import jax, jax.numpy as jnp
from jax import lax
import numpy as np


D_MODEL = 1024
BATCH = 4
SEQ = 4096
DEPTH = 4

CTX_LEN = 256
GRID_W = 64
N_Q_HEADS = 8
N_KV_HEADS = 2
HEAD_DIM = 64
WINDOW = 128
ATT_BLOCK = 128
ROPE_BASE = 10000.0
ATT_Q = N_Q_HEADS * HEAD_DIM
ATT_KV = N_KV_HEADS * HEAD_DIM
CONV_DIM = 512
CONV_WIDTH = 31
M_HEADS = 4
M_HEAD_DIM = 128
M_WIDTH = M_HEADS * M_HEAD_DIM
M_CHUNK = 64
M_SHORT_CONV = 3
N_GATE_COLS = 4 * M_HEADS
N_BRANCHES = 3
D_FF = -(-8 * D_MODEL // (3 * 256)) * 256

SPLIT_SIZES = (ATT_Q, ATT_KV, ATT_KV, 2 * CONV_DIM, 2 * M_WIDTH, M_WIDTH, M_WIDTH, N_GATE_COLS, N_BRANCHES * D_MODEL)
SPLIT_POINTS = tuple(sum(SPLIT_SIZES[:i + 1]) for i in range(len(SPLIT_SIZES) - 1))
D_IN = sum(SPLIT_SIZES)

EPS = 1e-6
NEG_INF = -1e30

kernel_name = 'hybrid_gated_swa_conformer_mlstm_dit'


def rms_norm(x, g):
    xf = x.astype(jnp.float32)
    y = xf * lax.rsqrt(jnp.mean(xf * xf, axis=-1, keepdims=True) + EPS)
    return (y * g.astype(jnp.float32)).astype(x.dtype)


def layer_norm(x, g, b):
    xf = x.astype(jnp.float32)
    mu = jnp.mean(xf, axis=-1, keepdims=True)
    var = jnp.mean(jnp.square(xf - mu), axis=-1, keepdims=True)
    y = (xf - mu) * lax.rsqrt(var + EPS)
    return (y * g.astype(jnp.float32) + b.astype(jnp.float32)).astype(x.dtype)


def modulate(h, shift, scale):
    return h * (1.0 + scale) + shift


def to_heads(a, n):
    B, T, _ = a.shape
    return a.reshape(B, T, n, -1).transpose(0, 2, 1, 3)


def from_heads(a):
    B, n, T, d = a.shape
    return a.transpose(0, 2, 1, 3).reshape(B, T, n * d)


def rope_tables(n_tokens):
    rows = n_tokens // GRID_W
    row = jnp.repeat(jnp.arange(rows, dtype=jnp.float32), GRID_W)
    col = jnp.tile(jnp.arange(GRID_W, dtype=jnp.float32), rows)
    n_freq = HEAD_DIM // 4
    inv_freq = ROPE_BASE ** (-jnp.arange(n_freq, dtype=jnp.float32) / n_freq)
    ang_r = row[:, None] * inv_freq
    ang_c = col[:, None] * inv_freq
    ang = jnp.concatenate([ang_r, ang_r, ang_c, ang_c], axis=-1)
    return jnp.cos(ang), jnp.sin(ang)


def rotate_half(h):
    a, b = jnp.split(h, 2, axis=-1)
    return jnp.concatenate([-b, a], axis=-1)


def apply_rope(x, cos, sin):
    half = HEAD_DIM // 2
    xr = jnp.concatenate([rotate_half(x[..., :half]), rotate_half(x[..., half:])], axis=-1)
    return x * cos.astype(x.dtype) + xr * sin.astype(x.dtype)


def depthwise_conv(x, w):
    pad = w.shape[0] // 2
    return lax.conv_general_dilated(
        x, w[:, None, :].astype(x.dtype), window_strides=(1,), padding=[(pad, pad)],
        dimension_numbers=('NWC', 'WIO', 'NWC'), feature_group_count=x.shape[-1])


def window_attention(q, k, v, k_ctx, v_ctx, sink):
    B, _, S, d = q.shape
    nb = S // ATT_BLOCK
    G = N_Q_HEADS // N_KV_HEADS
    qb = (q * d ** -0.5).reshape(B, N_KV_HEADS, G, nb, ATT_BLOCK, d)

    def band(a):
        ap = jnp.pad(a, ((0, 0), (0, 0), (ATT_BLOCK, ATT_BLOCK), (0, 0))).reshape(B, N_KV_HEADS, nb + 2, ATT_BLOCK, d)
        return jnp.concatenate([ap[:, :, :-2], ap[:, :, 1:-1], ap[:, :, 2:]], axis=3)

    kb, vb = band(k), band(v)
    s_loc = jnp.einsum('bhgnqd,bhnkd->bhgnqk', qb, kb).astype(jnp.float32)
    s_ctx = jnp.einsum('bhgnqd,bhkd->bhgnqk', qb, k_ctx).astype(jnp.float32)
    blk = jnp.arange(nb)[:, None, None]
    q_pos = blk * ATT_BLOCK + jnp.arange(ATT_BLOCK)[None, :, None]
    key_pos = (blk - 1) * ATT_BLOCK + jnp.arange(3 * ATT_BLOCK)[None, None, :]
    valid = (jnp.abs(key_pos - q_pos) <= WINDOW) & (key_pos >= 0) & (key_pos < S)
    s_loc = jnp.where(valid, s_loc, NEG_INF)
    sk = jnp.broadcast_to(sink.astype(jnp.float32).reshape(1, N_KV_HEADS, G, 1, 1, 1), s_loc.shape[:-1] + (1,))
    p = jax.nn.softmax(jnp.concatenate([s_loc, s_ctx, sk], axis=-1), axis=-1).astype(v.dtype)
    n_loc = 3 * ATT_BLOCK
    n_ctx = k_ctx.shape[2]
    o = (jnp.einsum('bhgnqk,bhnkd->bhgnqd', p[..., :n_loc], vb)
         + jnp.einsum('bhgnqk,bhkd->bhgnqd', p[..., n_loc:n_loc + n_ctx], v_ctx))
    return o.reshape(B, N_Q_HEADS, S, d)


def context_attention(q, k, v, sink):
    B, _, L, d = q.shape
    G = N_Q_HEADS // N_KV_HEADS
    qg = (q * d ** -0.5).reshape(B, N_KV_HEADS, G, L, d)
    s = jnp.einsum('bhgqd,bhkd->bhgqk', qg, k).astype(jnp.float32)
    sk = jnp.broadcast_to(sink.astype(jnp.float32).reshape(1, N_KV_HEADS, G, 1, 1), s.shape[:-1] + (1,))
    p = jax.nn.softmax(jnp.concatenate([s, sk], axis=-1), axis=-1)[..., :L].astype(v.dtype)
    o = jnp.einsum('bhgqk,bhkd->bhgqd', p, v)
    return o.reshape(B, N_Q_HEADS, L, d)


def conformer_conv(u, w_dw, b_dw, g_ln, b_ln, w_pw):
    val, gate = jnp.split(u, 2, axis=-1)
    y = depthwise_conv(val * jax.nn.sigmoid(gate), w_dw) + b_dw
    y = jax.nn.silu(layer_norm(y, g_ln, b_ln))
    return y @ w_pw


def mlstm_chunkwise(q, k, v, log_i, log_f, state):
    B, H, T, dk = q.shape
    dv = v.shape[-1]
    L = M_CHUNK
    nc = T // L

    def chunks(a):
        return jnp.moveaxis(a.reshape(a.shape[:2] + (nc, L) + a.shape[3:]), 2, 0)

    tril = jnp.tril(jnp.ones((L, L), dtype=bool))

    def step(carry, xs):
        C, n, m = carry
        qc, kc, vc, ic, fc = xs
        b = jnp.cumsum(fc, axis=-1)
        a = b + m[..., None]
        dmat = jnp.where(tril, b[..., :, None] - b[..., None, :] + ic[..., None, :], NEG_INF)
        mt = jnp.maximum(a, dmat.max(axis=-1))
        w_inter = jnp.exp(a - mt)
        s = jnp.einsum('bhtd,bhsd->bhts', qc, kc) * jnp.exp(dmat - mt[..., None])
        num = w_inter[..., None] * jnp.einsum('bhtd,bhde->bhte', qc, C) + jnp.einsum('bhts,bhse->bhte', s, vc)
        den = w_inter * jnp.einsum('bhtd,bhd->bht', qc, n) + s.sum(axis=-1)
        h = num / jnp.maximum(jnp.abs(den), jnp.exp(-mt))[..., None]
        b_end = b[..., -1]
        g = b_end[..., None] - b + ic
        m_new = jnp.maximum(b_end + m, g.max(axis=-1))
        decay = jnp.exp(b_end + m - m_new)
        wk = jnp.exp(g - m_new[..., None])
        C_new = decay[..., None, None] * C + jnp.einsum('bhs,bhsd,bhse->bhde', wk, kc, vc)
        n_new = decay[..., None] * n + jnp.einsum('bhs,bhsd->bhd', wk, kc)
        return (C_new, n_new, m_new), h

    state, h = lax.scan(step, state, (chunks(q), chunks(k), chunks(v), chunks(log_i), chunks(log_f)))
    h = jnp.moveaxis(h, 0, 2).reshape(B, H, T, dv)
    return h, state


def mlstm_branch(qk, v, o, gates, qk_c, v_c, o_c, gates_c, b_mgate, w_mconv, g_mnorm, w_mout, with_ctx_out):
    def prep(qk, v, gates):
        q, k = jnp.split(jax.nn.silu(depthwise_conv(qk, w_mconv)), 2, axis=-1)
        q = to_heads(q, M_HEADS).astype(jnp.float32)
        k = to_heads(k, M_HEADS).astype(jnp.float32) * M_HEAD_DIM ** -0.5
        v = to_heads(v, M_HEADS).astype(jnp.float32)
        g = jnp.moveaxis((gates + b_mgate).astype(jnp.float32), -1, 1)
        i_fw, i_bw, f_fw, f_bw = jnp.split(g, 4, axis=1)
        fwd = (q, k, v, i_fw, jax.nn.log_sigmoid(f_fw))
        bwd = tuple(jnp.flip(a, axis=2) for a in (q, k, v, i_bw, jax.nn.log_sigmoid(f_bw)))
        return fwd, bwd

    lat_f, lat_b = prep(qk, v, gates)
    ctx_f, ctx_b = prep(qk_c, v_c, gates_c)
    B = qk.shape[0]
    init = (jnp.zeros((B, M_HEADS, M_HEAD_DIM, M_HEAD_DIM), jnp.float32),
            jnp.zeros((B, M_HEADS, M_HEAD_DIM), jnp.float32),
            jnp.full((B, M_HEADS), NEG_INF, jnp.float32))
    hc_f, st_f = mlstm_chunkwise(*ctx_f, init)
    hc_b, st_b = mlstm_chunkwise(*ctx_b, init)
    hl_f, _ = mlstm_chunkwise(*lat_f, st_f)
    hl_b, _ = mlstm_chunkwise(*lat_b, st_b)

    def readout(h_f, h_b_rev, o):
        h = jax.nn.sigmoid(to_heads(o, M_HEADS).astype(jnp.float32)) * (h_f + jnp.flip(h_b_rev, axis=2))
        mu = jnp.mean(h, axis=-1, keepdims=True)
        var = jnp.mean(jnp.square(h - mu), axis=-1, keepdims=True)
        h = from_heads((h - mu) * lax.rsqrt(var + EPS)) * g_mnorm.astype(jnp.float32)
        return h.astype(o.dtype) @ w_mout

    y = readout(hl_f, hl_b, o)
    y_ctx = readout(hc_f, hc_b, o_c) if with_ctx_out else None
    return y, y_ctx


def gated_merge(gate_logits, ya, yb, yc):
    ga, gb, gc = jnp.split(jax.nn.sigmoid(gate_logits), N_BRANCHES, axis=-1)
    return ga * ya + gb * yb + gc * yc


def hybrid_mixer(h, hc, w_in, b_mgate, sink, w_att_out, w_dw, b_dw, g_cln, b_cln, w_pw,
                 w_mconv, g_mnorm, w_mout, w_out, cos, sin, with_ctx_out):
    qa, ka, va, glu, qk_m, v_m, o_m, gate_m, br = jnp.split(h @ w_in, SPLIT_POINTS, axis=-1)
    qac, kac, vac, gluc, qk_mc, v_mc, o_mc, gate_mc, brc = jnp.split(hc @ w_in, SPLIT_POINTS, axis=-1)
    k_ctx = to_heads(kac, N_KV_HEADS)
    v_ctx = to_heads(vac, N_KV_HEADS)
    q = apply_rope(to_heads(qa, N_Q_HEADS), cos, sin)
    k = apply_rope(to_heads(ka, N_KV_HEADS), cos, sin)
    ya = from_heads(window_attention(q, k, to_heads(va, N_KV_HEADS), k_ctx, v_ctx, sink)) @ w_att_out
    yb = conformer_conv(glu, w_dw, b_dw, g_cln, b_cln, w_pw)
    yc, yc_ctx = mlstm_branch(qk_m, v_m, o_m, gate_m, qk_mc, v_mc, o_mc, gate_mc,
                              b_mgate, w_mconv, g_mnorm, w_mout, with_ctx_out)
    y = gated_merge(br, ya, yb, yc) @ w_out
    if not with_ctx_out:
        return y, None
    ya_c = from_heads(context_attention(to_heads(qac, N_Q_HEADS), k_ctx, v_ctx, sink)) @ w_att_out
    yb_c = conformer_conv(gluc, w_dw, b_dw, g_cln, b_cln, w_pw)
    y_ctx = gated_merge(brc, ya_c, yb_c, yc_ctx) @ w_out
    return y, y_ctx


def swiglu(h, w_gate, w_up, w_down):
    return (jax.nn.silu(h @ w_gate) * (h @ w_up)) @ w_down


def setup_inputs(seed: int = 0) -> dict:
    key = jax.random.key(seed)
    ks = jax.random.split(key, 27)
    f32 = jnp.float32

    def nrm(k, shape):
        return jax.random.normal(k, shape, f32)

    def w(k, shape, fan_in):
        return nrm(k, shape) * fan_in ** -0.5

    def gain(k, shape):
        return 1.0 + 0.05 * nrm(k, shape)

    def small(k, shape):
        return 0.02 * nrm(k, shape)

    b_mgate = jnp.concatenate([0.1 * nrm(ks[9], (DEPTH, 2 * M_HEADS)),
                               3.0 + 0.5 * nrm(ks[25], (DEPTH, 2 * M_HEADS))], axis=-1)
    return {
        'x': nrm(ks[0], (BATCH, SEQ, D_MODEL)),
        'c': nrm(ks[1], (BATCH, D_MODEL)),
        'ctx': nrm(ks[2], (BATCH, CTX_LEN, D_MODEL)),
        'c_ctx': nrm(ks[3], (D_MODEL,)),
        'w_ada': 0.5 * w(ks[4], (DEPTH, D_MODEL, 6 * D_MODEL), D_MODEL),
        'b_ada': small(ks[5], (DEPTH, 6 * D_MODEL)),
        'g_norm_mix': gain(ks[6], (DEPTH, D_MODEL)),
        'g_norm_ffn': gain(ks[7], (DEPTH, D_MODEL)),
        'w_in': w(ks[8], (DEPTH, D_MODEL, D_IN), D_MODEL),
        'b_mgate': b_mgate,
        'att_sink': 0.5 * nrm(ks[10], (DEPTH, N_Q_HEADS)),
        'w_att_out': w(ks[11], (DEPTH, ATT_Q, D_MODEL), ATT_Q),
        'w_conv_dw': w(ks[12], (DEPTH, CONV_WIDTH, CONV_DIM), CONV_WIDTH),
        'b_conv_dw': small(ks[13], (DEPTH, CONV_DIM)),
        'g_conv_ln': gain(ks[14], (DEPTH, CONV_DIM)),
        'b_conv_ln': small(ks[15], (DEPTH, CONV_DIM)),
        'w_conv_pw': w(ks[16], (DEPTH, CONV_DIM, D_MODEL), CONV_DIM),
        'w_mconv': w(ks[17], (DEPTH, M_SHORT_CONV, 2 * M_WIDTH), M_SHORT_CONV),
        'g_mlstm_norm': gain(ks[18], (DEPTH, M_WIDTH)),
        'w_mlstm_out': w(ks[19], (DEPTH, M_WIDTH, D_MODEL), M_WIDTH),
        'w_out': w(ks[20], (DEPTH, D_MODEL, D_MODEL), D_MODEL),
        'w_ff_gate': w(ks[21], (DEPTH, D_MODEL, D_FF), D_MODEL),
        'w_ff_up': w(ks[22], (DEPTH, D_MODEL, D_FF), D_MODEL),
        'w_ff_down': w(ks[23], (DEPTH, D_FF, D_MODEL), D_FF),
        'g_final': gain(ks[24], (D_MODEL,)),
    }


def reference(x, c, ctx, c_ctx, w_ada, b_ada, g_norm_mix, g_norm_ffn, w_in, b_mgate, att_sink,
              w_att_out, w_conv_dw, b_conv_dw, g_conv_ln, b_conv_ln, w_conv_pw, w_mconv,
              g_mlstm_norm, w_mlstm_out, w_out, w_ff_gate, w_ff_up, w_ff_down, g_final):
    cos, sin = rope_tables(x.shape[1])
    for l in range(DEPTH):
        with_ctx_out = l < DEPTH - 1
        mod = jax.nn.silu(c) @ w_ada[l] + b_ada[l]
        sh1, sc1, gt1, sh2, sc2, gt2 = jnp.split(mod[:, None, :], 6, axis=-1)
        mod_c = jax.nn.silu(c_ctx) @ w_ada[l] + b_ada[l]
        csh1, csc1, cgt1, csh2, csc2, cgt2 = jnp.split(mod_c, 6, axis=-1)
        h = modulate(rms_norm(x, g_norm_mix[l]), sh1, sc1)
        hc = modulate(rms_norm(ctx, g_norm_mix[l]), csh1, csc1)
        y, y_ctx = hybrid_mixer(h, hc, w_in[l], b_mgate[l], att_sink[l], w_att_out[l],
                                w_conv_dw[l], b_conv_dw[l], g_conv_ln[l], b_conv_ln[l], w_conv_pw[l],
                                w_mconv[l], g_mlstm_norm[l], w_mlstm_out[l], w_out[l],
                                cos, sin, with_ctx_out)
        x = x + gt1 * y
        x = x + gt2 * swiglu(modulate(rms_norm(x, g_norm_ffn[l]), sh2, sc2), w_ff_gate[l], w_ff_up[l], w_ff_down[l])
        if with_ctx_out:
            ctx = ctx + cgt1 * y_ctx
            ctx = ctx + cgt2 * swiglu(modulate(rms_norm(ctx, g_norm_ffn[l]), csh2, csc2), w_ff_gate[l], w_ff_up[l], w_ff_down[l])
    return rms_norm(x, g_final)
```

```python
import contextlib
import os
import numpy as np
KD = int(os.environ.get('KD', '9'))
import concourse.bass as bass
import concourse.mybir as mybir
from concourse.bass_utils import run_bass_kernel_spmd

F32 = mybir.dt.float32
BF16 = mybir.dt.bfloat16
AF = mybir.ActivationFunctionType
ALU = mybir.AluOpType
AX = mybir.AxisListType

COMPUTE = ("pe", "act", "dve", "pool", "sp")
EPOCH = 6000
DMA_USES = 1800


class _Rec:
    def __getattr__(self, name):
        def f(*a, **k):
            self.call = (name, a, k)
            return self
        return f


class Prog:
    def __init__(self, nc, n_dma_sems=10):
        self.nc = nc
        self.ins = []
        self.per_eng = {e: [] for e in COMPUTE}
        self.last_writer = {}
        self.readers = {}
        self.nstreams = len(COMPUTE)
        self.stream_id = {e: i for i, e in enumerate(COMPUTE)}
        self.stream_pos = [0] * self.nstreams
        self.stream_last = {}
        self.ic = {e: [0] * self.nstreams for e in COMPUTE}
        self.vc_snap = []
        self.dma_pool = {}
        self.dma_rr = {}
        self.dma_last = {}
        self.dma_uses = {}
        self.n_dma_sems = n_dma_sems
        self.signal = set()

    def _new_stream(self):
        sid = self.nstreams
        self.nstreams += 1
        self.stream_pos.append(0)
        for e in COMPUTE:
            self.ic[e] = self.ic[e] + [0]
        return sid

    def _dma_stream(self, q):
        pool = self.dma_pool.setdefault(q, [])
        if len(pool) < self.n_dma_sems:
            sid = self._new_stream()
            pool.append(sid)
            self.dma_uses[sid] = 0
            self.dma_rr[q] = len(pool) - 1
            return sid
        k = (self.dma_rr[q] + 1) % len(pool)
        self.dma_rr[q] = k
        sid = pool[k]
        if self.dma_uses[sid] >= DMA_USES:
            sid = self._new_stream()
            pool[k] = sid
            self.dma_uses[sid] = 0
        return sid

    enabled = True

    def add(self, eng, fn, reads=(), writes=(), dma=False, force=False, extra=()):
        if not self.enabled:
            return None
        rec = _Rec()
        fn(rec)
        fn = (lambda call: (lambda e: getattr(e, call[0])(*call[1], **call[2])))(rec.call)
        iid = len(self.ins)
        deps = set(extra)
        for k in reads:
            w = self.last_writer.get(k)
            if w is not None:
                deps.add(w)
        for k in writes:
            w = self.last_writer.get(k)
            if w is not None:
                deps.add(w)
            for r in self.readers.get(k, ()):
                deps.add(r)
        if dma:
            sid = self._dma_stream(eng)
            prev = self.dma_last.get(sid)
            if prev is not None:
                deps.add(prev)
            self.dma_last[sid] = iid
            self.dma_uses[sid] += 1
        else:
            sid = self.stream_id[eng]
        self.stream_pos[sid] += 1
        pos = self.stream_pos[sid]
        self.stream_last[sid] = iid
        ic = self.ic[eng]
        waits = []
        rset = set(reads)
        for d in sorted(deps):
            deng, _, ddma, _, dsid, dpos = self.ins[d]
            if not ddma and deng == eng and not dma and not force:
                if eng == "pe":
                    continue
                raw = False
                for k in rset:
                    if self.last_writer.get(k) == d:
                        raw = True
                        break
                if not raw:
                    continue
            if len(ic) < self.nstreams:
                ic = ic + [0] * (self.nstreams - len(ic))
            if ic[dsid] >= dpos:
                continue
            waits.append(d)
            self.signal.add(d)
            snap, s2, p2 = self.vc_snap[d]
            if len(snap) < self.nstreams:
                snap = snap + [0] * (self.nstreams - len(snap))
            new = [a if a > b else b for a, b in zip(ic, snap)]
            if new[s2] < p2:
                new[s2] = p2
            ic = new
        self.ic[eng] = ic
        self.vc_snap.append((ic, sid, pos))
        self.ins.append((eng, fn, dma, waits, sid, pos))
        self.per_eng[eng].append(iid)
        for k in reads:
            self.readers.setdefault(k, []).append(iid)
        for k in writes:
            self.last_writer[k] = iid
            self.readers[k] = []
        return iid

    def pe(self, fn, r=(), w=()):
        return self.add("pe", fn, r, w)

    def act(self, fn, r=(), w=()):
        return self.add("act", fn, r, w)

    def dve(self, fn, r=(), w=()):
        return self.add("dve", fn, r, w)

    def pool(self, fn, r=(), w=()):
        return self.add("pool", fn, r, w)

    def dma(self, q, out, in_, r=(), w=(), **kw):
        return self.add(q, lambda e: e.dma_start(out=out, in_=in_, **kw), r, w, dma=True)

    def barrier(self):
        lasts = list(self.stream_last.values())
        for e in COMPUTE:
            self.add(e, lambda en: en.nop(), (), (), force=True, extra=lasts)

    def emit(self):
        nc = self.nc
        sig_count = {}
        stream_sig = [0] * self.nstreams
        for iid, (eng, fn, dma, waits, sid, pos) in enumerate(self.ins):
            if dma:
                sig_count[iid] = 16 * pos
            elif iid in self.signal:
                stream_sig[sid] += 1
                sig_count[iid] = stream_sig[sid]
        stack = contextlib.ExitStack()
        sems = {}

        def sem_for(sid, cnt):
            if sid < len(COMPUTE):
                ep = (cnt - 1) // EPOCH
                key = (sid, ep)
                val = cnt - ep * EPOCH
            else:
                key = (sid, 0)
                val = cnt
            if key not in sems:
                sems[key] = stack.enter_context(nc.semaphore("s%d_%d" % key))
            return sems[key], val

        for iid in sorted(sig_count):
            sem_for(self.ins[iid][4], sig_count[iid])
        with stack:
            with nc.Block() as block:
                def run(engname):
                    def body(e):
                        for iid in self.per_eng[engname]:
                            _, fn, dma, waits, sid, pos = self.ins[iid]
                            for d in waits:
                                s, v = sem_for(self.ins[d][4], sig_count[d])
                                e.wait_ge(s, v)
                            r = fn(e)
                            if iid in sig_count:
                                s, v = sem_for(sid, sig_count[iid])
                                r.then_inc(s, 16 if dma else 1)
                    return body
                block.sync(run("sp"))
                block.gpsimd(run("pool"))
                block.scalar(run("act"))
                block.vector(run("dve"))
                block.tensor(run("pe"))
        return len(self.ins), len(sems)


D = 1024
SEQ = 4096
CTX = 256
T = SEQ + CTX
NT = T // 128
NCH = T // 64
DEPTH = 4
DFF = 2816
NFM = 49
CK, CGLU, CQKM, CBR, CGATE = 4, 8, 16, 24, 48
NTM = 1152
LN_DK = float(np.log(128.0 ** -0.5))
EPS = 1e-6

GROUPS512 = [(g * 512, 512) for g in range(8)] + [(4096, 256)]
GROUPS256 = [(g * 256, 256) for g in range(17)]
ORDER_F = [64, 65, 66, 67] + list(range(64))
ORDER_B = [67, 66, 65, 64] + list(range(63, -1, -1))


class Scope:
    uid = 0

    def __init__(self, nc):
        self.nc = nc
        self.st = contextlib.ExitStack()
        self.n = 0

    def __enter__(self):
        self.st.__enter__()
        return self

    def __exit__(self, *a):
        return self.st.__exit__(*a)

    def sb(self, name, shape, dt):
        Scope.uid += 1
        return self.st.enter_context(self.nc.sbuf_tensor("%s_u%d" % (name, Scope.uid), list(shape), dt))

    def ps(self, name, shape, dt=F32):
        Scope.uid += 1
        return self.st.enter_context(self.nc.psum_tensor("%s_u%d" % (name, Scope.uid), list(shape), dt))


def build(n_layers=DEPTH, debug=False, stop=None):
    nc = bass.Bass("TRN2", target_bir_lowering=False)
    P = Prog(nc)

    def phase(l, k):
        P.enabled = stop is None or (l, k) <= stop

    def din(name, shape, dt=F32):
        return nc.dram_tensor(name, list(shape), dt, kind="ExternalInput").ap()

    def dscr(name, shape, dt):
        kind = "ExternalOutput" if debug else "Internal"
        return nc.dram_tensor(name, list(shape), dt, kind=kind).ap()

    x_in = din("x", [SEQ, D])
    ctx_in = din("ctx", [CTX, D])
    cc_in = din("cc", [128, 8, 2])
    w_ada = din("w_ada", [DEPTH, D, 6 * D])
    b_ada = din("b_ada", [DEPTH, 6 * D])
    g_mix = din("g_norm_mix", [DEPTH, D])
    g_ffn = din("g_norm_ffn", [DEPTH, D])
    w_fm = din("w_fm", [DEPTH, D, NFM * 128])
    w_tm = din("w_tm", [DEPTH, D, NTM])
    bm_in = din("bm", [DEPTH, 64, 2])
    sink_in = din("att_sink", [DEPTH, 8])
    w_ao = din("w_att_out", [DEPTH, 512, D])
    cvec_in = din("cvec", [DEPTH, 128, 4, 34])
    w_pw = din("w_conv_pw", [DEPTH, 512, D])
    wm_in = din("wm", [DEPTH, 128, 8, 3])
    gmn_in = din("gmn", [DEPTH, 128, 4])
    w_mo = din("w_mlstm_out", [DEPTH, 512, D])
    w_out = din("w_out", [DEPTH, D, D])
    w_g = din("w_ff_gate", [DEPTH, D, DFF])
    w_u = din("w_ff_up", [DEPTH, D, DFF])
    w_d = din("w_ff_down", [DEPTH, DFF, D])
    g_fin = din("g_final", [D])
    ident_in = din("ident", [128, 128])
    rperm_in = din("rperm", [128, 128])
    cos_in = din("cosT", [128, SEQ])
    sin_in = din("sinT", [128, SEQ])
    mprev_in = din("mprev", [128, 128])
    mnext_in = din("mnext", [128, 128])
    bmf_in = din("bmf", [128, 128])
    bmb_in = din("bmb", [128, 128])
    sel_in = din("sel", [64, 8, 128])
    out = nc.dram_tensor("out", [SEQ, D], F32, kind="ExternalOutput").ap()

    xs = dscr("xs", [T, D], F32)
    modd = dscr("modd", [DEPTH, 2, 6 * D], F32)
    projT = dscr("projT", [48 * 128, T], BF16)
    gatesT = dscr("gatesT", [128, T], F32)
    projM = dscr("projM", [9, 128, NT * 128], BF16)
    attT = dscr("attT", [512, T], BF16)
    cbT = dscr("cbT", [512, T], BF16)
    mhT = dscr("mhT", [512, T], BF16)

    with Scope(nc) as G:
        id32 = G.sb("id32", [128, 128], F32)
        idb = G.sb("idb", [128, 128], BF16)
        ones32 = G.sb("ones32", [128, 128], F32)
        P.dma("sp", id32[:], ident_in[:, :], w=["id32"])
        P.dve(lambda e: e.tensor_copy(out=idb[:], in_=id32[:]), ["id32"], ["idb"])
        P.dve(lambda e: e.memset(ones32[:], 1.0), [], ["ones32"])
        for i in range(16):
            P.dma("sp", xs[i * 256:(i + 1) * 256, :], x_in[i * 256:(i + 1) * 256, :], w=["xs_i%d" % i])
        P.dma("sp", xs[SEQ:T, :], ctx_in[:, :], w=["xs_c"])

        with Scope(nc) as S:
            cc = S.sb("cc", [128, 8, 2], F32)
            scc = S.sb("scc", [128, 8, 2], F32)
            wa = [S.sb("wa%d" % i, [128, 8, 512], F32) for i in range(2)]
            bad = S.sb("bad", [2, 6 * D], F32)
            mrow = S.sb("mrow", [2, 6 * D], F32)
            pm = [S.ps("pm%d" % i, [2, 512]) for i in range(2)]
            P.dma("sp", cc[:], cc_in[:, :, :], w=["cc"])
            P.act(lambda e: e.activation(out=scc[:], in_=cc[:], func=AF.Silu), ["cc"], ["scc"])
            it = 0
            for l in range(n_layers):
                P.dma("sp", bad[:], b_ada[l, :].partition_broadcast(2), w=["bad"])
                for cg in range(12):
                    s = it % 2
                    it += 1
                    P.dma("sp", wa[s][:], w_ada[l, :, cg * 512:(cg + 1) * 512].rearrange("(c p) n -> p c n", p=128),
                          w=["wa%d" % s])
                    for k in range(8):
                        P.pe(lambda e, s=s, k=k: e.matmul(pm[s][:], lhsT=scc[:, k, :], rhs=wa[s][:, k, :],
                                                         start=(k == 0), stop=(k == 7)),
                             ["scc", "wa%d" % s], ["pm%d" % s])
                    P.dve(lambda e, s=s, cg=cg: e.tensor_tensor(out=mrow[:, cg * 512:(cg + 1) * 512], in0=pm[s][:],
                                                               in1=bad[:, cg * 512:(cg + 1) * 512], op=ALU.add),
                          ["pm%d" % s, "bad"], ["mrow"])
                P.dma("pool", modd[l, :, :], mrow[:], r=["mrow"], w=["modd"])
            P.barrier()

        def load_mod(eng_q, tile, l, stream, k, key):
            P.dma(eng_q, tile[:], modd[l, stream, k * D:(k + 1) * D].partition_broadcast(128), r=["modd"], w=[key])

        def make_gain(gt, gkey, sct, sckey, gvec_dram, tmp, tmpkey):
            P.dma("sp", tmp[:], gvec_dram.partition_broadcast(128), w=[tmpkey])
            P.dve(lambda e: e.scalar_tensor_tensor(out=gt[:], in0=sct[:], scalar=1.0, in1=tmp[:],
                                                   op0=ALU.add, op1=ALU.mult),
                  [sckey, tmpkey], [gkey])

        for l in range(n_layers):
            last = (l == DEPTH - 1)
            phase(l, 1)
            with Scope(nc) as S:
                wfm = S.sb("wfm", [128, 8, NFM * 128], BF16)
                wtm = S.sb("wtm", [128, 8, NTM], BF16)
                cosT = S.sb("cosT", [128, SEQ], F32)
                sinT = S.sb("sinT", [128, SEQ], F32)
                rp32 = S.sb("rp32", [128, 128], F32)
                rpb = S.sb("rpb", [128, 128], BF16)
                Gb = S.sb("Gb", [128, D], F32)
                Sb = S.sb("Sb", [128, D], F32)
                xg = S.sb("xg", [128, 4, D], F32)
                tmpf = S.sb("tmpf", [128, D], F32)
                hb = S.sb("hb", [128, D], BF16)
                hT = S.sb("hT", [128, 8, 512], BF16)
                ss = S.sb("ss", [128, 4], F32)
                rstd = S.sb("rstd", [128, 4], F32)
                fo = [S.sb("fo%d" % i, [128, 512], BF16) for i in range(3)]
                qraw = [S.sb("qraw%d" % i, [128, 512], BF16) for i in range(2)]
                t1 = [S.sb("t1%d" % i, [128, 512], F32) for i in range(2)]
                t2 = [S.sb("t2%d" % i, [128, 512], F32) for i in range(2)]
                go = S.sb("go", [128, 512], F32)
                tmo = [S.sb("tmo%d" % i, [128, NTM], BF16) for i in range(2)]
                pT = S.ps("pT", [128, 8, 128], BF16)
                pTM = [S.ps("pTM%d" % i, [128, 512]) for i in range(2)]
                pFM = [S.ps("pFM%d" % i, [128, 512]) for i in range(3)]
                pR = S.ps("pR", [128, 512])

                for k in range(8):
                    P.dma("pool", wfm[:, k, :], w_fm[l, k * 128:(k + 1) * 128, :], w=["wfm%d" % k])
                P.dma("pool", wtm[:], w_tm[l, :, :].rearrange("(c p) n -> p c n", p=128), w=["wtm"])
                wfm_keys = ["wfm%d" % k for k in range(8)]
                P.dma("sp", cosT[:], cos_in[:, :], w=["cosT"])
                P.dma("sp", sinT[:], sin_in[:, :], w=["sinT"])
                P.dma("sp", rp32[:], rperm_in[:, :], w=["rp32"])
                P.dve(lambda e: e.tensor_copy(out=rpb[:], in_=rp32[:]), ["rp32"], ["rpb"])
                ifm = 0
                ifo = 0
                iq = 0
                itm = 0
                cur_stream = None
                for gi, (t0, n) in enumerate(GROUPS512):
                    stream = 0 if t0 < SEQ else 1
                    if stream != cur_stream:
                        cur_stream = stream
                        load_mod("sp", Sb, l, stream, 0, "Sb")
                        load_mod("sp", tmpf, l, stream, 1, "tmpf")
                        P.dma("sp", Gb[:], g_mix[l, :].partition_broadcast(128), w=["Gb"])
                        P.dve(lambda e: e.scalar_tensor_tensor(out=Gb[:], in0=tmpf[:], scalar=1.0, in1=Gb[:],
                                                               op0=ALU.add, op1=ALU.mult),
                              ["tmpf", "Gb"], ["Gb"])
                    ntl = n // 128
                    P.dve(lambda e: e.memset(ss[:], 0.0), [], ["ss"])
                    for j in range(ntl):
                        P.dma("sp", xg[:, j, :], xs[t0 + j * 128:t0 + (j + 1) * 128, :], r=["xs"], w=["xg%d" % j])
                        P.act(lambda e, j=j: e.activation(out=tmpf[:], in_=xg[:, j, :], func=AF.Square, scale=1.0 / 32,
                                                          accum_out=ss[:, j:j + 1]),
                              ["xg%d" % j, "ss"], ["tmpf", "ss"])
                    P.act(lambda e: e.activation(out=rstd[:], in_=ss[:], func=AF.Sqrt, bias=EPS, scale=1.0),
                          ["ss"], ["rstd"])
                    P.dve(lambda e: e.reciprocal(out=rstd[:], in_=rstd[:]), ["rstd"], ["rstd"])
                    for j in range(ntl):
                        P.dve(lambda e, j=j: e.scalar_tensor_tensor(out=tmpf[:], in0=xg[:, j, :], scalar=rstd[:, j:j + 1],
                                                                    in1=Gb[:], op0=ALU.mult, op1=ALU.mult),
                              ["xg%d" % j, "rstd", "Gb"], ["tmpf"])
                        P.dve(lambda e: e.tensor_tensor(out=hb[:], in0=tmpf[:], in1=Sb[:], op=ALU.add),
                              ["tmpf", "Sb"], ["hb"])
                        for c in range(8):
                            P.pe(lambda e, c=c: e.transpose(out=pT[:, c, :], in_=hb[:, c * 128:(c + 1) * 128],
                                                            identity=idb[:]),
                                 ["hb", "idb"], ["pT"])
                        P.act(lambda e, j=j: e.copy(out=hT[:, :, j * 128:(j + 1) * 128], in_=pT[:]), ["pT"], ["hT"])
                    for j in range(ntl):
                        so = itm % 2
                        itm += 1
                        for (c0, cn) in ((0, 512), (512, 512), (1024, 128)):
                            sp_ = ifm % 2
                            ifm += 1
                            for k in range(8):
                                P.pe(lambda e, sp_=sp_, k=k, j=j, c0=c0, cn=cn: e.matmul(
                                    pTM[sp_][:, 0:cn], lhsT=hT[:, k, j * 128:(j + 1) * 128], rhs=wtm[:, k, c0:c0 + cn],
                                    start=(k == 0), stop=(k == 7)), ["hT", "wtm"], ["pTM%d" % sp_])
                            P.dve(lambda e, sp_=sp_, so=so, c0=c0, cn=cn: e.tensor_copy(
                                out=tmo[so][:, c0:c0 + cn], in_=pTM[sp_][:, 0:cn]), ["pTM%d" % sp_], ["tmo%d" % so])
                        tt_ = (t0 // 128) + j
                        P.dma("pool", projM[:, :, tt_ * 128:(tt_ + 1) * 128].rearrange("b p d -> p b d"),
                              tmo[so][:].rearrange("p (b d) -> p b d", d=128), r=["tmo%d" % so], w=["projM"])
                    for c in range(NFM):
                        sp_ = ifm % 3
                        ifm += 1
                        for k in range(8):
                            P.pe(lambda e, sp_=sp_, k=k, c=c, n=n: e.matmul(
                                pFM[sp_][:, 0:n], lhsT=wfm[:, k, c * 128:(c + 1) * 128], rhs=hT[:, k, 0:n],
                                start=(k == 0), stop=(k == 7)), ["hT", "wfm%d" % k], ["pFM%d" % sp_])
                        pk = "pFM%d" % sp_
                        if c == CGATE:
                            P.act(lambda e, sp_=sp_, n=n: e.copy(out=go[:, 0:n], in_=pFM[sp_][:, 0:n]), [pk], ["go"])
                            P.dma("pool", gatesT[:, t0:t0 + n], go[:, 0:n], r=["go"], w=["gatesT"])
                            continue
                        so = ifo % 3
                        ifo += 1
                        fk = "fo%d" % so
                        if c < CGLU and stream == 0:
                            sq = iq % 2
                            iq += 1
                            qk_ = "qraw%d" % sq
                            P.act(lambda e, sp_=sp_, sq=sq, n=n: e.copy(out=qraw[sq][:, 0:n], in_=pFM[sp_][:, 0:n]),
                                  [pk], [qk_])
                            P.pe(lambda e, sq=sq, n=n: e.matmul(pR[:, 0:n], lhsT=rpb[:], rhs=qraw[sq][:, 0:n],
                                                                start=True, stop=True), [qk_, "rpb"], ["pR"])
                            P.dve(lambda e, sq=sq, n=n, t0=t0: e.tensor_tensor(out=t1[sq][:, 0:n], in0=qraw[sq][:, 0:n],
                                                                               in1=cosT[:, t0:t0 + n], op=ALU.mult),
                                  [qk_, "cosT"], ["t1%d" % sq])
                            P.dve(lambda e, sq=sq, n=n, t0=t0: e.tensor_tensor(out=t2[sq][:, 0:n], in0=pR[:, 0:n],
                                                                               in1=sinT[:, t0:t0 + n], op=ALU.mult),
                                  ["pR", "sinT"], ["t2%d" % sq])
                            P.pool(lambda e, sq=sq, so=so, n=n: e.tensor_tensor(out=fo[so][:, 0:n], in0=t2[sq][:, 0:n],
                                                                                in1=t1[sq][:, 0:n], op=ALU.add),
                                   ["t2%d" % sq, "t1%d" % sq], [fk])
                        elif CBR <= c < CGATE:
                            P.act(lambda e, sp_=sp_, so=so, n=n: e.activation(out=fo[so][:, 0:n], in_=pFM[sp_][:, 0:n],
                                                                              func=AF.Sigmoid), [pk], [fk])
                        else:
                            if c % 2 == 0:
                                P.act(lambda e, sp_=sp_, so=so, n=n: e.copy(out=fo[so][:, 0:n], in_=pFM[sp_][:, 0:n]),
                                      [pk], [fk])
                            else:
                                P.dve(lambda e, sp_=sp_, so=so, n=n: e.tensor_copy(out=fo[so][:, 0:n],
                                                                                   in_=pFM[sp_][:, 0:n]), [pk], [fk])
                        P.dma("pool", projT[c * 128:(c + 1) * 128, t0:t0 + n], fo[so][:, 0:n], r=[fk], w=["projT"])
                P.barrier()

            phase(l, 2)
            with Scope(nc) as S:
                qT = S.sb("qT", [128, 4, T], BF16)
                kT = S.sb("kT", [128, 4, T], BF16)
                va = S.sb("va", [128, NT, 2, 65], BF16)
                vtmp = S.sb("vtmp", [128, NT * 128], BF16)
                aT = S.sb("aT", [128, 4, T], BF16)
                m32 = S.sb("m32", [128, 2, 128], F32)
                mk = S.sb("mk", [128, 2, 4, 128], BF16)
                sk = S.sb("sk", [128, 8], F32)
                esk = S.sb("esk", [128, 8], F32)
                pt = [S.sb("pt%d" % i, [128, 5, 512], BF16) for i in range(2)]
                den = S.sb("den", [128, 4], F32)
                att = [S.sb("att%d" % i, [128, 512], BF16) for i in range(2)]
                pST = [S.ps("pST%d" % i, [128, 512]) for i in range(3)]
                pPV = [S.ps("pPV%d" % i, [128, 4, 128]) for i in range(2)]
                pAT = S.ps("pAT", [128, 4, 128], BF16)
                for c in range(4):
                    P.dma("sp", qT[:, c, :], projT[c * 128:(c + 1) * 128, :], r=["projT"], w=["qT"])
                for c in range(4):
                    P.dma("sp", kT[:, c, :], projT[(CK + c) * 128:(CK + 1 + c) * 128, :], r=["projT"], w=["kT"])
                P.dma("sp", vtmp[:], projM[0, :, :], r=["projM"], w=["vtmp"])
                P.dve(lambda e: e.tensor_copy(out=va[:, :, :, 0:64],
                                              in_=vtmp[:].rearrange("p (t g d) -> p t g d", g=2, d=64)), ["vtmp"], ["va"])
                P.pool(lambda e: e.memset(va[:, :, :, 64:65], 1.0), [], ["va"])
                P.dma("sp", m32[:, 0, :], mprev_in[:, :], w=["m32"])
                P.dma("sp", m32[:, 1, :], mnext_in[:, :], w=["m32"])
                for i in range(2):
                    for hh in range(4):
                        P.dve(lambda e, i=i, hh=hh: e.tensor_copy(out=mk[:, i, hh, :], in_=m32[:, i, :]), ["m32"], ["mk"])
                P.dma("sp", sk[:], sink_in[l, :].partition_broadcast(128), w=["sk"])
                P.act(lambda e: e.activation(out=esk[:], in_=sk[:], func=AF.Exp), ["sk"], ["esk"])
                ist = 0
                ipv = 0
                iat = 0
                qblocks = list(range(NT)) if not last else list(range(32))
                if KD < 1:
                    qblocks = []
                for n in qblocks:
                    sa = iat % 2
                    iat += 1
                    ak = "att%d" % sa
                    for g in range(2):
                        if n < 32:
                            kbs = [(32, None), (33, None)]
                            if n > 0:
                                kbs.append((n - 1, 0))
                            kbs.append((n, None))
                            if n < 31:
                                kbs.append((n + 1, 1))
                        else:
                            kbs = [(32, None), (33, None)]
                        spt = ipv % 2
                        ptk = "pt%d" % spt
                        pvk = "pPV%d" % spt
                        ipv += 1
                        for bi, (kb, mi) in enumerate(kbs):
                            s_ = ist % 3
                            ist += 1
                            for hh in range(4):
                                h = 4 * g + hh
                                p0 = (h % 2) * 64
                                P.pe(lambda e, s_=s_, hh=hh, h=h, kb=kb, n=n, g=g: e.matmul(
                                    pST[s_][:, hh * 128:(hh + 1) * 128],
                                    lhsT=kT[:, 2 * g + (h % 2), kb * 128:(kb + 1) * 128],
                                    rhs=qT[:, h // 2, n * 128:(n + 1) * 128], start=True, stop=True),
                                    ["qT", "kT"], ["pST%d" % s_])
                            P.act(lambda e, s_=s_, spt=spt, bi=bi: e.activation(out=pt[spt][:, bi, :], in_=pST[s_][:],
                                                                               func=AF.Exp, scale=0.125),
                                  ["pST%d" % s_], [ptk + "_%d" % bi])
                            if mi is not None:
                                P.pool(lambda e, spt=spt, bi=bi, mi=mi: e.tensor_tensor(
                                    out=pt[spt][:, bi, :], in0=pt[spt][:, bi, :],
                                    in1=mk[:, mi, :, :].rearrange("p a b -> p (a b)"), op=ALU.mult),
                                    [ptk + "_%d" % bi, "mk"], [ptk + "_%d" % bi])
                        for hh in range(4):
                            for bi, (kb, mi) in enumerate(kbs):
                                P.pe(lambda e, spt=spt, bi=bi, hh=hh, kb=kb, g=g, nb=len(kbs): e.matmul(
                                    pPV[spt][:, hh, 0:65], lhsT=pt[spt][:, bi, hh * 128:(hh + 1) * 128],
                                    rhs=va[:, kb, g, :], start=(bi == 0), stop=(bi == nb - 1)),
                                    [ptk + "_%d" % bi, "va"], [pvk])
                        if KD < 3:
                            continue
                        P.dve(lambda e, spt=spt, g=g: e.tensor_tensor(out=den[:], in0=pPV[spt][:, :, 64],
                                                                      in1=esk[:, 4 * g:4 * g + 4], op=ALU.add),
                              [pvk, "esk"], ["den"])
                        P.dve(lambda e: e.reciprocal(out=den[:], in_=den[:]), ["den"], ["den"])
                        for hh in range(4):
                            h = 4 * g + hh
                            P.dve(lambda e, spt=spt, hh=hh, h=h, sa=sa: e.tensor_scalar(
                                out=att[sa][:, h * 64:(h + 1) * 64], in0=pPV[spt][:, hh, 0:64],
                                scalar1=den[:, hh:hh + 1], scalar2=None, op0=ALU.mult), [pvk, "den"], [ak])
                    for c in (range(4) if KD >= 4 else []):
                        P.pe(lambda e, sa=sa, c=c: e.transpose(out=pAT[:, c, :], in_=att[sa][:, c * 128:(c + 1) * 128],
                                                               identity=idb[:]), [ak, "idb"], ["pAT"])
                    P.act(lambda e, n=n: e.copy(out=aT[:, :, n * 128:(n + 1) * 128], in_=pAT[:]), ["pAT"], ["aT"])
                nq = len(qblocks) * 128
                for c in (range(4) if nq else []):
                    P.dma("pool", attT[c * 128:(c + 1) * 128, 0:nq], aT[:, c, 0:nq], r=["aT"], w=["attT"])
                P.barrier()

            phase(l, 3)
            with Scope(nc) as S:
                cv = S.sb("cv", [128, 4, 34], F32)
                Y = S.sb("Y", [128, 4, 512], F32)
                zb = S.sb("zb", [128, 4, 15 + SEQ + 15], BF16)
                zcb = S.sb("zcb", [128, 4, 15 + CTX + 15], BF16)
                dg = S.sb("dg", [128, 4, 31, 128], BF16)
                vl = [S.sb("vl%d" % i, [128, T], BF16) for i in range(2)]
                gl = [S.sb("gl%d" % i, [128, T], BF16) for i in range(2)]
                sg = S.sb("sg", [128, T], F32)
                sq = [S.sb("sq%d" % i, [128, 512], F32) for i in range(2)]
                mean = S.sb("mean", [128, 512], F32)
                m2 = S.sb("m2", [128, 512], F32)
                rs = S.sb("rs", [128, 512], F32)
                tt = [S.sb("tt%d" % i, [128, 512], F32) for i in range(2)]
                co = [S.sb("co%d" % i, [128, 512], BF16) for i in range(2)]
                pY = [S.ps("pY%d" % i, [128, 512]) for i in range(2)]
                pS1 = S.ps("pS1", [128, 512])
                pS2 = S.ps("pS2", [128, 512])
                P.dma("sp", cv[:], cvec_in[l, :, :, :], w=["cv"])
                P.pool(lambda e: e.memset(zb[:], 0.0), [], ["zb"])
                P.pool(lambda e: e.memset(zcb[:], 0.0), [], ["zcb"])
                for c in range(4):
                    for j in range(31):
                        eng = P.dve if (j % 3) else P.pool
                        eng(lambda e, c=c, j=j: e.tensor_scalar(out=dg[:, c, j, :], in0=idb[:], scalar1=cv[:, c, j:j + 1],
                                                                scalar2=None, op0=ALU.mult), ["idb", "cv"], ["dg%d" % c])
                for c in range(4):
                    s_ = c % 2
                    P.dma("sp", vl[s_][:], projT[(CGLU + c) * 128:(CGLU + 1 + c) * 128, :], r=["projT"], w=["vl%d" % s_])
                    P.dma("sp", gl[s_][:], projT[(CGLU + 4 + c) * 128:(CGLU + 5 + c) * 128, :], r=["projT"],
                          w=["gl%d" % s_])
                    P.act(lambda e, s_=s_: e.activation(out=sg[:], in_=gl[s_][:], func=AF.Sigmoid),
                          ["gl%d" % s_], ["sg"])
                    P.dve(lambda e, s_=s_, c=c: e.tensor_tensor(out=zb[:, c, 15:15 + SEQ], in0=vl[s_][:, 0:SEQ],
                                                                in1=sg[:, 0:SEQ], op=ALU.mult),
                          ["vl%d" % s_, "sg", "zb"], ["zb%d" % c])
                    P.dve(lambda e, s_=s_, c=c: e.tensor_tensor(out=zcb[:, c, 15:15 + CTX], in0=vl[s_][:, SEQ:T],
                                                                in1=sg[:, SEQ:T], op=ALU.mult),
                          ["vl%d" % s_, "sg", "zcb"], ["zcb%d" % c])
                groups = GROUPS512 if not last else GROUPS512[:8]
                isq = 0
                ico = 0
                ipy = 0
                for (t0, n) in groups:
                    ykeys = ["Yc%d" % c for c in range(4)]
                    for c in range(4):
                        sp_ = ipy % 2
                        ipy += 1
                        for j in range(31):
                            if t0 < SEQ:
                                P.pe(lambda e, sp_=sp_, c=c, j=j, t0=t0, n=n: e.matmul(
                                    pY[sp_][:, 0:n], lhsT=dg[:, c, j, :], rhs=zb[:, c, t0 + j:t0 + j + n],
                                    start=(j == 0), stop=(j == 30)), ["dg%d" % c, "zb%d" % c], ["pY%d" % sp_])
                            else:
                                P.pe(lambda e, sp_=sp_, c=c, j=j, n=n: e.matmul(
                                    pY[sp_][:, 0:n], lhsT=dg[:, c, j, :], rhs=zcb[:, c, j:j + n],
                                    start=(j == 0), stop=(j == 30)), ["dg%d" % c, "zcb%d" % c], ["pY%d" % sp_])
                        P.act(lambda e, sp_=sp_, c=c, n=n: e.activation(out=Y[:, c, 0:n], in_=pY[sp_][:, 0:n],
                                                                        func=AF.Identity, bias=cv[:, c, 31:32], scale=1.0),
                              ["pY%d" % sp_, "cv"], [ykeys[c]])
                    for c in range(4):
                        P.pe(lambda e, c=c, n=n: e.matmul(pS1[:, 0:n], lhsT=ones32[:], rhs=Y[:, c, 0:n],
                                                          start=(c == 0), stop=(c == 3)),
                             ["ones32", ykeys[c]], ["pS1"])
                    for c in range(4):
                        s_ = isq % 2
                        isq += 1
                        P.act(lambda e, c=c, s_=s_, n=n: e.activation(out=sq[s_][:, 0:n], in_=Y[:, c, 0:n],
                                                                      func=AF.Square), [ykeys[c]], ["sq%d" % s_])
                        P.pe(lambda e, c=c, s_=s_, n=n: e.matmul(pS2[:, 0:n], lhsT=ones32[:], rhs=sq[s_][:, 0:n],
                                                                 start=(c == 0), stop=(c == 3)),
                             ["ones32", "sq%d" % s_], ["pS2"])
                    P.act(lambda e, n=n: e.activation(out=mean[:, 0:n], in_=pS1[:, 0:n], func=AF.Copy, scale=1.0 / 512),
                          ["pS1"], ["mean"])
                    P.dve(lambda e, n=n: e.tensor_tensor(out=m2[:, 0:n], in0=mean[:, 0:n], in1=mean[:, 0:n], op=ALU.mult),
                          ["mean"], ["m2"])
                    P.dve(lambda e, n=n: e.scalar_tensor_tensor(out=rs[:, 0:n], in0=pS2[:, 0:n], scalar=1.0 / 512,
                                                                in1=m2[:, 0:n], op0=ALU.mult, op1=ALU.subtract),
                          ["pS2", "m2"], ["rs"])
                    P.act(lambda e, n=n: e.activation(out=rs[:, 0:n], in_=rs[:, 0:n], func=AF.Sqrt, bias=EPS, scale=1.0),
                          ["rs"], ["rs"])
                    P.dve(lambda e, n=n: e.reciprocal(out=rs[:, 0:n], in_=rs[:, 0:n]), ["rs"], ["rs"])
                    for c in range(4):
                        s_ = ico % 2
                        ico += 1
                        P.dve(lambda e, c=c, s_=s_, n=n: e.tensor_tensor(out=tt[s_][:, 0:n], in0=Y[:, c, 0:n],
                                                                         in1=mean[:, 0:n], op=ALU.subtract),
                              [ykeys[c], "mean"], ["tt%d" % s_])
                        P.dve(lambda e, s_=s_, n=n: e.tensor_tensor(out=tt[s_][:, 0:n], in0=tt[s_][:, 0:n],
                                                                    in1=rs[:, 0:n], op=ALU.mult),
                              ["tt%d" % s_, "rs"], ["tt%d" % s_])
                        P.act(lambda e, c=c, s_=s_, n=n: e.activation(out=co[s_][:, 0:n], in_=tt[s_][:, 0:n], func=AF.Silu,
                                                                      scale=cv[:, c, 32:33], bias=cv[:, c, 33:34]),
                              ["tt%d" % s_, "cv"], ["co%d" % s_])
                        P.dma("pool", cbT[c * 128:(c + 1) * 128, t0:t0 + n], co[s_][:, 0:n], r=["co%d" % s_],
                              w=["cbT"])
                P.barrier()

            phase(l, 4)
            with Scope(nc) as S4:
                wkT = S4.sb("wkT", [128, NT, 8], F32)
                wkH = S4.sb("wkH", [128, NT, 2, 8], F32)
                thT = S4.sb("thT", [128, NT, 8], F32)
                decB = S4.sb("decB", [128, 8, NCH], F32)
                wmv = S4.sb("wmv", [128, 8, 3], F32)
                gmn = S4.sb("gmn", [128, 4], F32)
                bm32 = S4.sb("bm32", [128, 2, 128], F32)
                bmk = S4.sb("bmk", [128, 2, 128], BF16)
                P.dma("sp", wmv[:], wm_in[l, :, :, :], w=["wmv"])
                P.dma("sp", gmn[:], gmn_in[l, :, :], w=["gmn"])
                P.dma("sp", bm32[:, 0, :], bmf_in[:, :], w=["bm32"])
                P.dma("sp", bm32[:, 1, :], bmb_in[:, :], w=["bm32"])
                P.dve(lambda e: e.tensor_copy(out=bmk[:], in_=bm32[:]), ["bm32"], ["bmk"])
                with Scope(nc) as S:
                    gI = S.sb("gI", [64, T], F32)
                    gF = S.sb("gF", [64, T], F32)
                    csp = S.sb("csp", [64, T], F32)
                    nb = S.sb("nb", [64, T], F32)
                    msk = S.sb("msk", [64, T], F32)
                    bmt = S.sb("bmt", [64, 2], F32)
                    nbf = S.sb("nbf", [64, 1], F32)
                    tot = S.sb("tot", [64, NCH], F32)
                    umax = S.sb("umax", [64, NCH], F32)
                    R = S.sb("R", [64, NCH], F32)
                    mc = S.sb("mc", [64, NCH + 1], F32)
                    dec = S.sb("dec", [64, NCH], F32)
                    sel = S.sb("sel", [64, 8, 128], F32)
                    pG = [S.ps("pG%d" % i, [128, 4, 2, 64]) for i in range(2)]
                    pD = [S.ps("pD%d" % i, [128, 4, NCH]) for i in range(2)]
                    v3 = lambda a: a[:].rearrange("p (c l) -> p c l", l=64)
                    P.dma("sp", gI[:], gatesT[0:64, :], r=["gatesT"], w=["gI"])
                    P.dma("sp", gF[:], gatesT[64:128, :], r=["gatesT"], w=["gF"])
                    P.dma("sp", bmt[:], bm_in[l, :, :], w=["bmt"])
                    P.dma("sp", sel[:], sel_in[:, :, :], w=["sel"])
                    P.dve(lambda e: e.tensor_scalar(out=nbf[:], in0=bmt[:, 1:2], scalar1=-1.0, scalar2=None, op0=ALU.mult),
                          ["bmt"], ["nbf"])
                    P.pool(lambda e: e.memset(msk[:], 1.0), [], ["msk"])
                    P.pool(lambda e: e.memset(v3(msk)[:, :, 0:1], 0.0), [], ["msk"])
                    P.act(lambda e: e.activation(out=gI[:], in_=gI[:], func=AF.Identity, bias=bmt[:, 0:1], scale=1.0),
                          ["gI", "bmt"], ["gI"])
                    P.act(lambda e: e.activation(out=gF[:], in_=gF[:], func=AF.Exp, bias=nbf[:], scale=-1.0),
                          ["gF", "nbf"], ["gF"])
                    P.act(lambda e: e.activation(out=gF[:], in_=gF[:], func=AF.Ln, bias=1.0, scale=1.0), ["gF"], ["gF"])
                    P.dve(lambda e: e.tensor_tensor_scan(out=csp[:], data0=msk[:], data1=gF[:], initial=0.0,
                                                         op0=ALU.mult, op1=ALU.add), ["msk", "gF"], ["csp"])
                    P.dve(lambda e: e.tensor_copy(out=tot[:], in_=v3(csp)[:, :, 63]), ["csp"], ["tot"])
                    P.dve(lambda e: e.tensor_copy(out=nb[0:32, :], in_=csp[0:32, :]), ["csp"], ["nb"])
                    P.dve(lambda e: e.tensor_tensor(out=nb[32:64, :], in0=gF[32:64, :], in1=csp[32:64, :],
                                                    op=ALU.subtract), ["gF", "csp"], ["nb"])
                    P.dve(lambda e: e.tensor_tensor(out=v3(nb)[32:64], in0=v3(nb)[32:64],
                                                    in1=tot[32:64, :].unsqueeze(2).broadcast_to([32, NCH, 64]),
                                                    op=ALU.add), ["nb", "tot"], ["nb"])
                    P.dve(lambda e: e.tensor_tensor(out=gI[:], in0=gI[:], in1=nb[:], op=ALU.add), ["gI", "nb"], ["gI"])
                    P.dve(lambda e: e.tensor_reduce(out=umax[:], in_=v3(gI), axis=AX.X, op=ALU.max), ["gI"], ["umax"])
                    P.dve(lambda e: e.memset(mc[:], -1e30), [], ["mc"])
                    for (r0, order) in ((0, ORDER_F), (32, ORDER_B)):
                        for j, c in enumerate(order):
                            P.dve(lambda e, r0=r0, c=c: e.tensor_tensor(out=R[r0:r0 + 32, c:c + 1],
                                                                        in0=mc[r0:r0 + 32, c:c + 1],
                                                                        in1=umax[r0:r0 + 32, c:c + 1], op=ALU.max),
                                  ["mc", "umax"], ["R"])
                            if j + 1 < len(order):
                                c2 = order[j + 1]
                                P.dve(lambda e, r0=r0, c=c, c2=c2: e.tensor_tensor(
                                    out=mc[r0:r0 + 32, c2:c2 + 1], in0=R[r0:r0 + 32, c:c + 1],
                                    in1=tot[r0:r0 + 32, c:c + 1], op=ALU.subtract), ["R", "tot"], ["mc"])
                    Rb = lambda: R[:, :].unsqueeze(2).broadcast_to([64, NCH, 64])
                    P.dve(lambda e: e.tensor_tensor(out=v3(gI), in0=v3(gI), in1=Rb(), op=ALU.subtract), ["gI", "R"], ["gI"])
                    P.dve(lambda e: e.tensor_tensor(out=v3(nb), in0=v3(nb), in1=Rb(), op=ALU.subtract), ["nb", "R"], ["nb"])
                    P.dve(lambda e: e.tensor_tensor(out=dec[:], in0=mc[:, 0:NCH], in1=R[:], op=ALU.subtract),
                          ["mc", "R"], ["dec"])
                    P.dve(lambda e: e.tensor_scalar(out=gI[:], in0=gI[:], scalar1=LN_DK, scalar2=None, op0=ALU.add),
                          ["gI"], ["gI"])
                    P.act(lambda e: e.activation(out=gI[:], in_=gI[:], func=AF.Exp), ["gI"], ["gI"])
                    P.act(lambda e: e.activation(out=nb[:], in_=nb[:], func=AF.Exp), ["nb"], ["nb"])
                    P.act(lambda e: e.activation(out=dec[:], in_=dec[:], func=AF.Exp), ["dec"], ["dec"])
                    for t4 in range(0, NT, 4):
                        s_ = (t4 // 4) % 2
                        nt4 = min(4, NT - t4)
                        for j in range(nt4):
                            t = t4 + j
                            P.pe(lambda e, s_=s_, j=j, t=t: e.transpose(out=pG[s_][:, j, 0, :],
                                                                        in_=gI[:, t * 128:(t + 1) * 128],
                                                                        identity=id32[0:64, 0:64]),
                                 ["gI", "id32"], ["pG%d" % s_])
                            P.pe(lambda e, s_=s_, j=j, t=t: e.transpose(out=pG[s_][:, j, 1, :],
                                                                        in_=nb[:, t * 128:(t + 1) * 128],
                                                                        identity=id32[0:64, 0:64]),
                                 ["nb", "id32"], ["pG%d" % s_])
                        for d in range(2):
                            P.dve(lambda e, s_=s_, t4=t4, nt4=nt4, d=d: e.tensor_copy(
                                out=wkT[:, t4:t4 + nt4, d * 4:d * 4 + 4], in_=pG[s_][:, 0:nt4, 0, d * 32:d * 32 + 4]),
                                ["pG%d" % s_], ["wkT"])
                            P.dve(lambda e, s_=s_, t4=t4, nt4=nt4, d=d: e.tensor_copy(
                                out=thT[:, t4:t4 + nt4, d * 4:d * 4 + 4], in_=pG[s_][:, 0:nt4, 1, d * 32:d * 32 + 4]),
                                ["pG%d" % s_], ["thT"])
                    P.pool(lambda e: e.memset(wkH[:], 0.0), [], ["wkH"])
                    P.dve(lambda e: e.tensor_copy(out=wkH[0:64, :, 0, :], in_=wkT[0:64, :, :]), ["wkT", "wkH"], ["wkH"])
                    P.dve(lambda e: e.tensor_copy(out=wkH[64:128, :, 1, :], in_=wkT[64:128, :, :]), ["wkT", "wkH"], ["wkH"])
                    for d in range(2):
                        for hh in range(4):
                            P.pe(lambda e, d=d, hh=hh: e.matmul(pD[d][:, hh, :], lhsT=sel[:, d * 4 + hh, :], rhs=dec[:],
                                                                start=True, stop=True), ["sel", "dec"], ["pD%d" % d])
                        P.dve(lambda e, d=d: e.tensor_copy(out=decB[:, d * 4:d * 4 + 4, :], in_=pD[d][:]),
                              ["pD%d" % d], ["decB"])
                    P.barrier()

                for h in range(4):
                    with Scope(nc) as S:
                        qr = S.sb("qr", [128, T], BF16)
                        kr = S.sb("kr", [128, T], BF16)
                        acc = S.sb("acc", [128, T], F32)
                        qTm = S.sb("qTm", [128, T], BF16)
                        kTm = S.sb("kTm", [128, T], BF16)
                        qTp = S.sb("qTp", [128, NT, 2, 128], BF16)
                        ktm = S.sb("ktm", [128, NT, 128], BF16)
                        vau = S.sb("vau", [128, NT, 129], BF16)
                        otm = S.sb("otm", [128, NT, 128], BF16)
                        hsum = S.sb("hsum", [128, NT, 128], F32)
                        C32 = [[S.sb("C32_%d_%d" % (i, p_), [128, 129], F32) for p_ in range(2)] for i in range(2)]
                        Cbf = [[S.sb("Cbf_%d_%d" % (i, p_), [128, 129], BF16) for p_ in range(2)] for i in range(2)]
                        ptl = [[S.sb("ptl%d_%d" % (d, i), [128, 128], BF16) for i in range(2)] for d in range(2)]
                        kp = [[S.sb("kp%d_%d" % (d, i), [128, 128], BF16) for i in range(2)] for d in range(2)]
                        dn = [S.sb("dn%d" % d, [128, 1], F32) for d in range(2)]
                        sgm = S.sb("sgm", [128, 128], F32)
                        junk = S.sb("junk", [128, 128], F32)
                        s1 = S.sb("s1", [128, NT], F32)
                        s2 = S.sb("s2", [128, NT], F32)
                        mu = S.sb("mu", [128, NT], F32)
                        rsd = S.sb("rsd", [128, NT], F32)
                        hn = [S.sb("hn%d" % i, [128, 128], BF16) for i in range(2)]
                        mho = S.sb("mho", [128, T], BF16)
                        pKT = S.ps("pKT", [128, 8, 128], BF16)
                        pSm = [S.ps("pSm%d" % d, [128, 128]) for d in range(2)]
                        pIN = [S.ps("pIN%d" % d, [128, 129]) for d in range(2)]
                        pKV = [S.ps("pKV%d" % d, [128, 129]) for d in range(2)]
                        pHT = S.ps("pHT", [128, 8, 128], BF16)

                        P.dma("sp", qr[:], projT[(CQKM + h) * 128:(CQKM + 1 + h) * 128, :], r=["projT"], w=["qr"])
                        P.dma("sp", kr[:], projT[(CQKM + 4 + h) * 128:(CQKM + 5 + h) * 128, :], r=["projT"], w=["kr"])
                        P.dma("sp", mho[:], projM[1 + h, :, :], r=["projM"], w=["mho"])
                        P.dve(lambda e: e.tensor_copy(out=vau[:, :, 0:128], in_=mho[:].rearrange("p (t d) -> p t d", d=128)),
                              ["mho"], ["vau"])
                        P.pool(lambda e: e.memset(vau[:, :, 128:129], 1.0), [], ["vau"])
                        P.dma("sp", otm[:].rearrange("p t d -> p (t d)"), projM[5 + h, :, :], r=["projM"], w=["otm"])
                        P.pool(lambda e: e.memset(qTp[:], 0.0), [], ["qTp"])
                        P.pool(lambda e: e.memset(hsum[:], 0.0), [], ["hsum"])
                        for d in range(2):
                            P.pool(lambda e, d=d: e.memset(C32[d][0][:], 0.0), [], ["C32_%d_0" % d])
                            P.pool(lambda e, d=d: e.memset(Cbf[d][0][:], 0.0), [], ["Cbf_%d_0" % d])
                        for (src, sk_, dst, dk_, wc) in ((qr, "qr", qTm, "qTm", h), (kr, "kr", kTm, "kTm", 4 + h)):
                            P.dve(lambda e, src=src, wc=wc: e.tensor_scalar(out=acc[:], in0=src[:], scalar1=wmv[:, wc, 1:2],
                                                                            scalar2=None, op0=ALU.mult),
                                  [sk_, "wmv"], ["acc"])
                            for (a, b) in ((0, SEQ), (SEQ, T)):
                                P.dve(lambda e, src=src, wc=wc, a=a, b=b: e.scalar_tensor_tensor(
                                    out=acc[:, a + 1:b], in0=src[:, a:b - 1], scalar=wmv[:, wc, 0:1], in1=acc[:, a + 1:b],
                                    op0=ALU.mult, op1=ALU.add), [sk_, "wmv", "acc"], ["acc"])
                                P.dve(lambda e, src=src, wc=wc, a=a, b=b: e.scalar_tensor_tensor(
                                    out=acc[:, a:b - 1], in0=src[:, a + 1:b], scalar=wmv[:, wc, 2:3], in1=acc[:, a:b - 1],
                                    op0=ALU.mult, op1=ALU.add), [sk_, "wmv", "acc"], ["acc"])
                            P.act(lambda e, dst=dst: e.activation(out=dst[:], in_=acc[:], func=AF.Silu), ["acc"], [dk_])
                        q4 = qTm[:].rearrange("p (t two l) -> p t two l", two=2, l=64)
                        P.dve(lambda e: e.tensor_copy(out=qTp[:, :, 0, 0:64], in_=q4[:, :, 0, :]), ["qTm"], ["qTp"])
                        P.dve(lambda e: e.tensor_copy(out=qTp[:, :, 1, 64:128], in_=q4[:, :, 1, :]), ["qTm"], ["qTp"])
                        for t8 in range(0, NT, 8):
                            n8 = min(8, NT - t8)
                            for j in range(n8):
                                P.pe(lambda e, j=j, t=t8 + j: e.transpose(out=pKT[:, j, :], in_=kTm[:, t * 128:(t + 1) * 128],
                                                                          identity=idb[:]), ["kTm", "idb"], ["pKT"])
                            P.act(lambda e, t8=t8, n8=n8: e.copy(out=ktm[:, t8:t8 + n8, :], in_=pKT[:, 0:n8, :]),
                                  ["pKT"], ["ktm"])
                        done_tiles = [set(), set()]
                        slot = [0, 0]
                        ptl_of = [{}, {}]
                        for j in range(NCH):
                            for d, order in ((0, ORDER_F), (1, ORDER_B)):
                                c = order[j]
                                t = c // 2
                                r0 = (c % 2) * 64
                                col = d * 4 + h
                                ks = j % 2
                                cur, nxt = j % 2, (j + 1) % 2
                                P.act(lambda e, d=d, t=t, r0=r0, ks=ks, col=col: e.activation(
                                    out=kp[d][ks][:, :], in_=ktm[:, t, :], func=AF.Copy,
                                    scale=wkH[:, t, r0 // 64, col:col + 1]), ["ktm", "wkH"], ["kp%d_%d" % (d, ks)])
                                P.pe(lambda e, d=d, t=t, ks=ks: e.matmul(pKV[d][:], lhsT=kp[d][ks][:, :],
                                                                         rhs=vau[:, t, :], start=True, stop=True),
                                     ["kp%d_%d" % (d, ks), "vau"], ["pKV%d" % d])
                                P.dve(lambda e, d=d, c=c, col=col, cur=cur, nxt=nxt: e.scalar_tensor_tensor(
                                    out=C32[d][nxt][:], in0=C32[d][cur][:], scalar=decB[:, col, c:c + 1], in1=pKV[d][:],
                                    op0=ALU.mult, op1=ALU.add),
                                    ["C32_%d_%d" % (d, cur), "decB", "pKV%d" % d], ["C32_%d_%d" % (d, nxt)])
                                if j + 1 < NCH:
                                    c2 = order[j + 1]
                                    P.act(lambda e, d=d, c2=c2, col=col, nxt=nxt: e.activation(
                                        out=Cbf[d][nxt][:], in_=C32[d][nxt][:], func=AF.Copy,
                                        scale=decB[:, col, c2:c2 + 1]),
                                        ["C32_%d_%d" % (d, nxt), "decB"], ["Cbf_%d_%d" % (d, nxt)])
                                if t not in done_tiles[d]:
                                    done_tiles[d].add(t)
                                    sl = slot[d] % 2
                                    slot[d] += 1
                                    ptl_of[d][t] = sl
                                    P.pe(lambda e, d=d, t=t: e.matmul(pSm[d][:], lhsT=kTm[:, t * 128:(t + 1) * 128],
                                                                      rhs=qTm[:, t * 128:(t + 1) * 128], start=True,
                                                                      stop=True), ["kTm", "qTm"], ["pSm%d" % d])
                                    P.dve(lambda e, d=d, t=t, sl=sl, col=col: e.scalar_tensor_tensor(
                                        out=ptl[d][sl][:], in0=pSm[d][:], scalar=wkT[:, t, col:col + 1], in1=bmk[:, d, :],
                                        op0=ALU.mult, op1=ALU.mult), ["pSm%d" % d, "wkT", "bmk"], ["ptl%d_%d" % (d, sl)])
                                sl = ptl_of[d][t]
                                P.pe(lambda e, d=d, t=t, r0=r0, cur=cur: e.matmul(pIN[d][:], lhsT=qTp[:, t, r0 // 64, :],
                                                                                  rhs=Cbf[d][cur][:], start=True, stop=False),
                                     ["qTp", "Cbf_%d_%d" % (d, cur)], ["pIN%d" % d])
                                P.pe(lambda e, d=d, t=t, sl=sl: e.matmul(pIN[d][:], lhsT=ptl[d][sl][:, :],
                                                                         rhs=vau[:, t, :], start=False, stop=True),
                                     ["ptl%d_%d" % (d, sl), "vau"], ["pIN%d" % d])
                                P.act(lambda e, d=d, r0=r0: e.activation(
                                    out=dn[d][r0:r0 + 64, :], in_=pIN[d][r0:r0 + 64, 128:129], func=AF.Abs),
                                    ["pIN%d" % d], ["dn%d" % d])
                                P.dve(lambda e, d=d, t=t, r0=r0, col=col: e.tensor_tensor(
                                    out=dn[d][r0:r0 + 64, :], in0=dn[d][r0:r0 + 64, :],
                                    in1=thT[r0:r0 + 64, t, col:col + 1], op=ALU.max),
                                    ["dn%d" % d, "thT"], ["dn%d" % d])
                                P.dve(lambda e, d=d, r0=r0: e.reciprocal(out=dn[d][r0:r0 + 64, :], in_=dn[d][r0:r0 + 64, :]),
                                      ["dn%d" % d], ["dn%d" % d])
                                hk = "hsum_%d" % c
                                P.dve(lambda e, d=d, t=t, r0=r0: e.scalar_tensor_tensor(
                                    out=hsum[r0:r0 + 64, t, :], in0=pIN[d][r0:r0 + 64, 0:128], scalar=dn[d][r0:r0 + 64, :],
                                    in1=hsum[r0:r0 + 64, t, :], op0=ALU.mult, op1=ALU.add),
                                    ["pIN%d" % d, "dn%d" % d, hk, "hsum"], [hk])
                        hkeys = ["hsum_%d" % c for c in range(NCH)]
                        ntr = NT if not last else 32
                        P.dve(lambda e: e.memset(s1[:], 0.0), [], ["s1"])
                        P.dve(lambda e: e.memset(s2[:], 0.0), [], ["s2"])
                        for t in range(ntr):
                            hk2 = [hkeys[2 * t], hkeys[2 * t + 1]]
                            P.act(lambda e, t=t: e.activation(out=sgm[:], in_=otm[:, t, :], func=AF.Sigmoid),
                                  ["otm"], ["sgm"])
                            P.dve(lambda e, t=t: e.tensor_tensor(out=hsum[:, t, :], in0=hsum[:, t, :], in1=sgm[:],
                                                                 op=ALU.mult), hk2 + ["sgm", "hsum"], ["hg_%d" % t])
                            P.act(lambda e, t=t: e.activation(out=junk[:], in_=hsum[:, t, :], func=AF.Copy,
                                                              accum_out=s1[:, t:t + 1]), ["hg_%d" % t, "s1"],
                                  ["junk", "s1"])
                            P.act(lambda e, t=t: e.activation(out=junk[:], in_=hsum[:, t, :], func=AF.Square,
                                                              accum_out=s2[:, t:t + 1]), ["hg_%d" % t, "s2"],
                                  ["junk", "s2"])
                        P.dve(lambda e: e.tensor_scalar(out=mu[:], in0=s1[:], scalar1=1.0 / 128, scalar2=None, op0=ALU.mult),
                              ["s1"], ["mu"])
                        P.dve(lambda e: e.tensor_tensor(out=rsd[:], in0=mu[:], in1=mu[:], op=ALU.mult), ["mu"], ["rsd"])
                        P.dve(lambda e: e.scalar_tensor_tensor(out=rsd[:], in0=s2[:], scalar=1.0 / 128, in1=rsd[:],
                                                               op0=ALU.mult, op1=ALU.subtract), ["s2", "rsd"], ["rsd"])
                        P.act(lambda e: e.activation(out=rsd[:], in_=rsd[:], func=AF.Sqrt, bias=EPS, scale=1.0),
                              ["rsd"], ["rsd"])
                        P.dve(lambda e: e.reciprocal(out=rsd[:], in_=rsd[:]), ["rsd"], ["rsd"])
                        for t8 in range(0, ntr, 8):
                            n8 = min(8, ntr - t8)
                            for j in range(n8):
                                t = t8 + j
                                s_ = t % 2
                                P.dve(lambda e, t=t, s_=s_: e.tensor_scalar(out=hn[s_][:], in0=hsum[:, t, :],
                                                                            scalar1=mu[:, t:t + 1], scalar2=rsd[:, t:t + 1],
                                                                            op0=ALU.subtract, op1=ALU.mult),
                                      ["hg_%d" % t, "mu", "rsd"], ["hn%d" % s_])
                                P.pe(lambda e, j=j, s_=s_: e.transpose(out=pHT[:, j, :], in_=hn[s_][:], identity=idb[:]),
                                     ["hn%d" % s_, "idb"], ["pHT"])
                            P.act(lambda e, t8=t8, n8=n8: e.activation(
                                out=mho[:, t8 * 128:(t8 + n8) * 128].rearrange("p (a b) -> p a b", b=128),
                                in_=pHT[:, 0:n8, :], func=AF.Copy, scale=gmn[:, h:h + 1]), ["pHT", "gmn"], ["mho"])
                        P.dma("pool", mhT[h * 128:(h + 1) * 128, 0:ntr * 128], mho[:, 0:ntr * 128], r=["mho"], w=["mhT"])
                        P.barrier()

            phase(l, 5)
            with Scope(nc) as S:
                wA = S.sb("wA", [128, 4, D], BF16)
                wB = S.sb("wB", [128, 4, D], BF16)
                wC = S.sb("wC", [128, 4, D], BF16)
                wO = S.sb("wO", [128, 8, D], BF16)
                gt = S.sb("gt", [128, D], F32)
                aG = [S.sb("aG%d" % i, [128, 4, 512], BF16) for i in range(2)]
                bG = [S.sb("bG%d" % i, [128, 4, 512], BF16) for i in range(2)]
                cG = [S.sb("cG%d" % i, [128, 4, 512], BF16) for i in range(2)]
                br = [S.sb("br%d" % i, [128, 24, 512], BF16) for i in range(2)]
                m1 = [S.sb("m1_%d" % i, [128, 512], F32) for i in range(2)]
                m2_ = [S.sb("m2_%d" % i, [128, 512], F32) for i in range(2)]
                m3 = [S.sb("m3_%d" % i, [128, 512], F32) for i in range(2)]
                mg = S.sb("mg", [128, 8, 512], BF16)
                xt = [S.sb("xt%d" % i, [128, D], F32) for i in range(2)]
                ty = [S.sb("ty%d" % i, [128, 512], F32) for i in range(2)]
                pA = [S.ps("pA%d" % i, [128, 512]) for i in range(2)]
                pB = [S.ps("pB%d" % i, [128, 512]) for i in range(2)]
                pC = [S.ps("pC%d" % i, [128, 512]) for i in range(2)]
                pY = [S.ps("pY%d" % i, [128, 512]) for i in range(2)]
                P.dma("pool", wA[:], w_ao[l, :, :].rearrange("(c p) n -> p c n", p=128), w=["wA"])
                P.dma("pool", wB[:], w_pw[l, :, :].rearrange("(c p) n -> p c n", p=128), w=["wB"])
                P.dma("pool", wC[:], w_mo[l, :, :].rearrange("(c p) n -> p c n", p=128), w=["wC"])
                P.dma("pool", wO[:], w_out[l, :, :].rearrange("(c p) n -> p c n", p=128), w=["wO"])
                groups = GROUPS512 if not last else GROUPS512[:8]
                cur_stream = None
                ix = 0
                iy = 0
                for gi, (t0, n) in enumerate(groups):
                    stream = 0 if t0 < SEQ else 1
                    if stream != cur_stream:
                        cur_stream = stream
                        load_mod("sp", gt, l, stream, 2, "gt")
                    s_ = gi % 2
                    P.dma("sp", aG[s_][:, :, 0:n], attT[:, t0:t0 + n].rearrange("(c p) n -> p c n", p=128),
                          r=["attT"], w=["aG%d" % s_])
                    P.dma("sp", bG[s_][:, :, 0:n], cbT[:, t0:t0 + n].rearrange("(c p) n -> p c n", p=128),
                          r=["cbT"], w=["bG%d" % s_])
                    P.dma("sp", cG[s_][:, :, 0:n], mhT[:, t0:t0 + n].rearrange("(c p) n -> p c n", p=128),
                          r=["mhT"], w=["cG%d" % s_])
                    P.dma("sp", br[s_][:, :, 0:n],
                          projT[CBR * 128:CGATE * 128, t0:t0 + n].rearrange("(c p) n -> p c n", p=128),
                          r=["projT"], w=["br%d" % s_])
                    for j in range(8):
                        u = j % 2
                        for (pp, pn, ww, wn, src, sn) in ((pA, "pA", wA, "wA", aG, "aG"), (pB, "pB", wB, "wB", bG, "bG"),
                                                          (pC, "pC", wC, "wC", cG, "cG")):
                            for k in range(4):
                                P.pe(lambda e, pp=pp, ww=ww, src=src, u=u, k=k, j=j, s_=s_, n=n: e.matmul(
                                    pp[u][:, 0:n], lhsT=ww[:, k, j * 128:(j + 1) * 128], rhs=src[s_][:, k, 0:n],
                                    start=(k == 0), stop=(k == 3)), [wn, "%s%d" % (sn, s_)], ["%s%d" % (pn, u)])
                        brk = "br%d" % s_
                        P.dve(lambda e, u=u, s_=s_, j=j, n=n: e.tensor_tensor(out=m1[u][:, 0:n], in0=pA[u][:, 0:n],
                                                                              in1=br[s_][:, j, 0:n], op=ALU.mult),
                              ["pA%d" % u, brk], ["m1_%d" % u])
                        P.dve(lambda e, u=u, s_=s_, j=j, n=n: e.tensor_tensor(out=m2_[u][:, 0:n], in0=pB[u][:, 0:n],
                                                                              in1=br[s_][:, 8 + j, 0:n], op=ALU.mult),
                              ["pB%d" % u, brk], ["m2_%d" % u])
                        P.dve(lambda e, u=u, s_=s_, j=j, n=n: e.tensor_tensor(out=m3[u][:, 0:n], in0=pC[u][:, 0:n],
                                                                              in1=br[s_][:, 16 + j, 0:n], op=ALU.mult),
                              ["pC%d" % u, brk], ["m3_%d" % u])
                        P.pool(lambda e, u=u, n=n: e.tensor_tensor(out=m1[u][:, 0:n], in0=m1[u][:, 0:n], in1=m2_[u][:, 0:n],
                                                                   op=ALU.add), ["m1_%d" % u, "m2_%d" % u], ["m1_%d" % u])
                        P.pool(lambda e, u=u, j=j, n=n: e.tensor_tensor(out=mg[:, j, 0:n], in0=m1[u][:, 0:n],
                                                                        in1=m3[u][:, 0:n], op=ALU.add),
                               ["m1_%d" % u, "m3_%d" % u], ["mg%d" % j])
                    mgk = ["mg%d" % j for j in range(8)]
                    for jt in range(n // 128):
                        sx = ix % 2
                        ix += 1
                        xk = "xt%d" % sx
                        P.dma("sp", xt[sx][:], xs[t0 + jt * 128:t0 + (jt + 1) * 128, :], r=["xs"], w=[xk])
                        for cg in range(2):
                            sy = iy % 2
                            iy += 1
                            for k in range(8):
                                P.pe(lambda e, sy=sy, k=k, jt=jt, cg=cg: e.matmul(
                                    pY[sy][:], lhsT=mg[:, k, jt * 128:(jt + 1) * 128], rhs=wO[:, k, cg * 512:(cg + 1) * 512],
                                    start=(k == 0), stop=(k == 7)), [mgk[k], "wO"], ["pY%d" % sy])
                            P.dve(lambda e, sy=sy, cg=cg: e.tensor_tensor(out=ty[sy][:], in0=pY[sy][:],
                                                                          in1=gt[:, cg * 512:(cg + 1) * 512], op=ALU.mult),
                                  ["pY%d" % sy, "gt"], ["ty%d" % sy])
                            P.pool(lambda e, sy=sy, sx=sx, cg=cg: e.tensor_tensor(
                                out=xt[sx][:, cg * 512:(cg + 1) * 512], in0=xt[sx][:, cg * 512:(cg + 1) * 512],
                                in1=ty[sy][:], op=ALU.add), ["ty%d" % sy, xk], [xk])
                        P.dma("pool", xs[t0 + jt * 128:t0 + (jt + 1) * 128, :], xt[sx][:], r=[xk], w=["xs"])
                P.barrier()

            phase(l, 6)
            with Scope(nc) as S:
                wG = S.sb("wG", [128, 8, DFF], BF16)
                wU = S.sb("wU", [128, 8, DFF], BF16)
                wD = S.sb("wD", [128, 22, D], BF16)
                Gb = S.sb("Gb", [128, D], F32)
                Sb = S.sb("Sb", [128, D], F32)
                gt = S.sb("gt", [128, D], F32)
                gfin = S.sb("gfin", [128, D], F32)
                xg = S.sb("xg", [128, 2, D], F32)
                tmpf = S.sb("tmpf", [128, D], F32)
                hb = S.sb("hb", [128, D], BF16)
                hT = S.sb("hT", [128, 8, 256], BF16)
                ss = S.sb("ss", [128, 2], F32)
                rstd = S.sb("rstd", [128, 2], F32)
                aT = S.sb("aT", [128, 22, 256], BF16)
                sgl = [S.sb("sgl%d" % i, [128, 256], F32) for i in range(2)]
                ty = [S.sb("ty%d" % i, [128, 512], F32) for i in range(2)]
                pT = S.ps("pT", [128, 8, 128], BF16)
                pGa = [S.ps("pGa%d" % i, [128, 256]) for i in range(2)]
                pUa = [S.ps("pUa%d" % i, [128, 256]) for i in range(2)]
                pDn = [S.ps("pDn%d" % i, [128, 512]) for i in range(2)]
                for k in range(8):
                    P.dma("pool", wG[:, k, :], w_g[l, k * 128:(k + 1) * 128, :], w=["wG"])
                    P.dma("pool", wU[:, k, :], w_u[l, k * 128:(k + 1) * 128, :], w=["wU"])
                for k in range(0, 22, 2):
                    P.dma("pool", wD[:, k:k + 2, :], w_d[l, k * 128:(k + 2) * 128, :].rearrange("(c p) n -> p c n", p=128),
                          w=["wD"])
                groups = GROUPS256 if not last else GROUPS256[:16]
                cur_stream = None
                iu = 0
                iy = 0
                for gi, (t0, n) in enumerate(groups):
                    stream = 0 if t0 < SEQ else 1
                    if stream != cur_stream:
                        cur_stream = stream
                        load_mod("sp", Sb, l, stream, 3, "Sb")
                        load_mod("sp", tmpf, l, stream, 4, "tmpf")
                        P.dma("sp", Gb[:], g_ffn[l, :].partition_broadcast(128), w=["Gb"])
                        P.dve(lambda e: e.scalar_tensor_tensor(out=Gb[:], in0=tmpf[:], scalar=1.0, in1=Gb[:],
                                                               op0=ALU.add, op1=ALU.mult), ["tmpf", "Gb"], ["Gb"])
                        load_mod("sp", gt, l, stream, 5, "gt")
                    P.dve(lambda e: e.memset(ss[:], 0.0), [], ["ss"])
                    for j in range(2):
                        P.dma("sp", xg[:, j, :], xs[t0 + j * 128:t0 + (j + 1) * 128, :], r=["xs"], w=["xg%d" % j])
                        P.act(lambda e, j=j: e.activation(out=tmpf[:], in_=xg[:, j, :], func=AF.Square, scale=1.0 / 32,
                                                          accum_out=ss[:, j:j + 1]), ["xg%d" % j, "ss"], ["tmpf", "ss"])
                    P.act(lambda e: e.activation(out=rstd[:], in_=ss[:], func=AF.Sqrt, bias=EPS, scale=1.0),
                          ["ss"], ["rstd"])
                    P.dve(lambda e: e.reciprocal(out=rstd[:], in_=rstd[:]), ["rstd"], ["rstd"])
                    for j in range(2):
                        P.dve(lambda e, j=j: e.scalar_tensor_tensor(out=tmpf[:], in0=xg[:, j, :], scalar=rstd[:, j:j + 1],
                                                                    in1=Gb[:], op0=ALU.mult, op1=ALU.mult),
                              ["xg%d" % j, "rstd", "Gb"], ["tmpf"])
                        P.dve(lambda e: e.tensor_tensor(out=hb[:], in0=tmpf[:], in1=Sb[:], op=ALU.add),
                              ["tmpf", "Sb"], ["hb"])
                        for c in range(8):
                            P.pe(lambda e, c=c: e.transpose(out=pT[:, c, :], in_=hb[:, c * 128:(c + 1) * 128],
                                                            identity=idb[:]), ["hb", "idb"], ["pT"])
                        P.act(lambda e, j=j: e.copy(out=hT[:, :, j * 128:(j + 1) * 128], in_=pT[:]), ["pT"], ["hT"])
                    for f in range(22):
                        u = iu % 2
                        iu += 1
                        for k in range(8):
                            P.pe(lambda e, u=u, k=k, f=f: e.matmul(pGa[u][:], lhsT=wG[:, k, f * 128:(f + 1) * 128],
                                                                   rhs=hT[:, k, :], start=(k == 0), stop=(k == 7)),
                                 ["wG", "hT"], ["pGa%d" % u])
                        for k in range(8):
                            P.pe(lambda e, u=u, k=k, f=f: e.matmul(pUa[u][:], lhsT=wU[:, k, f * 128:(f + 1) * 128],
                                                                   rhs=hT[:, k, :], start=(k == 0), stop=(k == 7)),
                                 ["wU", "hT"], ["pUa%d" % u])
                        P.act(lambda e, u=u: e.activation(out=sgl[u][:], in_=pGa[u][:], func=AF.Silu),
                              ["pGa%d" % u], ["sgl%d" % u])
                        P.dve(lambda e, u=u, f=f: e.tensor_tensor(out=aT[:, f, :], in0=pUa[u][:], in1=sgl[u][:],
                                                                  op=ALU.mult), ["pUa%d" % u, "sgl%d" % u], ["aT%d" % f])
                    atk = ["aT%d" % f for f in range(22)]
                    for j in range(2):
                        xk = "xg%d" % j
                        for cg in range(2):
                            sy = iy % 2
                            iy += 1
                            for f in range(22):
                                P.pe(lambda e, sy=sy, f=f, j=j, cg=cg: e.matmul(
                                    pDn[sy][:], lhsT=aT[:, f, j * 128:(j + 1) * 128], rhs=wD[:, f, cg * 512:(cg + 1) * 512],
                                    start=(f == 0), stop=(f == 21)), [atk[f], "wD"], ["pDn%d" % sy])
                            P.dve(lambda e, sy=sy, cg=cg: e.tensor_tensor(out=ty[sy][:], in0=pDn[sy][:],
                                                                          in1=gt[:, cg * 512:(cg + 1) * 512], op=ALU.mult),
                                  ["pDn%d" % sy, "gt"], ["ty%d" % sy])
                            P.pool(lambda e, sy=sy, j=j, cg=cg: e.tensor_tensor(
                                out=xg[:, j, cg * 512:(cg + 1) * 512], in0=xg[:, j, cg * 512:(cg + 1) * 512],
                                in1=ty[sy][:], op=ALU.add), ["ty%d" % sy, xk], [xk])
                        if not last:
                            P.dma("pool", xs[t0 + j * 128:t0 + (j + 1) * 128, :], xg[:, j, :], r=[xk], w=["xs"])
                        else:
                            if gi == 0 and j == 0:
                                P.dma("sp", gfin[:], g_fin.partition_broadcast(128), r=[], w=["gfin"])
                            P.dve(lambda e: e.memset(ss[:, 0:1], 0.0), [], ["ss"])
                            P.act(lambda e, j=j: e.activation(out=tmpf[:], in_=xg[:, j, :], func=AF.Square, scale=1.0 / 32,
                                                              accum_out=ss[:, 0:1]), [xk, "ss"], ["tmpf", "ss"])
                            P.act(lambda e: e.activation(out=rstd[:, 0:1], in_=ss[:, 0:1], func=AF.Sqrt, bias=EPS,
                                                         scale=1.0), ["ss"], ["rstd"])
                            P.dve(lambda e: e.reciprocal(out=rstd[:, 0:1], in_=rstd[:, 0:1]), ["rstd"], ["rstd"])
                            P.dve(lambda e, j=j: e.scalar_tensor_tensor(out=xg[:, j, :], in0=xg[:, j, :],
                                                                        scalar=rstd[:, 0:1], in1=gfin[:], op0=ALU.mult,
                                                                        op1=ALU.mult), [xk, "rstd", "gfin"], [xk])
                            P.dma("pool", out[t0 + j * 128:t0 + (j + 1) * 128, :], xg[:, j, :], r=[xk], w=["out"])
                P.barrier()
        P.enabled = True
        if n_layers < DEPTH:
            with Scope(nc) as S:
                xo = S.sb("xo", [128, D], F32)
                for t in range(32):
                    P.dma("sp", xo[:], xs[t * 128:(t + 1) * 128, :], r=["xs"], w=["xo"])
                    P.dma("sp", out[t * 128:(t + 1) * 128, :], xo[:], r=["xo"], w=["out"])
                P.barrier()
        P.barrier()
        stats = P.emit()
    return nc, stats


def _consts():
    ident = np.eye(128, dtype=np.float32)
    rperm = np.zeros((128, 128), np.float32)
    sign = np.zeros(128, np.float32)
    for m in range(128):
        d = m % 64
        base = m - d
        hb = (d // 32) * 32
        dd = d % 32
        if dd < 16:
            pm, sg = hb + dd + 16, -1.0
        else:
            pm, sg = hb + dd - 16, 1.0
        rperm[base + pm, m] = 1.0
        sign[m] = sg
    t = np.arange(SEQ)
    row = (t // 64).astype(np.float32)
    col = (t % 64).astype(np.float32)
    inv = (10000.0 ** (-np.arange(16, dtype=np.float32) / 16)).astype(np.float32)
    ang_r = row[:, None] * inv[None, :]
    ang_c = col[:, None] * inv[None, :]
    ang = np.concatenate([ang_r, ang_r, ang_c, ang_c], axis=-1).astype(np.float32)
    cos = np.cos(ang).astype(np.float32).T
    sin = np.sin(ang).astype(np.float32).T
    cosT = np.ascontiguousarray(np.concatenate([cos, cos], axis=0))
    sinT = np.ascontiguousarray(np.concatenate([sin, sin], axis=0) * sign[:, None]).astype(np.float32)
    a = np.arange(128)
    mprev = (a[:, None] >= a[None, :]).astype(np.float32)
    mnext = (a[:, None] <= a[None, :]).astype(np.float32)
    same = (a[:, None] // 64) == (a[None, :] // 64)
    bmf = (same & (a[:, None] <= a[None, :])).astype(np.float32)
    bmb = (same & (a[:, None] >= a[None, :])).astype(np.float32)
    sel = np.zeros((64, 8, 128), np.float32)
    for d in range(2):
        for h in range(4):
            sel[d * 32 + h, d * 4 + h, :] = 1.0
    return dict(ident=ident, rperm=rperm, cosT=cosT, sinT=sinT, mprev=mprev, mnext=mnext, bmf=bmf, bmb=bmb, sel=sel)


def _prep(inp):
    f = lambda a: np.ascontiguousarray(np.asarray(a, dtype=np.float32))
    w_in = f(inp["w_in"])
    L = w_in.shape[0]
    qa = w_in[:, :, 0:512]
    ka = w_in[:, :, 512:640]
    va = w_in[:, :, 640:768]
    glu = w_in[:, :, 768:1792]
    qkm = w_in[:, :, 1792:2816]
    vm = w_in[:, :, 2816:3328]
    om = w_in[:, :, 3328:3840]
    gm = w_in[:, :, 3840:3856]
    br = w_in[:, :, 3856:6928]
    gch = np.zeros((L, D, 128), np.float32)
    gch[:, :, 0:4] = gm[:, :, 0:4]
    gch[:, :, 32:36] = gm[:, :, 4:8]
    gch[:, :, 64:68] = gm[:, :, 8:12]
    gch[:, :, 96:100] = gm[:, :, 12:16]
    z64 = np.zeros((L, D, 64), np.float32)
    w_fm = np.concatenate([qa, ka[:, :, 0:64], z64, z64, ka[:, :, 0:64], ka[:, :, 64:128], z64, z64, ka[:, :, 64:128],
                           glu, qkm, br, gch], axis=2)
    assert w_fm.shape[2] == NFM * 128
    w_tm = np.concatenate([va, vm, om], axis=2)
    bmg = f(inp["b_mgate"])
    bm = np.zeros((L, 64, 2), np.float32)
    bm[:, 0:4, 0] = bmg[:, 0:4]
    bm[:, 32:36, 0] = bmg[:, 4:8]
    bm[:, 0:4, 1] = bmg[:, 8:12]
    bm[:, 32:36, 1] = bmg[:, 12:16]
    cv = np.zeros((L, 512, 34), np.float32)
    cv[:, :, 0:31] = np.transpose(f(inp["w_conv_dw"]), (0, 2, 1))
    cv[:, :, 31] = f(inp["b_conv_dw"])
    cv[:, :, 32] = f(inp["g_conv_ln"])
    cv[:, :, 33] = f(inp["b_conv_ln"])
    cvec = np.ascontiguousarray(cv.reshape(L, 4, 128, 34).transpose(0, 2, 1, 3))
    wm = np.ascontiguousarray(np.transpose(f(inp["w_mconv"]), (0, 2, 1)).reshape(L, 8, 128, 3).transpose(0, 2, 1, 3))
    gmn = np.ascontiguousarray(f(inp["g_mlstm_norm"]).reshape(L, 4, 128).transpose(0, 2, 1))
    common = dict(
        w_ada=f(inp["w_ada"]), b_ada=f(inp["b_ada"]), g_norm_mix=f(inp["g_norm_mix"]), g_norm_ffn=f(inp["g_norm_ffn"]),
        w_fm=np.ascontiguousarray(w_fm), w_tm=np.ascontiguousarray(w_tm), bm=bm, att_sink=f(inp["att_sink"]),
        w_att_out=f(inp["w_att_out"]), cvec=cvec, w_conv_pw=f(inp["w_conv_pw"]), wm=wm, gmn=gmn,
        w_mlstm_out=f(inp["w_mlstm_out"]), w_out=f(inp["w_out"]), w_ff_gate=f(inp["w_ff_gate"]),
        w_ff_up=f(inp["w_ff_up"]), w_ff_down=f(inp["w_ff_down"]), g_final=f(inp["g_final"]))
    common.update(_consts())
    x = f(inp["x"])
    ctx = f(inp["ctx"])
    c = f(inp["c"])
    c_ctx = f(inp["c_ctx"])
    maps = []
    for core in range(8):
        b = core % 4
        cc = np.stack([c[b], c_ctx], axis=-1).reshape(8, 128, 2).transpose(1, 0, 2)
        m = dict(common)
        m["x"] = np.ascontiguousarray(x[b])
        m["ctx"] = np.ascontiguousarray(ctx[b])
        m["cc"] = np.ascontiguousarray(cc)
        maps.append(m)
    return maps


_NC_CACHE = {}


def kernel(**inputs):
    maps = _prep(inputs)
    if "nc" not in _NC_CACHE:
        _NC_CACHE["nc"] = build()[0]
    nc = _NC_CACHE["nc"]
    res = run_bass_kernel_spmd(nc, maps, core_ids=list(range(8)))
    outs = [np.asarray(res.results[b]["out"], dtype=np.float32) for b in range(4)]
    return np.stack(outs, axis=0)
```

```python
import contextlib
import os
import numpy as np
KD = int(os.environ.get('KD', '9'))
import concourse.bass as bass
import concourse.mybir as mybir
from concourse.bass_utils import run_bass_kernel_spmd

F32 = mybir.dt.float32
BF16 = mybir.dt.bfloat16
AF = mybir.ActivationFunctionType
ALU = mybir.AluOpType
AX = mybir.AxisListType

COMPUTE = ("pe", "act", "dve", "pool", "sp")
EPOCH = 6000
DMA_USES = 1800


class _Rec:
    def __getattr__(self, name):
        def f(*a, **k):
            self.call = (name, a, k)
            return self
        return f


class Prog:
    def __init__(self, nc, n_dma_sems=10):
        self.nc = nc
        self.ins = []
        self.per_eng = {e: [] for e in COMPUTE}
        self.last_writer = {}
        self.readers = {}
        self.nstreams = len(COMPUTE)
        self.stream_id = {e: i for i, e in enumerate(COMPUTE)}
        self.stream_pos = [0] * self.nstreams
        self.stream_last = {}
        self.ic = {e: [0] * self.nstreams for e in COMPUTE}
        self.vc_snap = []
        self.dma_pool = {}
        self.dma_rr = {}
        self.dma_last = {}
        self.dma_uses = {}
        self.n_dma_sems = n_dma_sems
        self.signal = set()

    def _new_stream(self):
        sid = self.nstreams
        self.nstreams += 1
        self.stream_pos.append(0)
        for e in COMPUTE:
            self.ic[e] = self.ic[e] + [0]
        return sid

    def _dma_stream(self, q):
        pool = self.dma_pool.setdefault(q, [])
        if len(pool) < self.n_dma_sems:
            sid = self._new_stream()
            pool.append(sid)
            self.dma_uses[sid] = 0
            self.dma_rr[q] = len(pool) - 1
            return sid
        k = (self.dma_rr[q] + 1) % len(pool)
        self.dma_rr[q] = k
        sid = pool[k]
        if self.dma_uses[sid] >= DMA_USES:
            sid = self._new_stream()
            pool[k] = sid
            self.dma_uses[sid] = 0
        return sid

    enabled = True

    def add(self, eng, fn, reads=(), writes=(), dma=False, force=False, extra=()):
        if not self.enabled:
            return None
        rec = _Rec()
        fn(rec)
        fn = (lambda call: (lambda e: getattr(e, call[0])(*call[1], **call[2])))(rec.call)
        iid = len(self.ins)
        deps = set(extra)
        for k in reads:
            w = self.last_writer.get(k)
            if w is not None:
                deps.add(w)
        for k in writes:
            w = self.last_writer.get(k)
            if w is not None:
                deps.add(w)
            for r in self.readers.get(k, ()):
                deps.add(r)
        if dma:
            sid = self._dma_stream(eng)
            prev = self.dma_last.get(sid)
            if prev is not None:
                deps.add(prev)
            self.dma_last[sid] = iid
            self.dma_uses[sid] += 1
        else:
            sid = self.stream_id[eng]
        self.stream_pos[sid] += 1
        pos = self.stream_pos[sid]
        self.stream_last[sid] = iid
        ic = self.ic[eng]
        waits = []
        rset = set(reads)
        for d in sorted(deps):
            deng, _, ddma, _, dsid, dpos = self.ins[d]
            if not ddma and deng == eng and not dma and not force:
                if eng == "pe":
                    continue
                raw = False
                for k in rset:
                    if self.last_writer.get(k) == d:
                        raw = True
                        break
                if not raw:
                    continue
            if len(ic) < self.nstreams:
                ic = ic + [0] * (self.nstreams - len(ic))
            if ic[dsid] >= dpos:
                continue
            waits.append(d)
            self.signal.add(d)
            snap, s2, p2 = self.vc_snap[d]
            if len(snap) < self.nstreams:
                snap = snap + [0] * (self.nstreams - len(snap))
            new = [a if a > b else b for a, b in zip(ic, snap)]
            if new[s2] < p2:
                new[s2] = p2
            ic = new
        self.ic[eng] = ic
        self.vc_snap.append((ic, sid, pos))
        self.ins.append((eng, fn, dma, waits, sid, pos))
        self.per_eng[eng].append(iid)
        for k in reads:
            self.readers.setdefault(k, []).append(iid)
        for k in writes:
            self.last_writer[k] = iid
            self.readers[k] = []
        return iid

    def pe(self, fn, r=(), w=()):
        return self.add("pe", fn, r, w)

    def act(self, fn, r=(), w=()):
        return self.add("act", fn, r, w)

    def dve(self, fn, r=(), w=()):
        return self.add("dve", fn, r, w)

    def pool(self, fn, r=(), w=()):
        return self.add("pool", fn, r, w)

    def dma(self, q, out, in_, r=(), w=(), **kw):
        return self.add(q, lambda e: e.dma_start(out=out, in_=in_, **kw), r, w, dma=True)

    def barrier(self):
        lasts = list(self.stream_last.values())
        for e in COMPUTE:
            self.add(e, lambda en: en.nop(), (), (), force=True, extra=lasts)

    def emit(self):
        nc = self.nc
        sig_count = {}
        stream_sig = [0] * self.nstreams
        for iid, (eng, fn, dma, waits, sid, pos) in enumerate(self.ins):
            if dma:
                sig_count[iid] = 16 * pos
            elif iid in self.signal:
                stream_sig[sid] += 1
                sig_count[iid] = stream_sig[sid]
        stack = contextlib.ExitStack()
        sems = {}

        def sem_for(sid, cnt):
            if sid < len(COMPUTE):
                ep = (cnt - 1) // EPOCH
                key = (sid, ep)
                val = cnt - ep * EPOCH
            else:
                key = (sid, 0)
                val = cnt
            if key not in sems:
                sems[key] = stack.enter_context(nc.semaphore("s%d_%d" % key))
            return sems[key], val

        for iid in sorted(sig_count):
            sem_for(self.ins[iid][4], sig_count[iid])
        with stack:
            with nc.Block() as block:
                def run(engname):
                    def body(e):
                        for iid in self.per_eng[engname]:
                            _, fn, dma, waits, sid, pos = self.ins[iid]
                            for d in waits:
                                s, v = sem_for(self.ins[d][4], sig_count[d])
                                e.wait_ge(s, v)
                            r = fn(e)
                            if iid in sig_count:
                                s, v = sem_for(sid, sig_count[iid])
                                r.then_inc(s, 16 if dma else 1)
                    return body
                block.sync(run("sp"))
                block.gpsimd(run("pool"))
                block.scalar(run("act"))
                block.vector(run("dve"))
                block.tensor(run("pe"))
        return len(self.ins), len(sems)


D = 1024
SEQ = 4096
CTX = 256
T = SEQ + CTX
NT = T // 128
NCH = T // 64
DEPTH = 4
DFF = 2816
NFM = 49
CK, CGLU, CQKM, CBR, CGATE = 4, 8, 16, 24, 48
NTM = 1152
LN_DK = float(np.log(128.0 ** -0.5))
EPS = 1e-6

GROUPS512 = [(g * 512, 512) for g in range(8)] + [(4096, 256)]
GROUPS256 = [(g * 256, 256) for g in range(17)]
ORDER_F = [64, 65, 66, 67] + list(range(64))
ORDER_B = [67, 66, 65, 64] + list(range(63, -1, -1))


class Scope:
    uid = 0

    def __init__(self, nc):
        self.nc = nc
        self.st = contextlib.ExitStack()
        self.n = 0

    def __enter__(self):
        self.st.__enter__()
        return self

    def __exit__(self, *a):
        return self.st.__exit__(*a)

    def sb(self, name, shape, dt):
        Scope.uid += 1
        return self.st.enter_context(self.nc.sbuf_tensor("%s_u%d" % (name, Scope.uid), list(shape), dt))

    def ps(self, name, shape, dt=F32):
        Scope.uid += 1
        return self.st.enter_context(self.nc.psum_tensor("%s_u%d" % (name, Scope.uid), list(shape), dt))


def build(n_layers=DEPTH, debug=False, stop=None):
    nc = bass.Bass("TRN2", target_bir_lowering=False)
    P = Prog(nc)

    def phase(l, k):
        P.enabled = stop is None or (l, k) <= stop

    def din(name, shape, dt=F32):
        return nc.dram_tensor(name, list(shape), dt, kind="ExternalInput").ap()

    def dscr(name, shape, dt):
        kind = "ExternalOutput" if debug else "Internal"
        return nc.dram_tensor(name, list(shape), dt, kind=kind).ap()

    x_in = din("x", [SEQ, D])
    ctx_in = din("ctx", [CTX, D])
    cc_in = din("cc", [128, 8, 2])
    w_ada = din("w_ada", [DEPTH, D, 6 * D])
    b_ada = din("b_ada", [DEPTH, 6 * D])
    g_mix = din("g_norm_mix", [DEPTH, D])
    g_ffn = din("g_norm_ffn", [DEPTH, D])
    w_fm = din("w_fm", [DEPTH, D, NFM * 128])
    w_tm = din("w_tm", [DEPTH, D, NTM])
    bm_in = din("bm", [DEPTH, 64, 2])
    sink_in = din("att_sink", [DEPTH, 8])
    w_ao = din("w_att_out", [DEPTH, 512, D])
    cvec_in = din("cvec", [DEPTH, 128, 4, 34])
    w_pw = din("w_conv_pw", [DEPTH, 512, D])
    wm_in = din("wm", [DEPTH, 128, 8, 3])
    gmn_in = din("gmn", [DEPTH, 128, 4])
    w_mo = din("w_mlstm_out", [DEPTH, 512, D])
    w_out = din("w_out", [DEPTH, D, D])
    w_g = din("w_ff_gate", [DEPTH, D, DFF])
    w_u = din("w_ff_up", [DEPTH, D, DFF])
    w_d = din("w_ff_down", [DEPTH, DFF, D])
    g_fin = din("g_final", [D])
    ident_in = din("ident", [128, 128])
    rperm_in = din("rperm", [128, 128])
    cos_in = din("cosT", [128, SEQ])
    sin_in = din("sinT", [128, SEQ])
    mprev_in = din("mprev", [128, 128])
    mnext_in = din("mnext", [128, 128])
    bmf_in = din("bmf", [128, 128])
    bmb_in = din("bmb", [128, 128])
    sel_in = din("sel", [64, 8, 128])
    out = nc.dram_tensor("out", [SEQ, D], F32, kind="ExternalOutput").ap()

    xs = dscr("xs", [T, D], F32)
    modd = dscr("modd", [DEPTH, 2, 6 * D], F32)
    projT = dscr("projT", [48 * 128, T], BF16)
    gatesT = dscr("gatesT", [128, T], F32)
    projM = dscr("projM", [9, 128, NT * 128], BF16)
    attT = dscr("attT", [512, T], BF16)
    cbT = dscr("cbT", [512, T], BF16)
    mhT = dscr("mhT", [512, T], BF16)

    with Scope(nc) as G:
        id32 = G.sb("id32", [128, 128], F32)
        idb = G.sb("idb", [128, 128], BF16)
        ones32 = G.sb("ones32", [128, 128], F32)
        P.dma("sp", id32[:], ident_in[:, :], w=["id32"])
        P.dve(lambda e: e.tensor_copy(out=idb[:], in_=id32[:]), ["id32"], ["idb"])
        P.dve(lambda e: e.memset(ones32[:], 1.0), [], ["ones32"])
        for i in range(16):
            P.dma("sp", xs[i * 256:(i + 1) * 256, :], x_in[i * 256:(i + 1) * 256, :], w=["xs_i%d" % i])
        P.dma("sp", xs[SEQ:T, :], ctx_in[:, :], w=["xs_c"])

        with Scope(nc) as S:
            cc = S.sb("cc", [128, 8, 2], F32)
            scc = S.sb("scc", [128, 8, 2], F32)
            wa = [S.sb("wa%d" % i, [128, 8, 512], F32) for i in range(2)]
            bad = S.sb("bad", [2, 6 * D], F32)
            mrow = S.sb("mrow", [2, 6 * D], F32)
            pm = [S.ps("pm%d" % i, [2, 512]) for i in range(2)]
            P.dma("sp", cc[:], cc_in[:, :, :], w=["cc"])
            P.act(lambda e: e.activation(out=scc[:], in_=cc[:], func=AF.Silu), ["cc"], ["scc"])
            it = 0
            for l in range(n_layers):
                P.dma("sp", bad[:], b_ada[l, :].partition_broadcast(2), w=["bad"])
                for cg in range(12):
                    s = it % 2
                    it += 1
                    P.dma("sp", wa[s][:], w_ada[l, :, cg * 512:(cg + 1) * 512].rearrange("(c p) n -> p c n", p=128),
                          w=["wa%d" % s])
                    for k in range(8):
                        P.pe(lambda e, s=s, k=k: e.matmul(pm[s][:], lhsT=scc[:, k, :], rhs=wa[s][:, k, :],
                                                         start=(k == 0), stop=(k == 7)),
                             ["scc", "wa%d" % s], ["pm%d" % s])
                    P.dve(lambda e, s=s, cg=cg: e.tensor_tensor(out=mrow[:, cg * 512:(cg + 1) * 512], in0=pm[s][:],
                                                               in1=bad[:, cg * 512:(cg + 1) * 512], op=ALU.add),
                          ["pm%d" % s, "bad"], ["mrow"])
                P.dma("pool", modd[l, :, :], mrow[:], r=["mrow"], w=["modd"])
            P.barrier()

        def load_mod(eng_q, tile, l, stream, k, key):
            P.dma(eng_q, tile[:], modd[l, stream, k * D:(k + 1) * D].partition_broadcast(128), r=["modd"], w=[key])

        def make_gain(gt, gkey, sct, sckey, gvec_dram, tmp, tmpkey):
            P.dma("sp", tmp[:], gvec_dram.partition_broadcast(128), w=[tmpkey])
            P.dve(lambda e: e.scalar_tensor_tensor(out=gt[:], in0=sct[:], scalar=1.0, in1=tmp[:],
                                                   op0=ALU.add, op1=ALU.mult),
                  [sckey, tmpkey], [gkey])

        for l in range(n_layers):
            last = (l == DEPTH - 1)
            phase(l, 1)
            with Scope(nc) as S:
                wfm = S.sb("wfm", [128, 8, NFM * 128], BF16)
                wtm = S.sb("wtm", [128, 8, NTM], BF16)
                cosT = S.sb("cosT", [128, SEQ], F32)
                sinT = S.sb("sinT", [128, SEQ], F32)
                rp32 = S.sb("rp32", [128, 128], F32)
                rpb = S.sb("rpb", [128, 128], BF16)
                Gb = S.sb("Gb", [128, D], F32)
                Sb = S.sb("Sb", [128, D], F32)
                xg = S.sb("xg", [128, 4, D], F32)
                tmpf = S.sb("tmpf", [128, D], F32)
                hb = S.sb("hb", [128, D], BF16)
                hT = S.sb("hT", [128, 8, 512], BF16)
                ss = S.sb("ss", [128, 4], F32)
                rstd = S.sb("rstd", [128, 4], F32)
                fo = [S.sb("fo%d" % i, [128, 512], BF16) for i in range(3)]
                qraw = [S.sb("qraw%d" % i, [128, 512], BF16) for i in range(2)]
                t1 = [S.sb("t1%d" % i, [128, 512], F32) for i in range(2)]
                t2 = [S.sb("t2%d" % i, [128, 512], F32) for i in range(2)]
                go = S.sb("go", [128, 512], F32)
                tmo = [S.sb("tmo%d" % i, [128, NTM], BF16) for i in range(2)]
                pT = S.ps("pT", [128, 8, 128], BF16)
                pTM = [S.ps("pTM%d" % i, [128, 512]) for i in range(2)]
                pFM = [S.ps("pFM%d" % i, [128, 512]) for i in range(3)]
                pR = S.ps("pR", [128, 512])

                for k in range(8):
                    P.dma("pool", wfm[:, k, :], w_fm[l, k * 128:(k + 1) * 128, :], w=["wfm%d" % k])
                P.dma("pool", wtm[:], w_tm[l, :, :].rearrange("(c p) n -> p c n", p=128), w=["wtm"])
                wfm_keys = ["wfm%d" % k for k in range(8)]
                P.dma("sp", cosT[:], cos_in[:, :], w=["cosT"])
                P.dma("sp", sinT[:], sin_in[:, :], w=["sinT"])
                P.dma("sp", rp32[:], rperm_in[:, :], w=["rp32"])
                P.dve(lambda e: e.tensor_copy(out=rpb[:], in_=rp32[:]), ["rp32"], ["rpb"])
                ifm = 0
                ifo = 0
                iq = 0
                itm = 0
                cur_stream = None
                for gi, (t0, n) in enumerate(GROUPS512):
                    stream = 0 if t0 < SEQ else 1
                    if stream != cur_stream:
                        cur_stream = stream
                        load_mod("sp", Sb, l, stream, 0, "Sb")
                        load_mod("sp", tmpf, l, stream, 1, "tmpf")
                        P.dma("sp", Gb[:], g_mix[l, :].partition_broadcast(128), w=["Gb"])
                        P.dve(lambda e: e.scalar_tensor_tensor(out=Gb[:], in0=tmpf[:], scalar=1.0, in1=Gb[:],
                                                               op0=ALU.add, op1=ALU.mult),
                              ["tmpf", "Gb"], ["Gb"])
                    ntl = n // 128
                    P.dve(lambda e: e.memset(ss[:], 0.0), [], ["ss"])
                    for j in range(ntl):
                        P.dma("sp", xg[:, j, :], xs[t0 + j * 128:t0 + (j + 1) * 128, :], r=["xs"], w=["xg%d" % j])
                        P.act(lambda e, j=j: e.activation(out=tmpf[:], in_=xg[:, j, :], func=AF.Square, scale=1.0 / 32,
                                                          accum_out=ss[:, j:j + 1]),
                              ["xg%d" % j, "ss"], ["tmpf", "ss"])
                    P.act(lambda e: e.activation(out=rstd[:], in_=ss[:], func=AF.Sqrt, bias=EPS, scale=1.0),
                          ["ss"], ["rstd"])
                    P.dve(lambda e: e.reciprocal(out=rstd[:], in_=rstd[:]), ["rstd"], ["rstd"])
                    for j in range(ntl):
                        P.dve(lambda e, j=j: e.scalar_tensor_tensor(out=tmpf[:], in0=xg[:, j, :], scalar=rstd[:, j:j + 1],
                                                                    in1=Gb[:], op0=ALU.mult, op1=ALU.mult),
                              ["xg%d" % j, "rstd", "Gb"], ["tmpf"])
                        P.dve(lambda e: e.tensor_tensor(out=hb[:], in0=tmpf[:], in1=Sb[:], op=ALU.add),
                              ["tmpf", "Sb"], ["hb"])
                        for c in range(8):
                            P.pe(lambda e, c=c: e.transpose(out=pT[:, c, :], in_=hb[:, c * 128:(c + 1) * 128],
                                                            identity=idb[:]),
                                 ["hb", "idb"], ["pT"])
                        P.act(lambda e, j=j: e.copy(out=hT[:, :, j * 128:(j + 1) * 128], in_=pT[:]), ["pT"], ["hT"])
                    for j in range(ntl):
                        so = itm % 2
                        itm += 1
                        for (c0, cn) in ((0, 512), (512, 512), (1024, 128)):
                            sp_ = ifm % 2
                            ifm += 1
                            for k in range(8):
                                P.pe(lambda e, sp_=sp_, k=k, j=j, c0=c0, cn=cn: e.matmul(
                                    pTM[sp_][:, 0:cn], lhsT=hT[:, k, j * 128:(j + 1) * 128], rhs=wtm[:, k, c0:c0 + cn],
                                    start=(k == 0), stop=(k == 7)), ["hT", "wtm"], ["pTM%d" % sp_])
                            P.dve(lambda e, sp_=sp_, so=so, c0=c0, cn=cn: e.tensor_copy(
                                out=tmo[so][:, c0:c0 + cn], in_=pTM[sp_][:, 0:cn]), ["pTM%d" % sp_], ["tmo%d" % so])
                        tt_ = (t0 // 128) + j
                        P.dma("pool", projM[:, :, tt_ * 128:(tt_ + 1) * 128].rearrange("b p d -> p b d"),
                              tmo[so][:].rearrange("p (b d) -> p b d", d=128), r=["tmo%d" % so], w=["projM"])
                    for c in range(NFM):
                        sp_ = ifm % 3
                        ifm += 1
                        for k in range(8):
                            P.pe(lambda e, sp_=sp_, k=k, c=c, n=n: e.matmul(
                                pFM[sp_][:, 0:n], lhsT=wfm[:, k, c * 128:(c + 1) * 128], rhs=hT[:, k, 0:n],
                                start=(k == 0), stop=(k == 7)), ["hT", "wfm%d" % k], ["pFM%d" % sp_])
                        pk = "pFM%d" % sp_
                        if c == CGATE:
                            P.act(lambda e, sp_=sp_, n=n: e.copy(out=go[:, 0:n], in_=pFM[sp_][:, 0:n]), [pk], ["go"])
                            P.dma("pool", gatesT[:, t0:t0 + n], go[:, 0:n], r=["go"], w=["gatesT"])
                            continue
                        so = ifo % 3
                        ifo += 1
                        fk = "fo%d" % so
                        if c < CGLU and stream == 0:
                            sq = iq % 2
                            iq += 1
                            qk_ = "qraw%d" % sq
                            P.act(lambda e, sp_=sp_, sq=sq, n=n: e.copy(out=qraw[sq][:, 0:n], in_=pFM[sp_][:, 0:n]),
                                  [pk], [qk_])
                            P.pe(lambda e, sq=sq, n=n: e.matmul(pR[:, 0:n], lhsT=rpb[:], rhs=qraw[sq][:, 0:n],
                                                                start=True, stop=True), [qk_, "rpb"], ["pR"])
                            P.dve(lambda e, sq=sq, n=n, t0=t0: e.tensor_tensor(out=t1[sq][:, 0:n], in0=qraw[sq][:, 0:n],
                                                                               in1=cosT[:, t0:t0 + n], op=ALU.mult),
                                  [qk_, "cosT"], ["t1%d" % sq])
                            P.dve(lambda e, sq=sq, n=n, t0=t0: e.tensor_tensor(out=t2[sq][:, 0:n], in0=pR[:, 0:n],
                                                                               in1=sinT[:, t0:t0 + n], op=ALU.mult),
                                  ["pR", "sinT"], ["t2%d" % sq])
                            P.pool(lambda e, sq=sq, so=so, n=n: e.tensor_tensor(out=fo[so][:, 0:n], in0=t2[sq][:, 0:n],
                                                                                in1=t1[sq][:, 0:n], op=ALU.add),
                                   ["t2%d" % sq, "t1%d" % sq], [fk])
                        elif CBR <= c < CGATE:
                            P.act(lambda e, sp_=sp_, so=so, n=n: e.activation(out=fo[so][:, 0:n], in_=pFM[sp_][:, 0:n],
                                                                              func=AF.Sigmoid), [pk], [fk])
                        else:
                            if c % 2 == 0:
                                P.act(lambda e, sp_=sp_, so=so, n=n: e.copy(out=fo[so][:, 0:n], in_=pFM[sp_][:, 0:n]),
                                      [pk], [fk])
                            else:
                                P.dve(lambda e, sp_=sp_, so=so, n=n: e.tensor_copy(out=fo[so][:, 0:n],
                                                                                   in_=pFM[sp_][:, 0:n]), [pk], [fk])
                        P.dma("pool", projT[c * 128:(c + 1) * 128, t0:t0 + n], fo[so][:, 0:n], r=[fk], w=["projT"])
                P.barrier()

            phase(l, 2)
            with Scope(nc) as S:
                qT = S.sb("qT", [128, 4, T], BF16)
                kT = S.sb("kT", [128, 4, T], BF16)
                va = S.sb("va", [128, NT, 2, 65], BF16)
                vtmp = S.sb("vtmp", [128, NT * 128], BF16)
                aT = S.sb("aT", [128, 4, T], BF16)
                m32 = S.sb("m32", [128, 2, 128], F32)
                mk = S.sb("mk", [128, 2, 4, 128], BF16)
                sk = S.sb("sk", [128, 8], F32)
                esk = S.sb("esk", [128, 8], F32)
                pt = [S.sb("pt%d" % i, [128, 5, 512], BF16) for i in range(2)]
                den = S.sb("den", [128, 4], F32)
                att = [S.sb("att%d" % i, [128, 512], BF16) for i in range(2)]
                pST = [S.ps("pST%d" % i, [128, 512]) for i in range(3)]
                pPV = [S.ps("pPV%d" % i, [128, 4, 128]) for i in range(2)]
                pAT = S.ps("pAT", [128, 4, 128], BF16)
                for c in range(4):
                    P.dma("sp", qT[:, c, :], projT[c * 128:(c + 1) * 128, :], r=["projT"], w=["qT"])
                for c in range(4):
                    P.dma("sp", kT[:, c, :], projT[(CK + c) * 128:(CK + 1 + c) * 128, :], r=["projT"], w=["kT"])
                P.dma("sp", vtmp[:], projM[0, :, :], r=["projM"], w=["vtmp"])
                P.dve(lambda e: e.tensor_copy(out=va[:, :, :, 0:64],
                                              in_=vtmp[:].rearrange("p (t g d) -> p t g d", g=2, d=64)), ["vtmp"], ["va"])
                P.pool(lambda e: e.memset(va[:, :, :, 64:65], 1.0), [], ["va"])
                P.dma("sp", m32[:, 0, :], mprev_in[:, :], w=["m32"])
                P.dma("sp", m32[:, 1, :], mnext_in[:, :], w=["m32"])
                for i in range(2):
                    for hh in range(4):
                        P.dve(lambda e, i=i, hh=hh: e.tensor_copy(out=mk[:, i, hh, :], in_=m32[:, i, :]), ["m32"], ["mk"])
                P.dma("sp", sk[:], sink_in[l, :].partition_broadcast(128), w=["sk"])
                P.act(lambda e: e.activation(out=esk[:], in_=sk[:], func=AF.Exp), ["sk"], ["esk"])
                ist = 0
                ipv = 0
                iat = 0
                qblocks = list(range(NT)) if not last else list(range(32))
                if KD < 1:
                    qblocks = []
                for n in qblocks:
                    sa = iat % 2
                    iat += 1
                    ak = "att%d" % sa
                    for g in range(2):
                        if n < 32:
                            kbs = [(32, None), (33, None)]
                            if n > 0:
                                kbs.append((n - 1, 0))
                            kbs.append((n, None))
                            if n < 31:
                                kbs.append((n + 1, 1))
                        else:
                            kbs = [(32, None), (33, None)]
                        spt = ipv % 2
                        ptk = "pt%d" % spt
                        pvk = "pPV%d" % spt
                        ipv += 1
                        for bi, (kb, mi) in enumerate(kbs):
                            s_ = ist % 3
                            ist += 1
                            for hh in range(4):
                                h = 4 * g + hh
                                p0 = (h % 2) * 64
                                P.pe(lambda e, s_=s_, hh=hh, h=h, kb=kb, n=n, g=g: e.matmul(
                                    pST[s_][:, hh * 128:(hh + 1) * 128],
                                    lhsT=kT[:, 2 * g + (h % 2), kb * 128:(kb + 1) * 128],
                                    rhs=qT[:, h // 2, n * 128:(n + 1) * 128], start=True, stop=True),
                                    ["qT", "kT"], ["pST%d" % s_])
                            P.act(lambda e, s_=s_, spt=spt, bi=bi: e.activation(out=pt[spt][:, bi, :], in_=pST[s_][:],
                                                                               func=AF.Exp, scale=0.125),
                                  ["pST%d" % s_], [ptk + "_%d" % bi])
                            if mi is not None:
                                P.pool(lambda e, spt=spt, bi=bi, mi=mi: e.tensor_tensor(
                                    out=pt[spt][:, bi, :], in0=pt[spt][:, bi, :],
                                    in1=mk[:, mi, :, :].rearrange("p a b -> p (a b)"), op=ALU.mult),
                                    [ptk + "_%d" % bi, "mk"], [ptk + "_%d" % bi])
                        for hh in range(4):
                            for bi, (kb, mi) in enumerate(kbs):
                                P.pe(lambda e, spt=spt, bi=bi, hh=hh, kb=kb, g=g, nb=len(kbs): e.matmul(
                                    pPV[spt][:, hh, 0:65], lhsT=pt[spt][:, bi, hh * 128:(hh + 1) * 128],
                                    rhs=va[:, kb, g, :], start=(bi == 0), stop=(bi == nb - 1)),
                                    [ptk + "_%d" % bi, "va"], [pvk])
                        if KD < 3:
                            continue
                        P.dve(lambda e, spt=spt, g=g: e.tensor_tensor(out=den[:], in0=pPV[spt][:, :, 64],
                                                                      in1=esk[:, 4 * g:4 * g + 4], op=ALU.add),
                              [pvk, "esk"], ["den"])
                        P.dve(lambda e: e.reciprocal(out=den[:], in_=den[:]), ["den"], ["den"])
                        for hh in range(4):
                            h = 4 * g + hh
                            P.dve(lambda e, spt=spt, hh=hh, h=h, sa=sa: e.tensor_scalar(
                                out=att[sa][:, h * 64:(h + 1) * 64], in0=pPV[spt][:, hh, 0:64],
                                scalar1=den[:, hh:hh + 1], scalar2=None, op0=ALU.mult), [pvk, "den"], [ak])
                    for c in (range(4) if KD >= 4 else []):
                        P.pe(lambda e, sa=sa, c=c: e.transpose(out=pAT[:, c, :], in_=att[sa][:, c * 128:(c + 1) * 128],
                                                               identity=idb[:]), [ak, "idb"], ["pAT"])
                    P.act(lambda e, n=n: e.copy(out=aT[:, :, n * 128:(n + 1) * 128], in_=pAT[:]), ["pAT"], ["aT"])
                nq = len(qblocks) * 128
                for c in (range(4) if nq else []):
                    P.dma("pool", attT[c * 128:(c + 1) * 128, 0:nq], aT[:, c, 0:nq], r=["aT"], w=["attT"])
                P.barrier()

            phase(l, 3)
            with Scope(nc) as S:
                cv = S.sb("cv", [128, 4, 34], F32)
                Y = S.sb("Y", [128, 4, 512], F32)
                zb = S.sb("zb", [128, 4, 15 + SEQ + 15], BF16)
                zcb = S.sb("zcb", [128, 4, 15 + CTX + 15], BF16)
                dg = S.sb("dg", [128, 4, 31, 128], BF16)
                vl = [S.sb("vl%d" % i, [128, T], BF16) for i in range(2)]
                gl = [S.sb("gl%d" % i, [128, T], BF16) for i in range(2)]
                sg = S.sb("sg", [128, T], F32)
                sq = [S.sb("sq%d" % i, [128, 512], F32) for i in range(2)]
                mean = S.sb("mean", [128, 512], F32)
                m2 = S.sb("m2", [128, 512], F32)
                rs = S.sb("rs", [128, 512], F32)
                tt = [S.sb("tt%d" % i, [128, 512], F32) for i in range(2)]
                co = [S.sb("co%d" % i, [128, 512], BF16) for i in range(2)]
                pY = [S.ps("pY%d" % i, [128, 512]) for i in range(2)]
                pS1 = S.ps("pS1", [128, 512])
                pS2 = S.ps("pS2", [128, 512])
                P.dma("sp", cv[:], cvec_in[l, :, :, :], w=["cv"])
                P.pool(lambda e: e.memset(zb[:], 0.0), [], ["zb"])
                P.pool(lambda e: e.memset(zcb[:], 0.0), [], ["zcb"])
                for c in range(4):
                    for j in range(31):
                        eng = P.dve if (j % 3) else P.pool
                        eng(lambda e, c=c, j=j: e.tensor_scalar(out=dg[:, c, j, :], in0=idb[:], scalar1=cv[:, c, j:j + 1],
                                                                scalar2=None, op0=ALU.mult), ["idb", "cv"], ["dg%d" % c])
                for c in range(4):
                    s_ = c % 2
                    P.dma("sp", vl[s_][:], projT[(CGLU + c) * 128:(CGLU + 1 + c) * 128, :], r=["projT"], w=["vl%d" % s_])
                    P.dma("sp", gl[s_][:], projT[(CGLU + 4 + c) * 128:(CGLU + 5 + c) * 128, :], r=["projT"],
                          w=["gl%d" % s_])
                    P.act(lambda e, s_=s_: e.activation(out=sg[:], in_=gl[s_][:], func=AF.Sigmoid),
                          ["gl%d" % s_], ["sg"])
                    P.dve(lambda e, s_=s_, c=c: e.tensor_tensor(out=zb[:, c, 15:15 + SEQ], in0=vl[s_][:, 0:SEQ],
                                                                in1=sg[:, 0:SEQ], op=ALU.mult),
                          ["vl%d" % s_, "sg", "zb"], ["zb%d" % c])
                    P.dve(lambda e, s_=s_, c=c: e.tensor_tensor(out=zcb[:, c, 15:15 + CTX], in0=vl[s_][:, SEQ:T],
                                                                in1=sg[:, SEQ:T], op=ALU.mult),
                          ["vl%d" % s_, "sg", "zcb"], ["zcb%d" % c])
                groups = GROUPS512 if not last else GROUPS512[:8]
                isq = 0
                ico = 0
                ipy = 0
                for (t0, n) in groups:
                    ykeys = ["Yc%d" % c for c in range(4)]
                    for c in range(4):
                        sp_ = ipy % 2
                        ipy += 1
                        for j in range(31):
                            if t0 < SEQ:
                                P.pe(lambda e, sp_=sp_, c=c, j=j, t0=t0, n=n: e.matmul(
                                    pY[sp_][:, 0:n], lhsT=dg[:, c, j, :], rhs=zb[:, c, t0 + j:t0 + j + n],
                                    start=(j == 0), stop=(j == 30)), ["dg%d" % c, "zb%d" % c], ["pY%d" % sp_])
                            else:
                                P.pe(lambda e, sp_=sp_, c=c, j=j, n=n: e.matmul(
                                    pY[sp_][:, 0:n], lhsT=dg[:, c, j, :], rhs=zcb[:, c, j:j + n],
                                    start=(j == 0), stop=(j == 30)), ["dg%d" % c, "zcb%d" % c], ["pY%d" % sp_])
                        P.act(lambda e, sp_=sp_, c=c, n=n: e.activation(out=Y[:, c, 0:n], in_=pY[sp_][:, 0:n],
                                                                        func=AF.Identity, bias=cv[:, c, 31:32], scale=1.0),
                              ["pY%d" % sp_, "cv"], [ykeys[c]])
                    for c in range(4):
                        P.pe(lambda e, c=c, n=n: e.matmul(pS1[:, 0:n], lhsT=ones32[:], rhs=Y[:, c, 0:n],
                                                          start=(c == 0), stop=(c == 3)),
                             ["ones32", ykeys[c]], ["pS1"])
                    for c in range(4):
                        s_ = isq % 2
                        isq += 1
                        P.act(lambda e, c=c, s_=s_, n=n: e.activation(out=sq[s_][:, 0:n], in_=Y[:, c, 0:n],
                                                                      func=AF.Square), [ykeys[c]], ["sq%d" % s_])
                        P.pe(lambda e, c=c, s_=s_, n=n: e.matmul(pS2[:, 0:n], lhsT=ones32[:], rhs=sq[s_][:, 0:n],
                                                                 start=(c == 0), stop=(c == 3)),
                             ["ones32", "sq%d" % s_], ["pS2"])
                    P.act(lambda e, n=n: e.activation(out=mean[:, 0:n], in_=pS1[:, 0:n], func=AF.Copy, scale=1.0 / 512),
                          ["pS1"], ["mean"])
                    P.dve(lambda e, n=n: e.tensor_tensor(out=m2[:, 0:n], in0=mean[:, 0:n], in1=mean[:, 0:n], op=ALU.mult),
                          ["mean"], ["m2"])
                    P.dve(lambda e, n=n: e.scalar_tensor_tensor(out=rs[:, 0:n], in0=pS2[:, 0:n], scalar=1.0 / 512,
                                                                in1=m2[:, 0:n], op0=ALU.mult, op1=ALU.subtract),
                          ["pS2", "m2"], ["rs"])
                    P.act(lambda e, n=n: e.activation(out=rs[:, 0:n], in_=rs[:, 0:n], func=AF.Sqrt, bias=EPS, scale=1.0),
                          ["rs"], ["rs"])
                    P.dve(lambda e, n=n: e.reciprocal(out=rs[:, 0:n], in_=rs[:, 0:n]), ["rs"], ["rs"])
                    for c in range(4):
                        s_ = ico % 2
                        ico += 1
                        P.dve(lambda e, c=c, s_=s_, n=n: e.tensor_tensor(out=tt[s_][:, 0:n], in0=Y[:, c, 0:n],
                                                                         in1=mean[:, 0:n], op=ALU.subtract),
                              [ykeys[c], "mean"], ["tt%d" % s_])
                        P.dve(lambda e, s_=s_, n=n: e.tensor_tensor(out=tt[s_][:, 0:n], in0=tt[s_][:, 0:n],
                                                                    in1=rs[:, 0:n], op=ALU.mult),
                              ["tt%d" % s_, "rs"], ["tt%d" % s_])
                        P.act(lambda e, c=c, s_=s_, n=n: e.activation(out=co[s_][:, 0:n], in_=tt[s_][:, 0:n], func=AF.Silu,
                                                                      scale=cv[:, c, 32:33], bias=cv[:, c, 33:34]),
                              ["tt%d" % s_, "cv"], ["co%d" % s_])
                        P.dma("pool", cbT[c * 128:(c + 1) * 128, t0:t0 + n], co[s_][:, 0:n], r=["co%d" % s_],
                              w=["cbT"])
                P.barrier()

            phase(l, 4)
            with Scope(nc) as S4:
                wkT = S4.sb("wkT", [128, NT, 8], F32)
                wkH = S4.sb("wkH", [128, NT, 2, 8], F32)
                thT = S4.sb("thT", [128, NT, 8], F32)
                decB = S4.sb("decB", [128, 8, NCH], F32)
                wmv = S4.sb("wmv", [128, 8, 3], F32)
                gmn = S4.sb("gmn", [128, 4], F32)
                bm32 = S4.sb("bm32", [128, 2, 128], F32)
                bmk = S4.sb("bmk", [128, 2, 128], BF16)
                P.dma("sp", wmv[:], wm_in[l, :, :, :], w=["wmv"])
                P.dma("sp", gmn[:], gmn_in[l, :, :], w=["gmn"])
                P.dma("sp", bm32[:, 0, :], bmf_in[:, :], w=["bm32"])
                P.dma("sp", bm32[:, 1, :], bmb_in[:, :], w=["bm32"])
                P.dve(lambda e: e.tensor_copy(out=bmk[:], in_=bm32[:]), ["bm32"], ["bmk"])
                with Scope(nc) as S:
                    gI = S.sb("gI", [64, T], F32)
                    gF = S.sb("gF", [64, T], F32)
                    csp = S.sb("csp", [64, T], F32)
                    nb = S.sb("nb", [64, T], F32)
                    msk = S.sb("msk", [64, T], F32)
                    bmt = S.sb("bmt", [64, 2], F32)
                    nbf = S.sb("nbf", [64, 1], F32)
                    tot = S.sb("tot", [64, NCH], F32)
                    umax = S.sb("umax", [64, NCH], F32)
                    R = S.sb("R", [64, NCH], F32)
                    mc = S.sb("mc", [64, NCH + 1], F32)
                    dec = S.sb("dec", [64, NCH], F32)
                    sel = S.sb("sel", [64, 8, 128], F32)
                    pG = [S.ps("pG%d" % i, [128, 4, 2, 64]) for i in range(2)]
                    pD = [S.ps("pD%d" % i, [128, 4, NCH]) for i in range(2)]
                    v3 = lambda a: a[:].rearrange("p (c l) -> p c l", l=64)
                    P.dma("sp", gI[:], gatesT[0:64, :], r=["gatesT"], w=["gI"])
                    P.dma("sp", gF[:], gatesT[64:128, :], r=["gatesT"], w=["gF"])
                    P.dma("sp", bmt[:], bm_in[l, :, :], w=["bmt"])
                    P.dma("sp", sel[:], sel_in[:, :, :], w=["sel"])
                    P.dve(lambda e: e.tensor_scalar(out=nbf[:], in0=bmt[:, 1:2], scalar1=-1.0, scalar2=None, op0=ALU.mult),
                          ["bmt"], ["nbf"])
                    P.pool(lambda e: e.memset(msk[:], 1.0), [], ["msk"])
                    P.pool(lambda e: e.memset(v3(msk)[:, :, 0:1], 0.0), [], ["msk"])
                    P.act(lambda e: e.activation(out=gI[:], in_=gI[:], func=AF.Identity, bias=bmt[:, 0:1], scale=1.0),
                          ["gI", "bmt"], ["gI"])
                    P.act(lambda e: e.activation(out=gF[:], in_=gF[:], func=AF.Exp, bias=nbf[:], scale=-1.0),
                          ["gF", "nbf"], ["gF"])
                    P.act(lambda e: e.activation(out=gF[:], in_=gF[:], func=AF.Ln, bias=1.0, scale=1.0), ["gF"], ["gF"])
                    P.dve(lambda e: e.tensor_tensor_scan(out=csp[:], data0=msk[:], data1=gF[:], initial=0.0,
                                                         op0=ALU.mult, op1=ALU.add), ["msk", "gF"], ["csp"])
                    P.dve(lambda e: e.tensor_copy(out=tot[:], in_=v3(csp)[:, :, 63]), ["csp"], ["tot"])
                    P.dve(lambda e: e.tensor_copy(out=nb[0:32, :], in_=csp[0:32, :]), ["csp"], ["nb"])
                    P.dve(lambda e: e.tensor_tensor(out=nb[32:64, :], in0=gF[32:64, :], in1=csp[32:64, :],
                                                    op=ALU.subtract), ["gF", "csp"], ["nb"])
                    P.dve(lambda e: e.tensor_tensor(out=v3(nb)[32:64], in0=v3(nb)[32:64],
                                                    in1=tot[32:64, :].unsqueeze(2).broadcast_to([32, NCH, 64]),
                                                    op=ALU.add), ["nb", "tot"], ["nb"])
                    P.dve(lambda e: e.tensor_tensor(out=gI[:], in0=gI[:], in1=nb[:], op=ALU.add), ["gI", "nb"], ["gI"])
                    P.dve(lambda e: e.tensor_reduce(out=umax[:], in_=v3(gI), axis=AX.X, op=ALU.max), ["gI"], ["umax"])
                    P.dve(lambda e: e.memset(mc[:], -1e30), [], ["mc"])
                    for (r0, order) in ((0, ORDER_F), (32, ORDER_B)):
                        for j, c in enumerate(order):
                            P.dve(lambda e, r0=r0, c=c: e.tensor_tensor(out=R[r0:r0 + 32, c:c + 1],
                                                                        in0=mc[r0:r0 + 32, c:c + 1],
                                                                        in1=umax[r0:r0 + 32, c:c + 1], op=ALU.max),
                                  ["mc", "umax"], ["R"])
                            if j + 1 < len(order):
                                c2 = order[j + 1]
                                P.dve(lambda e, r0=r0, c=c, c2=c2: e.tensor_tensor(
                                    out=mc[r0:r0 + 32, c2:c2 + 1], in0=R[r0:r0 + 32, c:c + 1],
                                    in1=tot[r0:r0 + 32, c:c + 1], op=ALU.subtract), ["R", "tot"], ["mc"])
                    Rb = lambda: R[:, :].unsqueeze(2).broadcast_to([64, NCH, 64])
                    P.dve(lambda e: e.tensor_tensor(out=v3(gI), in0=v3(gI), in1=Rb(), op=ALU.subtract), ["gI", "R"], ["gI"])
                    P.dve(lambda e: e.tensor_tensor(out=v3(nb), in0=v3(nb), in1=Rb(), op=ALU.subtract), ["nb", "R"], ["nb"])
                    P.dve(lambda e: e.tensor_tensor(out=dec[:], in0=mc[:, 0:NCH], in1=R[:], op=ALU.subtract),
                          ["mc", "R"], ["dec"])
                    P.dve(lambda e: e.tensor_scalar(out=gI[:], in0=gI[:], scalar1=LN_DK, scalar2=None, op0=ALU.add),
                          ["gI"], ["gI"])
                    P.act(lambda e: e.activation(out=gI[:], in_=gI[:], func=AF.Exp), ["gI"], ["gI"])
                    P.act(lambda e: e.activation(out=nb[:], in_=nb[:], func=AF.Exp), ["nb"], ["nb"])
                    P.act(lambda e: e.activation(out=dec[:], in_=dec[:], func=AF.Exp), ["dec"], ["dec"])
                    for t4 in range(0, NT, 4):
                        s_ = (t4 // 4) % 2
                        nt4 = min(4, NT - t4)
                        for j in range(nt4):
                            t = t4 + j
                            P.pe(lambda e, s_=s_, j=j, t=t: e.transpose(out=pG[s_][:, j, 0, :],
                                                                        in_=gI[:, t * 128:(t + 1) * 128],
                                                                        identity=id32[0:64, 0:64]),
                                 ["gI", "id32"], ["pG%d" % s_])
                            P.pe(lambda e, s_=s_, j=j, t=t: e.transpose(out=pG[s_][:, j, 1, :],
                                                                        in_=nb[:, t * 128:(t + 1) * 128],
                                                                        identity=id32[0:64, 0:64]),
                                 ["nb", "id32"], ["pG%d" % s_])
                        for d in range(2):
                            P.dve(lambda e, s_=s_, t4=t4, nt4=nt4, d=d: e.tensor_copy(
                                out=wkT[:, t4:t4 + nt4, d * 4:d * 4 + 4], in_=pG[s_][:, 0:nt4, 0, d * 32:d * 32 + 4]),
                                ["pG%d" % s_], ["wkT"])
                            P.dve(lambda e, s_=s_, t4=t4, nt4=nt4, d=d: e.tensor_copy(
                                out=thT[:, t4:t4 + nt4, d * 4:d * 4 + 4], in_=pG[s_][:, 0:nt4, 1, d * 32:d * 32 + 4]),
                                ["pG%d" % s_], ["thT"])
                    P.pool(lambda e: e.memset(wkH[:], 0.0), [], ["wkH"])
                    P.dve(lambda e: e.tensor_copy(out=wkH[0:64, :, 0, :], in_=wkT[0:64, :, :]), ["wkT", "wkH"], ["wkH"])
                    P.dve(lambda e: e.tensor_copy(out=wkH[64:128, :, 1, :], in_=wkT[64:128, :, :]), ["wkT", "wkH"], ["wkH"])
                    for d in range(2):
                        for hh in range(4):
                            P.pe(lambda e, d=d, hh=hh: e.matmul(pD[d][:, hh, :], lhsT=sel[:, d * 4 + hh, :], rhs=dec[:],
                                                                start=True, stop=True), ["sel", "dec"], ["pD%d" % d])
                        P.dve(lambda e, d=d: e.tensor_copy(out=decB[:, d * 4:d * 4 + 4, :], in_=pD[d][:]),
                              ["pD%d" % d], ["decB"])
                    P.barrier()

                for h in range(4):
                    with Scope(nc) as S:
                        qr = S.sb("qr", [128, T], BF16)
                        kr = S.sb("kr", [128, T], BF16)
                        acc = S.sb("acc", [128, T], F32)
                        qTm = S.sb("qTm", [128, T], BF16)
                        kTm = S.sb("kTm", [128, T], BF16)
                        qTp = S.sb("qTp", [128, NT, 2, 128], BF16)
                        ktm = S.sb("ktm", [128, NT, 128], BF16)
                        vau = S.sb("vau", [128, NT, 129], BF16)
                        otm = S.sb("otm", [128, NT, 128], BF16)
                        hraw = S.sb("hraw", [128, NT, 2, 129], F32)
                        dnv = S.sb("dnv", [128, 2, NT], F32)
                        C32 = [[S.sb("C32_%d_%d" % (i, p_), [128, 129], F32) for p_ in range(2)] for i in range(2)]
                        Cbf = [[S.sb("Cbf_%d_%d" % (i, p_), [128, 129], BF16) for p_ in range(2)] for i in range(2)]
                        ptl = [[S.sb("ptl%d_%d" % (d, i), [128, 128], BF16) for i in range(2)] for d in range(2)]
                        kp = [[S.sb("kp%d_%d" % (d, i), [128, 128], BF16) for i in range(2)] for d in range(2)]
                        dn = [S.sb("dn%d" % d, [128, 1], F32) for d in range(2)]
                        sgm = S.sb("sgm", [128, 128], F32)
                        junk = S.sb("junk", [128, 128], F32)
                        s1 = S.sb("s1", [128, NT], F32)
                        s2 = S.sb("s2", [128, NT], F32)
                        mu = S.sb("mu", [128, NT], F32)
                        rsd = S.sb("rsd", [128, NT], F32)
                        hn = [S.sb("hn%d" % i, [128, 128], BF16) for i in range(2)]
                        mho = S.sb("mho", [128, T], BF16)
                        pKT = S.ps("pKT", [128, 8, 128], BF16)
                        pSm = [S.ps("pSm%d" % d, [128, 128]) for d in range(2)]
                        pIN = [S.ps("pIN%d" % d, [128, 129]) for d in range(2)]
                        pKV = [S.ps("pKV%d" % d, [128, 129]) for d in range(2)]
                        pHT = S.ps("pHT", [128, 8, 128], BF16)

                        P.dma("sp", qr[:], projT[(CQKM + h) * 128:(CQKM + 1 + h) * 128, :], r=["projT"], w=["qr"])
                        P.dma("sp", kr[:], projT[(CQKM + 4 + h) * 128:(CQKM + 5 + h) * 128, :], r=["projT"], w=["kr"])
                        P.dma("sp", mho[:], projM[1 + h, :, :], r=["projM"], w=["mho"])
                        P.dve(lambda e: e.tensor_copy(out=vau[:, :, 0:128], in_=mho[:].rearrange("p (t d) -> p t d", d=128)),
                              ["mho"], ["vau"])
                        P.pool(lambda e: e.memset(vau[:, :, 128:129], 1.0), [], ["vau"])
                        P.dma("sp", otm[:].rearrange("p t d -> p (t d)"), projM[5 + h, :, :], r=["projM"], w=["otm"])
                        P.pool(lambda e: e.memset(qTp[:], 0.0), [], ["qTp"])
                        hsum = acc[:].rearrange("p (t d) -> p t d", d=128)
                        for d in range(2):
                            P.pool(lambda e, d=d: e.memset(C32[d][0][:], 0.0), [], ["C32_%d_0" % d])
                            P.pool(lambda e, d=d: e.memset(Cbf[d][0][:], 0.0), [], ["Cbf_%d_0" % d])
                        for (src, sk_, dst, dk_, wc) in ((qr, "qr", qTm, "qTm", h), (kr, "kr", kTm, "kTm", 4 + h)):
                            P.dve(lambda e, src=src, wc=wc: e.tensor_scalar(out=acc[:], in0=src[:], scalar1=wmv[:, wc, 1:2],
                                                                            scalar2=None, op0=ALU.mult),
                                  [sk_, "wmv"], ["acc"])
                            for (a, b) in ((0, SEQ), (SEQ, T)):
                                P.dve(lambda e, src=src, wc=wc, a=a, b=b: e.scalar_tensor_tensor(
                                    out=acc[:, a + 1:b], in0=src[:, a:b - 1], scalar=wmv[:, wc, 0:1], in1=acc[:, a + 1:b],
                                    op0=ALU.mult, op1=ALU.add), [sk_, "wmv", "acc"], ["acc"])
                                P.dve(lambda e, src=src, wc=wc, a=a, b=b: e.scalar_tensor_tensor(
                                    out=acc[:, a:b - 1], in0=src[:, a + 1:b], scalar=wmv[:, wc, 2:3], in1=acc[:, a:b - 1],
                                    op0=ALU.mult, op1=ALU.add), [sk_, "wmv", "acc"], ["acc"])
                            P.act(lambda e, dst=dst: e.activation(out=dst[:], in_=acc[:], func=AF.Silu), ["acc"], [dk_])
                        q4 = qTm[:].rearrange("p (t two l) -> p t two l", two=2, l=64)
                        P.dve(lambda e: e.tensor_copy(out=qTp[:, :, 0, 0:64], in_=q4[:, :, 0, :]), ["qTm"], ["qTp"])
                        P.dve(lambda e: e.tensor_copy(out=qTp[:, :, 1, 64:128], in_=q4[:, :, 1, :]), ["qTm"], ["qTp"])
                        for t8 in range(0, NT, 8):
                            n8 = min(8, NT - t8)
                            for j in range(n8):
                                P.pe(lambda e, j=j, t=t8 + j: e.transpose(out=pKT[:, j, :], in_=kTm[:, t * 128:(t + 1) * 128],
                                                                          identity=idb[:]), ["kTm", "idb"], ["pKT"])
                            P.act(lambda e, t8=t8, n8=n8: e.copy(out=ktm[:, t8:t8 + n8, :], in_=pKT[:, 0:n8, :]),
                                  ["pKT"], ["ktm"])
                        done_tiles = [set(), set()]
                        slot = [0, 0]
                        ptl_of = [{}, {}]
                        for j in range(NCH):
                            for d, order in ((0, ORDER_F), (1, ORDER_B)):
                                c = order[j]
                                t = c // 2
                                r0 = (c % 2) * 64
                                col = d * 4 + h
                                ks = j % 2
                                cur, nxt = j % 2, (j + 1) % 2
                                P.act(lambda e, d=d, t=t, r0=r0, ks=ks, col=col: e.activation(
                                    out=kp[d][ks][:, :], in_=ktm[:, t, :], func=AF.Copy,
                                    scale=wkH[:, t, r0 // 64, col:col + 1]), ["ktm", "wkH"], ["kp%d_%d" % (d, ks)])
                                P.pe(lambda e, d=d, t=t, ks=ks: e.matmul(pKV[d][:], lhsT=kp[d][ks][:, :],
                                                                         rhs=vau[:, t, :], start=True, stop=True),
                                     ["kp%d_%d" % (d, ks), "vau"], ["pKV%d" % d])
                                P.dve(lambda e, d=d, c=c, col=col, cur=cur, nxt=nxt: e.scalar_tensor_tensor(
                                    out=C32[d][nxt][:], in0=C32[d][cur][:], scalar=decB[:, col, c:c + 1], in1=pKV[d][:],
                                    op0=ALU.mult, op1=ALU.add),
                                    ["C32_%d_%d" % (d, cur), "decB", "pKV%d" % d], ["C32_%d_%d" % (d, nxt)])
                                if j + 1 < NCH:
                                    c2 = order[j + 1]
                                    P.act(lambda e, d=d, c2=c2, col=col, nxt=nxt: e.activation(
                                        out=Cbf[d][nxt][:], in_=C32[d][nxt][:], func=AF.Copy,
                                        scale=decB[:, col, c2:c2 + 1]),
                                        ["C32_%d_%d" % (d, nxt), "decB"], ["Cbf_%d_%d" % (d, nxt)])
                                if t not in done_tiles[d]:
                                    done_tiles[d].add(t)
                                    sl = slot[d] % 2
                                    slot[d] += 1
                                    ptl_of[d][t] = sl
                                    P.pe(lambda e, d=d, t=t: e.matmul(pSm[d][:], lhsT=kTm[:, t * 128:(t + 1) * 128],
                                                                      rhs=qTm[:, t * 128:(t + 1) * 128], start=True,
                                                                      stop=True), ["kTm", "qTm"], ["pSm%d" % d])
                                    P.dve(lambda e, d=d, t=t, sl=sl, col=col: e.scalar_tensor_tensor(
                                        out=ptl[d][sl][:], in0=pSm[d][:], scalar=wkT[:, t, col:col + 1], in1=bmk[:, d, :],
                                        op0=ALU.mult, op1=ALU.mult), ["pSm%d" % d, "wkT", "bmk"], ["ptl%d_%d" % (d, sl)])
                                sl = ptl_of[d][t]
                                P.pe(lambda e, d=d, t=t, r0=r0, cur=cur: e.matmul(pIN[d][:], lhsT=qTp[:, t, r0 // 64, :],
                                                                                  rhs=Cbf[d][cur][:], start=True, stop=False),
                                     ["qTp", "Cbf_%d_%d" % (d, cur)], ["pIN%d" % d])
                                P.pe(lambda e, d=d, t=t, sl=sl: e.matmul(pIN[d][:], lhsT=ptl[d][sl][:, :],
                                                                         rhs=vau[:, t, :], start=False, stop=True),
                                     ["ptl%d_%d" % (d, sl), "vau"], ["pIN%d" % d])
                                if d == 0:
                                    P.act(lambda e, d=d, t=t, r0=r0: e.copy(out=hraw[r0:r0 + 64, t, d, :],
                                                                            in_=pIN[d][r0:r0 + 64, :]),
                                          ["pIN%d" % d], ["hraw%d" % d])
                                else:
                                    P.dve(lambda e, d=d, t=t, r0=r0: e.tensor_copy(out=hraw[r0:r0 + 64, t, d, :],
                                                                                   in_=pIN[d][r0:r0 + 64, :]),
                                          ["pIN%d" % d], ["hraw%d" % d])
                        for d in range(2):
                            col = d * 4 + h
                            P.act(lambda e, d=d: e.activation(out=dnv[:, d, :], in_=hraw[:, :, d, 128], func=AF.Abs),
                                  ["hraw%d" % d], ["dnv%d" % d])
                            P.dve(lambda e, d=d, col=col: e.tensor_tensor(out=dnv[:, d, :], in0=dnv[:, d, :],
                                                                          in1=thT[:, :, col], op=ALU.max),
                                  ["dnv%d" % d, "thT"], ["dnv%d" % d])
                            P.dve(lambda e, d=d: e.reciprocal(out=dnv[:, d, :], in_=dnv[:, d, :]), ["dnv%d" % d], ["dnv%d" % d])
                        P.dve(lambda e: e.tensor_tensor(out=hsum[:], in0=hraw[:, :, 0, 0:128],
                                                        in1=dnv[:, 0, :].unsqueeze(2).broadcast_to([128, NT, 128]),
                                                        op=ALU.mult), ["hraw0", "dnv0"], ["hsum", "acc"])
                        P.dve(lambda e: e.tensor_tensor(out=hraw[:, :, 1, 0:128], in0=hraw[:, :, 1, 0:128],
                                                        in1=dnv[:, 1, :].unsqueeze(2).broadcast_to([128, NT, 128]),
                                                        op=ALU.mult), ["hraw1", "dnv1"], ["hraw1"])
                        P.pool(lambda e: e.tensor_tensor(out=hsum[:], in0=hsum[:], in1=hraw[:, :, 1, 0:128], op=ALU.add),
                               ["hsum", "hraw1"], ["hsum"])
                        ntr = NT if not last else 32
                        P.dve(lambda e: e.memset(s1[:], 0.0), [], ["s1"])
                        P.dve(lambda e: e.memset(s2[:], 0.0), [], ["s2"])
                        for t in range(ntr):
                            P.act(lambda e, t=t: e.activation(out=sgm[:], in_=otm[:, t, :], func=AF.Sigmoid),
                                  ["otm"], ["sgm"])
                            P.dve(lambda e, t=t: e.tensor_tensor(out=hsum[:, t, :], in0=hsum[:, t, :], in1=sgm[:],
                                                                 op=ALU.mult), ["sgm", "hsum", "acc"], ["hg_%d" % t])
                            P.act(lambda e, t=t: e.activation(out=junk[:], in_=hsum[:, t, :], func=AF.Copy,
                                                              accum_out=s1[:, t:t + 1]), ["hg_%d" % t, "s1"],
                                  ["junk", "s1"])
                            P.act(lambda e, t=t: e.activation(out=junk[:], in_=hsum[:, t, :], func=AF.Square,
                                                              accum_out=s2[:, t:t + 1]), ["hg_%d" % t, "s2"],
                                  ["junk", "s2"])
                        P.dve(lambda e: e.tensor_scalar(out=mu[:], in0=s1[:], scalar1=1.0 / 128, scalar2=None, op0=ALU.mult),
                              ["s1"], ["mu"])
                        P.dve(lambda e: e.tensor_tensor(out=rsd[:], in0=mu[:], in1=mu[:], op=ALU.mult), ["mu"], ["rsd"])
                        P.dve(lambda e: e.scalar_tensor_tensor(out=rsd[:], in0=s2[:], scalar=1.0 / 128, in1=rsd[:],
                                                               op0=ALU.mult, op1=ALU.subtract), ["s2", "rsd"], ["rsd"])
                        P.act(lambda e: e.activation(out=rsd[:], in_=rsd[:], func=AF.Sqrt, bias=EPS, scale=1.0),
                              ["rsd"], ["rsd"])
                        P.dve(lambda e: e.reciprocal(out=rsd[:], in_=rsd[:]), ["rsd"], ["rsd"])
                        for t8 in range(0, ntr, 8):
                            n8 = min(8, ntr - t8)
                            for j in range(n8):
                                t = t8 + j
                                s_ = t % 2
                                P.dve(lambda e, t=t, s_=s_: e.tensor_scalar(out=hn[s_][:], in0=hsum[:, t, :],
                                                                            scalar1=mu[:, t:t + 1], scalar2=rsd[:, t:t + 1],
                                                                            op0=ALU.subtract, op1=ALU.mult),
                                      ["hg_%d" % t, "mu", "rsd"], ["hn%d" % s_])
                                P.pe(lambda e, j=j, s_=s_: e.transpose(out=pHT[:, j, :], in_=hn[s_][:], identity=idb[:]),
                                     ["hn%d" % s_, "idb"], ["pHT"])
                            P.act(lambda e, t8=t8, n8=n8: e.activation(
                                out=mho[:, t8 * 128:(t8 + n8) * 128].rearrange("p (a b) -> p a b", b=128),
                                in_=pHT[:, 0:n8, :], func=AF.Copy, scale=gmn[:, h:h + 1]), ["pHT", "gmn"], ["mho"])
                        P.dma("pool", mhT[h * 128:(h + 1) * 128, 0:ntr * 128], mho[:, 0:ntr * 128], r=["mho"], w=["mhT"])
                        P.barrier()

            phase(l, 5)
            with Scope(nc) as S:
                wA = S.sb("wA", [128, 4, D], BF16)
                wB = S.sb("wB", [128, 4, D], BF16)
                wC = S.sb("wC", [128, 4, D], BF16)
                wO = S.sb("wO", [128, 8, D], BF16)
                gt = S.sb("gt", [128, D], F32)
                aG = [S.sb("aG%d" % i, [128, 4, 512], BF16) for i in range(2)]
                bG = [S.sb("bG%d" % i, [128, 4, 512], BF16) for i in range(2)]
                cG = [S.sb("cG%d" % i, [128, 4, 512], BF16) for i in range(2)]
                br = [S.sb("br%d" % i, [128, 24, 512], BF16) for i in range(2)]
                m1 = [S.sb("m1_%d" % i, [128, 512], F32) for i in range(2)]
                m2_ = [S.sb("m2_%d" % i, [128, 512], F32) for i in range(2)]
                m3 = [S.sb("m3_%d" % i, [128, 512], F32) for i in range(2)]
                mg = S.sb("mg", [128, 8, 512], BF16)
                xt = [S.sb("xt%d" % i, [128, D], F32) for i in range(2)]
                ty = [S.sb("ty%d" % i, [128, 512], F32) for i in range(2)]
                pA = [S.ps("pA%d" % i, [128, 512]) for i in range(2)]
                pB = [S.ps("pB%d" % i, [128, 512]) for i in range(2)]
                pC = [S.ps("pC%d" % i, [128, 512]) for i in range(2)]
                pY = [S.ps("pY%d" % i, [128, 512]) for i in range(2)]
                P.dma("pool", wA[:], w_ao[l, :, :].rearrange("(c p) n -> p c n", p=128), w=["wA"])
                P.dma("pool", wB[:], w_pw[l, :, :].rearrange("(c p) n -> p c n", p=128), w=["wB"])
                P.dma("pool", wC[:], w_mo[l, :, :].rearrange("(c p) n -> p c n", p=128), w=["wC"])
                P.dma("pool", wO[:], w_out[l, :, :].rearrange("(c p) n -> p c n", p=128), w=["wO"])
                groups = GROUPS512 if not last else GROUPS512[:8]
                cur_stream = None
                ix = 0
                iy = 0
                for gi, (t0, n) in enumerate(groups):
                    stream = 0 if t0 < SEQ else 1
                    if stream != cur_stream:
                        cur_stream = stream
                        load_mod("sp", gt, l, stream, 2, "gt")
                    s_ = gi % 2
                    P.dma("sp", aG[s_][:, :, 0:n], attT[:, t0:t0 + n].rearrange("(c p) n -> p c n", p=128),
                          r=["attT"], w=["aG%d" % s_])
                    P.dma("sp", bG[s_][:, :, 0:n], cbT[:, t0:t0 + n].rearrange("(c p) n -> p c n", p=128),
                          r=["cbT"], w=["bG%d" % s_])
                    P.dma("sp", cG[s_][:, :, 0:n], mhT[:, t0:t0 + n].rearrange("(c p) n -> p c n", p=128),
                          r=["mhT"], w=["cG%d" % s_])
                    P.dma("sp", br[s_][:, :, 0:n],
                          projT[CBR * 128:CGATE * 128, t0:t0 + n].rearrange("(c p) n -> p c n", p=128),
                          r=["projT"], w=["br%d" % s_])
                    for j in range(8):
                        u = j % 2
                        for (pp, pn, ww, wn, src, sn) in ((pA, "pA", wA, "wA", aG, "aG"), (pB, "pB", wB, "wB", bG, "bG"),
                                                          (pC, "pC", wC, "wC", cG, "cG")):
                            for k in range(4):
                                P.pe(lambda e, pp=pp, ww=ww, src=src, u=u, k=k, j=j, s_=s_, n=n: e.matmul(
                                    pp[u][:, 0:n], lhsT=ww[:, k, j * 128:(j + 1) * 128], rhs=src[s_][:, k, 0:n],
                                    start=(k == 0), stop=(k == 3)), [wn, "%s%d" % (sn, s_)], ["%s%d" % (pn, u)])
                        brk = "br%d" % s_
                        P.dve(lambda e, u=u, s_=s_, j=j, n=n: e.tensor_tensor(out=m1[u][:, 0:n], in0=pA[u][:, 0:n],
                                                                              in1=br[s_][:, j, 0:n], op=ALU.mult),
                              ["pA%d" % u, brk], ["m1_%d" % u])
                        P.dve(lambda e, u=u, s_=s_, j=j, n=n: e.tensor_tensor(out=m2_[u][:, 0:n], in0=pB[u][:, 0:n],
                                                                              in1=br[s_][:, 8 + j, 0:n], op=ALU.mult),
                              ["pB%d" % u, brk], ["m2_%d" % u])
                        P.dve(lambda e, u=u, s_=s_, j=j, n=n: e.tensor_tensor(out=m3[u][:, 0:n], in0=pC[u][:, 0:n],
                                                                              in1=br[s_][:, 16 + j, 0:n], op=ALU.mult),
                              ["pC%d" % u, brk], ["m3_%d" % u])
                        P.pool(lambda e, u=u, n=n: e.tensor_tensor(out=m1[u][:, 0:n], in0=m1[u][:, 0:n], in1=m2_[u][:, 0:n],
                                                                   op=ALU.add), ["m1_%d" % u, "m2_%d" % u], ["m1_%d" % u])
                        P.pool(lambda e, u=u, j=j, n=n: e.tensor_tensor(out=mg[:, j, 0:n], in0=m1[u][:, 0:n],
                                                                        in1=m3[u][:, 0:n], op=ALU.add),
                               ["m1_%d" % u, "m3_%d" % u], ["mg%d" % j])
                    mgk = ["mg%d" % j for j in range(8)]
                    for jt in range(n // 128):
                        sx = ix % 2
                        ix += 1
                        xk = "xt%d" % sx
                        P.dma("sp", xt[sx][:], xs[t0 + jt * 128:t0 + (jt + 1) * 128, :], r=["xs"], w=[xk])
                        for cg in range(2):
                            sy = iy % 2
                            iy += 1
                            for k in range(8):
                                P.pe(lambda e, sy=sy, k=k, jt=jt, cg=cg: e.matmul(
                                    pY[sy][:], lhsT=mg[:, k, jt * 128:(jt + 1) * 128], rhs=wO[:, k, cg * 512:(cg + 1) * 512],
                                    start=(k == 0), stop=(k == 7)), [mgk[k], "wO"], ["pY%d" % sy])
                            P.dve(lambda e, sy=sy, cg=cg: e.tensor_tensor(out=ty[sy][:], in0=pY[sy][:],
                                                                          in1=gt[:, cg * 512:(cg + 1) * 512], op=ALU.mult),
                                  ["pY%d" % sy, "gt"], ["ty%d" % sy])
                            P.pool(lambda e, sy=sy, sx=sx, cg=cg: e.tensor_tensor(
                                out=xt[sx][:, cg * 512:(cg + 1) * 512], in0=xt[sx][:, cg * 512:(cg + 1) * 512],
                                in1=ty[sy][:], op=ALU.add), ["ty%d" % sy, xk], [xk])
                        P.dma("pool", xs[t0 + jt * 128:t0 + (jt + 1) * 128, :], xt[sx][:], r=[xk], w=["xs"])
                P.barrier()

            phase(l, 6)
            with Scope(nc) as S:
                wG = S.sb("wG", [128, 8, DFF], BF16)
                wU = S.sb("wU", [128, 8, DFF], BF16)
                wD = S.sb("wD", [128, 22, D], BF16)
                Gb = S.sb("Gb", [128, D], F32)
                Sb = S.sb("Sb", [128, D], F32)
                gt = S.sb("gt", [128, D], F32)
                gfin = S.sb("gfin", [128, D], F32)
                xg = S.sb("xg", [128, 2, D], F32)
                tmpf = S.sb("tmpf", [128, D], F32)
                hb = S.sb("hb", [128, D], BF16)
                hT = S.sb("hT", [128, 8, 256], BF16)
                ss = S.sb("ss", [128, 2], F32)
                rstd = S.sb("rstd", [128, 2], F32)
                aT = S.sb("aT", [128, 22, 256], BF16)
                sgl = [S.sb("sgl%d" % i, [128, 256], F32) for i in range(2)]
                ty = [S.sb("ty%d" % i, [128, 512], F32) for i in range(2)]
                pT = S.ps("pT", [128, 8, 128], BF16)
                pGa = [S.ps("pGa%d" % i, [128, 256]) for i in range(2)]
                pUa = [S.ps("pUa%d" % i, [128, 256]) for i in range(2)]
                pDn = [S.ps("pDn%d" % i, [128, 512]) for i in range(2)]
                for k in range(8):
                    P.dma("pool", wG[:, k, :], w_g[l, k * 128:(k + 1) * 128, :], w=["wG"])
                    P.dma("pool", wU[:, k, :], w_u[l, k * 128:(k + 1) * 128, :], w=["wU"])
                for k in range(0, 22, 2):
                    P.dma("pool", wD[:, k:k + 2, :], w_d[l, k * 128:(k + 2) * 128, :].rearrange("(c p) n -> p c n", p=128),
                          w=["wD"])
                groups = GROUPS256 if not last else GROUPS256[:16]
                cur_stream = None
                iu = 0
                iy = 0
                for gi, (t0, n) in enumerate(groups):
                    stream = 0 if t0 < SEQ else 1
                    if stream != cur_stream:
                        cur_stream = stream
                        load_mod("sp", Sb, l, stream, 3, "Sb")
                        load_mod("sp", tmpf, l, stream, 4, "tmpf")
                        P.dma("sp", Gb[:], g_ffn[l, :].partition_broadcast(128), w=["Gb"])
                        P.dve(lambda e: e.scalar_tensor_tensor(out=Gb[:], in0=tmpf[:], scalar=1.0, in1=Gb[:],
                                                               op0=ALU.add, op1=ALU.mult), ["tmpf", "Gb"], ["Gb"])
                        load_mod("sp", gt, l, stream, 5, "gt")
                    P.dve(lambda e: e.memset(ss[:], 0.0), [], ["ss"])
                    for j in range(2):
                        P.dma("sp", xg[:, j, :], xs[t0 + j * 128:t0 + (j + 1) * 128, :], r=["xs"], w=["xg%d" % j])
                        P.act(lambda e, j=j: e.activation(out=tmpf[:], in_=xg[:, j, :], func=AF.Square, scale=1.0 / 32,
                                                          accum_out=ss[:, j:j + 1]), ["xg%d" % j, "ss"], ["tmpf", "ss"])
                    P.act(lambda e: e.activation(out=rstd[:], in_=ss[:], func=AF.Sqrt, bias=EPS, scale=1.0),
                          ["ss"], ["rstd"])
                    P.dve(lambda e: e.reciprocal(out=rstd[:], in_=rstd[:]), ["rstd"], ["rstd"])
                    for j in range(2):
                        P.dve(lambda e, j=j: e.scalar_tensor_tensor(out=tmpf[:], in0=xg[:, j, :], scalar=rstd[:, j:j + 1],
                                                                    in1=Gb[:], op0=ALU.mult, op1=ALU.mult),
                              ["xg%d" % j, "rstd", "Gb"], ["tmpf"])
                        P.dve(lambda e: e.tensor_tensor(out=hb[:], in0=tmpf[:], in1=Sb[:], op=ALU.add),
                              ["tmpf", "Sb"], ["hb"])
                        for c in range(8):
                            P.pe(lambda e, c=c: e.transpose(out=pT[:, c, :], in_=hb[:, c * 128:(c + 1) * 128],
                                                            identity=idb[:]), ["hb", "idb"], ["pT"])
                        P.act(lambda e, j=j: e.copy(out=hT[:, :, j * 128:(j + 1) * 128], in_=pT[:]), ["pT"], ["hT"])
                    for f in range(22):
                        u = iu % 2
                        iu += 1
                        for k in range(8):
                            P.pe(lambda e, u=u, k=k, f=f: e.matmul(pGa[u][:], lhsT=wG[:, k, f * 128:(f + 1) * 128],
                                                                   rhs=hT[:, k, :], start=(k == 0), stop=(k == 7)),
                                 ["wG", "hT"], ["pGa%d" % u])
                        for k in range(8):
                            P.pe(lambda e, u=u, k=k, f=f: e.matmul(pUa[u][:], lhsT=wU[:, k, f * 128:(f + 1) * 128],
                                                                   rhs=hT[:, k, :], start=(k == 0), stop=(k == 7)),
                                 ["wU", "hT"], ["pUa%d" % u])
                        P.act(lambda e, u=u: e.activation(out=sgl[u][:], in_=pGa[u][:], func=AF.Silu),
                              ["pGa%d" % u], ["sgl%d" % u])
                        P.dve(lambda e, u=u, f=f: e.tensor_tensor(out=aT[:, f, :], in0=pUa[u][:], in1=sgl[u][:],
                                                                  op=ALU.mult), ["pUa%d" % u, "sgl%d" % u], ["aT%d" % f])
                    atk = ["aT%d" % f for f in range(22)]
                    for j in range(2):
                        xk = "xg%d" % j
                        for cg in range(2):
                            sy = iy % 2
                            iy += 1
                            for f in range(22):
                                P.pe(lambda e, sy=sy, f=f, j=j, cg=cg: e.matmul(
                                    pDn[sy][:], lhsT=aT[:, f, j * 128:(j + 1) * 128], rhs=wD[:, f, cg * 512:(cg + 1) * 512],
                                    start=(f == 0), stop=(f == 21)), [atk[f], "wD"], ["pDn%d" % sy])
                            P.dve(lambda e, sy=sy, cg=cg: e.tensor_tensor(out=ty[sy][:], in0=pDn[sy][:],
                                                                          in1=gt[:, cg * 512:(cg + 1) * 512], op=ALU.mult),
                                  ["pDn%d" % sy, "gt"], ["ty%d" % sy])
                            P.pool(lambda e, sy=sy, j=j, cg=cg: e.tensor_tensor(
                                out=xg[:, j, cg * 512:(cg + 1) * 512], in0=xg[:, j, cg * 512:(cg + 1) * 512],
                                in1=ty[sy][:], op=ALU.add), ["ty%d" % sy, xk], [xk])
                        if not last:
                            P.dma("pool", xs[t0 + j * 128:t0 + (j + 1) * 128, :], xg[:, j, :], r=[xk], w=["xs"])
                        else:
                            if gi == 0 and j == 0:
                                P.dma("sp", gfin[:], g_fin.partition_broadcast(128), r=[], w=["gfin"])
                            P.dve(lambda e: e.memset(ss[:, 0:1], 0.0), [], ["ss"])
                            P.act(lambda e, j=j: e.activation(out=tmpf[:], in_=xg[:, j, :], func=AF.Square, scale=1.0 / 32,
                                                              accum_out=ss[:, 0:1]), [xk, "ss"], ["tmpf", "ss"])
                            P.act(lambda e: e.activation(out=rstd[:, 0:1], in_=ss[:, 0:1], func=AF.Sqrt, bias=EPS,
                                                         scale=1.0), ["ss"], ["rstd"])
                            P.dve(lambda e: e.reciprocal(out=rstd[:, 0:1], in_=rstd[:, 0:1]), ["rstd"], ["rstd"])
                            P.dve(lambda e, j=j: e.scalar_tensor_tensor(out=xg[:, j, :], in0=xg[:, j, :],
                                                                        scalar=rstd[:, 0:1], in1=gfin[:], op0=ALU.mult,
                                                                        op1=ALU.mult), [xk, "rstd", "gfin"], [xk])
                            P.dma("pool", out[t0 + j * 128:t0 + (j + 1) * 128, :], xg[:, j, :], r=[xk], w=["out"])
                P.barrier()
        P.enabled = True
        if n_layers < DEPTH:
            with Scope(nc) as S:
                xo = S.sb("xo", [128, D], F32)
                for t in range(32):
                    P.dma("sp", xo[:], xs[t * 128:(t + 1) * 128, :], r=["xs"], w=["xo"])
                    P.dma("sp", out[t * 128:(t + 1) * 128, :], xo[:], r=["xo"], w=["out"])
                P.barrier()
        P.barrier()
        stats = P.emit()
    return nc, stats


def _consts():
    ident = np.eye(128, dtype=np.float32)
    rperm = np.zeros((128, 128), np.float32)
    sign = np.zeros(128, np.float32)
    for m in range(128):
        d = m % 64
        base = m - d
        hb = (d // 32) * 32
        dd = d % 32
        if dd < 16:
            pm, sg = hb + dd + 16, -1.0
        else:
            pm, sg = hb + dd - 16, 1.0
        rperm[base + pm, m] = 1.0
        sign[m] = sg
    t = np.arange(SEQ)
    row = (t // 64).astype(np.float32)
    col = (t % 64).astype(np.float32)
    inv = (10000.0 ** (-np.arange(16, dtype=np.float32) / 16)).astype(np.float32)
    ang_r = row[:, None] * inv[None, :]
    ang_c = col[:, None] * inv[None, :]
    ang = np.concatenate([ang_r, ang_r, ang_c, ang_c], axis=-1).astype(np.float32)
    cos = np.cos(ang).astype(np.float32).T
    sin = np.sin(ang).astype(np.float32).T
    cosT = np.ascontiguousarray(np.concatenate([cos, cos], axis=0))
    sinT = np.ascontiguousarray(np.concatenate([sin, sin], axis=0) * sign[:, None]).astype(np.float32)
    a = np.arange(128)
    mprev = (a[:, None] >= a[None, :]).astype(np.float32)
    mnext = (a[:, None] <= a[None, :]).astype(np.float32)
    same = (a[:, None] // 64) == (a[None, :] // 64)
    bmf = (same & (a[:, None] <= a[None, :])).astype(np.float32)
    bmb = (same & (a[:, None] >= a[None, :])).astype(np.float32)
    sel = np.zeros((64, 8, 128), np.float32)
    for d in range(2):
        for h in range(4):
            sel[d * 32 + h, d * 4 + h, :] = 1.0
    return dict(ident=ident, rperm=rperm, cosT=cosT, sinT=sinT, mprev=mprev, mnext=mnext, bmf=bmf, bmb=bmb, sel=sel)


def _prep(inp):
    f = lambda a: np.ascontiguousarray(np.asarray(a, dtype=np.float32))
    w_in = f(inp["w_in"])
    L = w_in.shape[0]
    qa = w_in[:, :, 0:512]
    ka = w_in[:, :, 512:640]
    va = w_in[:, :, 640:768]
    glu = w_in[:, :, 768:1792]
    qkm = w_in[:, :, 1792:2816]
    vm = w_in[:, :, 2816:3328]
    om = w_in[:, :, 3328:3840]
    gm = w_in[:, :, 3840:3856]
    br = w_in[:, :, 3856:6928]
    gch = np.zeros((L, D, 128), np.float32)
    gch[:, :, 0:4] = gm[:, :, 0:4]
    gch[:, :, 32:36] = gm[:, :, 4:8]
    gch[:, :, 64:68] = gm[:, :, 8:12]
    gch[:, :, 96:100] = gm[:, :, 12:16]
    z64 = np.zeros((L, D, 64), np.float32)
    w_fm = np.concatenate([qa, ka[:, :, 0:64], z64, z64, ka[:, :, 0:64], ka[:, :, 64:128], z64, z64, ka[:, :, 64:128],
                           glu, qkm, br, gch], axis=2)
    assert w_fm.shape[2] == NFM * 128
    w_tm = np.concatenate([va, vm, om], axis=2)
    bmg = f(inp["b_mgate"])
    bm = np.zeros((L, 64, 2), np.float32)
    bm[:, 0:4, 0] = bmg[:, 0:4]
    bm[:, 32:36, 0] = bmg[:, 4:8]
    bm[:, 0:4, 1] = bmg[:, 8:12]
    bm[:, 32:36, 1] = bmg[:, 12:16]
    cv = np.zeros((L, 512, 34), np.float32)
    cv[:, :, 0:31] = np.transpose(f(inp["w_conv_dw"]), (0, 2, 1))
    cv[:, :, 31] = f(inp["b_conv_dw"])
    cv[:, :, 32] = f(inp["g_conv_ln"])
    cv[:, :, 33] = f(inp["b_conv_ln"])
    cvec = np.ascontiguousarray(cv.reshape(L, 4, 128, 34).transpose(0, 2, 1, 3))
    wm = np.ascontiguousarray(np.transpose(f(inp["w_mconv"]), (0, 2, 1)).reshape(L, 8, 128, 3).transpose(0, 2, 1, 3))
    gmn = np.ascontiguousarray(f(inp["g_mlstm_norm"]).reshape(L, 4, 128).transpose(0, 2, 1))
    common = dict(
        w_ada=f(inp["w_ada"]), b_ada=f(inp["b_ada"]), g_norm_mix=f(inp["g_norm_mix"]), g_norm_ffn=f(inp["g_norm_ffn"]),
        w_fm=np.ascontiguousarray(w_fm), w_tm=np.ascontiguousarray(w_tm), bm=bm, att_sink=f(inp["att_sink"]),
        w_att_out=f(inp["w_att_out"]), cvec=cvec, w_conv_pw=f(inp["w_conv_pw"]), wm=wm, gmn=gmn,
        w_mlstm_out=f(inp["w_mlstm_out"]), w_out=f(inp["w_out"]), w_ff_gate=f(inp["w_ff_gate"]),
        w_ff_up=f(inp["w_ff_up"]), w_ff_down=f(inp["w_ff_down"]), g_final=f(inp["g_final"]))
    common.update(_consts())
    x = f(inp["x"])
    ctx = f(inp["ctx"])
    c = f(inp["c"])
    c_ctx = f(inp["c_ctx"])
    maps = []
    for core in range(8):
        b = core % 4
        cc = np.stack([c[b], c_ctx], axis=-1).reshape(8, 128, 2).transpose(1, 0, 2)
        m = dict(common)
        m["x"] = np.ascontiguousarray(x[b])
        m["ctx"] = np.ascontiguousarray(ctx[b])
        m["cc"] = np.ascontiguousarray(cc)
        maps.append(m)
    return maps


_NC_CACHE = {}


def kernel(**inputs):
    maps = _prep(inputs)
    if "nc" not in _NC_CACHE:
        _NC_CACHE["nc"] = build()[0]
    nc = _NC_CACHE["nc"]
    res = run_bass_kernel_spmd(nc, maps, core_ids=list(range(8)))
    outs = [np.asarray(res.results[b]["out"], dtype=np.float32) for b in range(4)]
    return np.stack(outs, axis=0)
```

```python
import contextlib
import os
import numpy as np
KD = int(os.environ.get('KD', '9'))
import concourse.bass as bass
import concourse.mybir as mybir
from concourse.bass_utils import run_bass_kernel_spmd

F32 = mybir.dt.float32
BF16 = mybir.dt.bfloat16
AF = mybir.ActivationFunctionType
ALU = mybir.AluOpType
AX = mybir.AxisListType

COMPUTE = ("pe", "act", "dve", "pool", "sp")
EPOCH = 6000
DMA_USES = 1800


class _Rec:
    def __getattr__(self, name):
        def f(*a, **k):
            self.call = (name, a, k)
            return self
        return f


class Prog:
    def __init__(self, nc, n_dma_sems=10):
        self.nc = nc
        self.ins = []
        self.per_eng = {e: [] for e in COMPUTE}
        self.last_writer = {}
        self.readers = {}
        self.nstreams = len(COMPUTE)
        self.stream_id = {e: i for i, e in enumerate(COMPUTE)}
        self.stream_pos = [0] * self.nstreams
        self.stream_last = {}
        self.ic = {e: [0] * self.nstreams for e in COMPUTE}
        self.vc_snap = []
        self.dma_pool = {}
        self.dma_rr = {}
        self.dma_last = {}
        self.dma_uses = {}
        self.n_dma_sems = n_dma_sems
        self.signal = set()

    def _new_stream(self):
        sid = self.nstreams
        self.nstreams += 1
        self.stream_pos.append(0)
        for e in COMPUTE:
            self.ic[e] = self.ic[e] + [0]
        return sid

    def _dma_stream(self, q):
        pool = self.dma_pool.setdefault(q, [])
        if len(pool) < self.n_dma_sems:
            sid = self._new_stream()
            pool.append(sid)
            self.dma_uses[sid] = 0
            self.dma_rr[q] = len(pool) - 1
            return sid
        k = (self.dma_rr[q] + 1) % len(pool)
        self.dma_rr[q] = k
        sid = pool[k]
        if self.dma_uses[sid] >= DMA_USES:
            sid = self._new_stream()
            pool[k] = sid
            self.dma_uses[sid] = 0
        return sid

    enabled = True

    def add(self, eng, fn, reads=(), writes=(), dma=False, force=False, extra=()):
        if not self.enabled:
            return None
        rec = _Rec()
        fn(rec)
        fn = (lambda call: (lambda e: getattr(e, call[0])(*call[1], **call[2])))(rec.call)
        iid = len(self.ins)
        deps = set(extra)
        for k in reads:
            w = self.last_writer.get(k)
            if w is not None:
                deps.add(w)
        for k in writes:
            w = self.last_writer.get(k)
            if w is not None:
                deps.add(w)
            for r in self.readers.get(k, ()):
                deps.add(r)
        if dma:
            sid = self._dma_stream(eng)
            prev = self.dma_last.get(sid)
            if prev is not None:
                deps.add(prev)
            self.dma_last[sid] = iid
            self.dma_uses[sid] += 1
        else:
            sid = self.stream_id[eng]
        self.stream_pos[sid] += 1
        pos = self.stream_pos[sid]
        self.stream_last[sid] = iid
        ic = self.ic[eng]
        waits = []
        rset = set(reads)
        for d in sorted(deps):
            deng, _, ddma, _, dsid, dpos = self.ins[d]
            if not ddma and deng == eng and not dma and not force:
                if eng == "pe":
                    continue
                raw = False
                for k in rset:
                    if self.last_writer.get(k) == d:
                        raw = True
                        break
                if not raw:
                    continue
            if len(ic) < self.nstreams:
                ic = ic + [0] * (self.nstreams - len(ic))
            if ic[dsid] >= dpos:
                continue
            waits.append(d)
            self.signal.add(d)
            snap, s2, p2 = self.vc_snap[d]
            if len(snap) < self.nstreams:
                snap = snap + [0] * (self.nstreams - len(snap))
            new = [a if a > b else b for a, b in zip(ic, snap)]
            if new[s2] < p2:
                new[s2] = p2
            ic = new
        self.ic[eng] = ic
        self.vc_snap.append((ic, sid, pos))
        self.ins.append((eng, fn, dma, waits, sid, pos))
        self.per_eng[eng].append(iid)
        for k in reads:
            self.readers.setdefault(k, []).append(iid)
        for k in writes:
            self.last_writer[k] = iid
            self.readers[k] = []
        return iid

    def pe(self, fn, r=(), w=()):
        return self.add("pe", fn, r, w)

    def act(self, fn, r=(), w=()):
        return self.add("act", fn, r, w)

    def dve(self, fn, r=(), w=()):
        return self.add("dve", fn, r, w)

    def pool(self, fn, r=(), w=()):
        return self.add("pool", fn, r, w)

    def dma(self, q, out, in_, r=(), w=(), **kw):
        return self.add(q, lambda e: e.dma_start(out=out, in_=in_, **kw), r, w, dma=True)

    def barrier(self):
        lasts = list(self.stream_last.values())
        for e in COMPUTE:
            self.add(e, lambda en: en.nop(), (), (), force=True, extra=lasts)

    def emit(self):
        nc = self.nc
        sig_count = {}
        stream_sig = [0] * self.nstreams
        for iid, (eng, fn, dma, waits, sid, pos) in enumerate(self.ins):
            if dma:
                sig_count[iid] = 16 * pos
            elif iid in self.signal:
                stream_sig[sid] += 1
                sig_count[iid] = stream_sig[sid]
        stack = contextlib.ExitStack()
        sems = {}

        def sem_for(sid, cnt):
            if sid < len(COMPUTE):
                ep = (cnt - 1) // EPOCH
                key = (sid, ep)
                val = cnt - ep * EPOCH
            else:
                key = (sid, 0)
                val = cnt
            if key not in sems:
                sems[key] = stack.enter_context(nc.semaphore("s%d_%d" % key))
            return sems[key], val

        for iid in sorted(sig_count):
            sem_for(self.ins[iid][4], sig_count[iid])
        with stack:
            with nc.Block() as block:
                def run(engname):
                    def body(e):
                        for iid in self.per_eng[engname]:
                            _, fn, dma, waits, sid, pos = self.ins[iid]
                            for d in waits:
                                s, v = sem_for(self.ins[d][4], sig_count[d])
                                e.wait_ge(s, v)
                            r = fn(e)
                            if iid in sig_count:
                                s, v = sem_for(sid, sig_count[iid])
                                r.then_inc(s, 16 if dma else 1)
                    return body
                block.sync(run("sp"))
                block.gpsimd(run("pool"))
                block.scalar(run("act"))
                block.vector(run("dve"))
                block.tensor(run("pe"))
        return len(self.ins), len(sems)


D = 1024
SEQ = 4096
CTX = 256
T = SEQ + CTX
NT = T // 128
NCH = T // 64
DEPTH = 4
DFF = 2816
NFM = 49
CK, CGLU, CQKM, CBR, CGATE = 4, 8, 16, 24, 48
NTM = 1152
LN_DK = float(np.log(128.0 ** -0.5))
EPS = 1e-6

GROUPS512 = [(g * 512, 512) for g in range(8)] + [(4096, 256)]
GROUPS256 = [(g * 256, 256) for g in range(17)]
ORDER_F = [64, 65, 66, 67] + list(range(64))
ORDER_B = [67, 66, 65, 64] + list(range(63, -1, -1))


class Scope:
    uid = 0

    def __init__(self, nc):
        self.nc = nc
        self.st = contextlib.ExitStack()
        self.n = 0

    def __enter__(self):
        self.st.__enter__()
        return self

    def __exit__(self, *a):
        return self.st.__exit__(*a)

    def sb(self, name, shape, dt):
        Scope.uid += 1
        return self.st.enter_context(self.nc.sbuf_tensor("%s_u%d" % (name, Scope.uid), list(shape), dt))

    def ps(self, name, shape, dt=F32):
        Scope.uid += 1
        return self.st.enter_context(self.nc.psum_tensor("%s_u%d" % (name, Scope.uid), list(shape), dt))


def build(n_layers=DEPTH, debug=False, stop=None):
    nc = bass.Bass("TRN2", target_bir_lowering=False)
    P = Prog(nc)

    def phase(l, k):
        P.enabled = stop is None or (l, k) <= stop

    def din(name, shape, dt=F32):
        return nc.dram_tensor(name, list(shape), dt, kind="ExternalInput").ap()

    def dscr(name, shape, dt):
        kind = "ExternalOutput" if debug else "Internal"
        return nc.dram_tensor(name, list(shape), dt, kind=kind).ap()

    x_in = din("x", [SEQ, D])
    ctx_in = din("ctx", [CTX, D])
    cc_in = din("cc", [128, 8, 2])
    w_ada = din("w_ada", [DEPTH, D, 6 * D])
    b_ada = din("b_ada", [DEPTH, 6 * D])
    g_mix = din("g_norm_mix", [DEPTH, D])
    g_ffn = din("g_norm_ffn", [DEPTH, D])
    w_fm = din("w_fm", [DEPTH, D, NFM * 128])
    w_tm = din("w_tm", [DEPTH, D, NTM])
    bm_in = din("bm", [DEPTH, 64, 2])
    sink_in = din("att_sink", [DEPTH, 8])
    w_ao = din("w_att_out", [DEPTH, 512, D])
    cvec_in = din("cvec", [DEPTH, 128, 4, 34])
    w_pw = din("w_conv_pw", [DEPTH, 512, D])
    wm_in = din("wm", [DEPTH, 128, 8, 3])
    gmn_in = din("gmn", [DEPTH, 128, 4])
    w_mo = din("w_mlstm_out", [DEPTH, 512, D])
    w_out = din("w_out", [DEPTH, D, D])
    w_g = din("w_ff_gate", [DEPTH, D, DFF])
    w_u = din("w_ff_up", [DEPTH, D, DFF])
    w_d = din("w_ff_down", [DEPTH, DFF, D])
    g_fin = din("g_final", [D])
    ident_in = din("ident", [128, 128])
    rperm_in = din("rperm", [128, 128])
    cos_in = din("cosT", [128, SEQ])
    sin_in = din("sinT", [128, SEQ])
    mprev_in = din("mprev", [128, 128])
    mnext_in = din("mnext", [128, 128])
    bmf_in = din("bmf", [128, 128])
    bmb_in = din("bmb", [128, 128])
    sel_in = din("sel", [64, 8, 128])
    out = nc.dram_tensor("out", [SEQ, D], F32, kind="ExternalOutput").ap()

    xs = dscr("xs", [T, D], F32)
    modd = dscr("modd", [DEPTH, 2, 6 * D], F32)
    projT = dscr("projT", [48 * 128, T], BF16)
    gatesT = dscr("gatesT", [128, T], F32)
    projM = dscr("projM", [9, 128, NT * 128], BF16)
    attT = dscr("attT", [512, T], BF16)
    cbT = dscr("cbT", [512, T], BF16)
    mhT = dscr("mhT", [512, T], BF16)

    with Scope(nc) as G:
        id32 = G.sb("id32", [128, 128], F32)
        idb = G.sb("idb", [128, 128], BF16)
        ones32 = G.sb("ones32", [128, 128], F32)
        P.dma("sp", id32[:], ident_in[:, :], w=["id32"])
        P.dve(lambda e: e.tensor_copy(out=idb[:], in_=id32[:]), ["id32"], ["idb"])
        P.dve(lambda e: e.memset(ones32[:], 1.0), [], ["ones32"])
        for i in range(16):
            P.dma("sp", xs[i * 256:(i + 1) * 256, :], x_in[i * 256:(i + 1) * 256, :], w=["xs_i%d" % i])
        P.dma("sp", xs[SEQ:T, :], ctx_in[:, :], w=["xs_c"])

        with Scope(nc) as S:
            cc = S.sb("cc", [128, 8, 2], F32)
            scc = S.sb("scc", [128, 8, 2], F32)
            wa = [S.sb("wa%d" % i, [128, 8, 512], F32) for i in range(2)]
            bad = S.sb("bad", [2, 6 * D], F32)
            mrow = S.sb("mrow", [2, 6 * D], F32)
            pm = [S.ps("pm%d" % i, [2, 512]) for i in range(2)]
            P.dma("sp", cc[:], cc_in[:, :, :], w=["cc"])
            P.act(lambda e: e.activation(out=scc[:], in_=cc[:], func=AF.Silu), ["cc"], ["scc"])
            it = 0
            for l in range(n_layers):
                P.dma("sp", bad[:], b_ada[l, :].partition_broadcast(2), w=["bad"])
                for cg in range(12):
                    s = it % 2
                    it += 1
                    P.dma("sp", wa[s][:], w_ada[l, :, cg * 512:(cg + 1) * 512].rearrange("(c p) n -> p c n", p=128),
                          w=["wa%d" % s])
                    for k in range(8):
                        P.pe(lambda e, s=s, k=k: e.matmul(pm[s][:], lhsT=scc[:, k, :], rhs=wa[s][:, k, :],
                                                         start=(k == 0), stop=(k == 7)),
                             ["scc", "wa%d" % s], ["pm%d" % s])
                    P.dve(lambda e, s=s, cg=cg: e.tensor_tensor(out=mrow[:, cg * 512:(cg + 1) * 512], in0=pm[s][:],
                                                               in1=bad[:, cg * 512:(cg + 1) * 512], op=ALU.add),
                          ["pm%d" % s, "bad"], ["mrow"])
                P.dma("pool", modd[l, :, :], mrow[:], r=["mrow"], w=["modd"])
            P.barrier()

        def load_mod(eng_q, tile, l, stream, k, key):
            P.dma(eng_q, tile[:], modd[l, stream, k * D:(k + 1) * D].partition_broadcast(128), r=["modd"], w=[key])

        def make_gain(gt, gkey, sct, sckey, gvec_dram, tmp, tmpkey):
            P.dma("sp", tmp[:], gvec_dram.partition_broadcast(128), w=[tmpkey])
            P.dve(lambda e: e.scalar_tensor_tensor(out=gt[:], in0=sct[:], scalar=1.0, in1=tmp[:],
                                                   op0=ALU.add, op1=ALU.mult),
                  [sckey, tmpkey], [gkey])

        for l in range(n_layers):
            last = (l == DEPTH - 1)
            phase(l, 1)
            with Scope(nc) as S:
                wfm = S.sb("wfm", [128, 8, NFM * 128], BF16)
                wtm = S.sb("wtm", [128, 8, NTM], BF16)
                cosT = S.sb("cosT", [128, SEQ], F32)
                sinT = S.sb("sinT", [128, SEQ], F32)
                rp32 = S.sb("rp32", [128, 128], F32)
                rpb = S.sb("rpb", [128, 128], BF16)
                Gb = S.sb("Gb", [128, D], F32)
                Sb = S.sb("Sb", [128, D], F32)
                xg = S.sb("xg", [128, 4, D], F32)
                tmpf = S.sb("tmpf", [128, D], F32)
                hb = S.sb("hb", [128, D], BF16)
                hT = S.sb("hT", [128, 8, 512], BF16)
                ss = S.sb("ss", [128, 4], F32)
                rstd = S.sb("rstd", [128, 4], F32)
                fo = [S.sb("fo%d" % i, [128, 512], BF16) for i in range(3)]
                qraw = [S.sb("qraw%d" % i, [128, 512], BF16) for i in range(2)]
                t1 = [S.sb("t1%d" % i, [128, 512], F32) for i in range(2)]
                t2 = [S.sb("t2%d" % i, [128, 512], F32) for i in range(2)]
                go = S.sb("go", [128, 512], F32)
                tmo = [S.sb("tmo%d" % i, [128, NTM], BF16) for i in range(2)]
                pT = S.ps("pT", [128, 8, 128], BF16)
                pTM = [S.ps("pTM%d" % i, [128, 512]) for i in range(2)]
                pFM = [S.ps("pFM%d" % i, [128, 512]) for i in range(3)]
                pR = S.ps("pR", [128, 512])

                for k in range(8):
                    P.dma("pool", wfm[:, k, :], w_fm[l, k * 128:(k + 1) * 128, :], w=["wfm%d" % k])
                P.dma("pool", wtm[:], w_tm[l, :, :].rearrange("(c p) n -> p c n", p=128), w=["wtm"])
                wfm_keys = ["wfm%d" % k for k in range(8)]
                P.dma("sp", cosT[:], cos_in[:, :], w=["cosT"])
                P.dma("sp", sinT[:], sin_in[:, :], w=["sinT"])
                P.dma("sp", rp32[:], rperm_in[:, :], w=["rp32"])
                P.dve(lambda e: e.tensor_copy(out=rpb[:], in_=rp32[:]), ["rp32"], ["rpb"])
                ifm = 0
                ifo = 0
                iq = 0
                itm = 0
                cur_stream = None
                for gi, (t0, n) in enumerate(GROUPS512):
                    stream = 0 if t0 < SEQ else 1
                    if stream != cur_stream:
                        cur_stream = stream
                        load_mod("sp", Sb, l, stream, 0, "Sb")
                        load_mod("sp", tmpf, l, stream, 1, "tmpf")
                        P.dma("sp", Gb[:], g_mix[l, :].partition_broadcast(128), w=["Gb"])
                        P.dve(lambda e: e.scalar_tensor_tensor(out=Gb[:], in0=tmpf[:], scalar=1.0, in1=Gb[:],
                                                               op0=ALU.add, op1=ALU.mult),
                              ["tmpf", "Gb"], ["Gb"])
                    ntl = n // 128
                    P.dve(lambda e: e.memset(ss[:], 0.0), [], ["ss"])
                    for j in range(ntl):
                        P.dma("sp", xg[:, j, :], xs[t0 + j * 128:t0 + (j + 1) * 128, :], r=["xs"], w=["xg%d" % j])
                        P.act(lambda e, j=j: e.activation(out=tmpf[:], in_=xg[:, j, :], func=AF.Square, scale=1.0 / 32,
                                                          accum_out=ss[:, j:j + 1]),
                              ["xg%d" % j, "ss"], ["tmpf", "ss"])
                    P.act(lambda e: e.activation(out=rstd[:], in_=ss[:], func=AF.Sqrt, bias=EPS, scale=1.0),
                          ["ss"], ["rstd"])
                    P.dve(lambda e: e.reciprocal(out=rstd[:], in_=rstd[:]), ["rstd"], ["rstd"])
                    for j in range(ntl):
                        P.dve(lambda e, j=j: e.scalar_tensor_tensor(out=tmpf[:], in0=xg[:, j, :], scalar=rstd[:, j:j + 1],
                                                                    in1=Gb[:], op0=ALU.mult, op1=ALU.mult),
                              ["xg%d" % j, "rstd", "Gb"], ["tmpf"])
                        P.dve(lambda e: e.tensor_tensor(out=hb[:], in0=tmpf[:], in1=Sb[:], op=ALU.add),
                              ["tmpf", "Sb"], ["hb"])
                        for c in range(8):
                            P.pe(lambda e, c=c: e.transpose(out=pT[:, c, :], in_=hb[:, c * 128:(c + 1) * 128],
                                                            identity=idb[:]),
                                 ["hb", "idb"], ["pT"])
                        P.act(lambda e, j=j: e.copy(out=hT[:, :, j * 128:(j + 1) * 128], in_=pT[:]), ["pT"], ["hT"])
                    for j in range(ntl):
                        so = itm % 2
                        itm += 1
                        for (c0, cn) in ((0, 512), (512, 512), (1024, 128)):
                            sp_ = ifm % 2
                            ifm += 1
                            for k in range(8):
                                P.pe(lambda e, sp_=sp_, k=k, j=j, c0=c0, cn=cn: e.matmul(
                                    pTM[sp_][:, 0:cn], lhsT=hT[:, k, j * 128:(j + 1) * 128], rhs=wtm[:, k, c0:c0 + cn],
                                    start=(k == 0), stop=(k == 7)), ["hT", "wtm"], ["pTM%d" % sp_])
                            P.dve(lambda e, sp_=sp_, so=so, c0=c0, cn=cn: e.tensor_copy(
                                out=tmo[so][:, c0:c0 + cn], in_=pTM[sp_][:, 0:cn]), ["pTM%d" % sp_], ["tmo%d" % so])
                        tt_ = (t0 // 128) + j
                        P.dma("pool", projM[:, :, tt_ * 128:(tt_ + 1) * 128].rearrange("b p d -> p b d"),
                              tmo[so][:].rearrange("p (b d) -> p b d", d=128), r=["tmo%d" % so], w=["projM"])
                    for c in range(NFM):
                        sp_ = ifm % 3
                        ifm += 1
                        for k in range(8):
                            P.pe(lambda e, sp_=sp_, k=k, c=c, n=n: e.matmul(
                                pFM[sp_][:, 0:n], lhsT=wfm[:, k, c * 128:(c + 1) * 128], rhs=hT[:, k, 0:n],
                                start=(k == 0), stop=(k == 7)), ["hT", "wfm%d" % k], ["pFM%d" % sp_])
                        pk = "pFM%d" % sp_
                        if c == CGATE:
                            P.act(lambda e, sp_=sp_, n=n: e.copy(out=go[:, 0:n], in_=pFM[sp_][:, 0:n]), [pk], ["go"])
                            P.dma("pool", gatesT[:, t0:t0 + n], go[:, 0:n], r=["go"], w=["gatesT"])
                            continue
                        so = ifo % 3
                        ifo += 1
                        fk = "fo%d" % so
                        if c < CGLU and stream == 0:
                            sq = iq % 2
                            iq += 1
                            qk_ = "qraw%d" % sq
                            P.act(lambda e, sp_=sp_, sq=sq, n=n: e.copy(out=qraw[sq][:, 0:n], in_=pFM[sp_][:, 0:n]),
                                  [pk], [qk_])
                            P.pe(lambda e, sq=sq, n=n: e.matmul(pR[:, 0:n], lhsT=rpb[:], rhs=qraw[sq][:, 0:n],
                                                                start=True, stop=True), [qk_, "rpb"], ["pR"])
                            P.dve(lambda e, sq=sq, n=n, t0=t0: e.tensor_tensor(out=t1[sq][:, 0:n], in0=qraw[sq][:, 0:n],
                                                                               in1=cosT[:, t0:t0 + n], op=ALU.mult),
                                  [qk_, "cosT"], ["t1%d" % sq])
                            P.dve(lambda e, sq=sq, n=n, t0=t0: e.tensor_tensor(out=t2[sq][:, 0:n], in0=pR[:, 0:n],
                                                                               in1=sinT[:, t0:t0 + n], op=ALU.mult),
                                  ["pR", "sinT"], ["t2%d" % sq])
                            P.pool(lambda e, sq=sq, so=so, n=n: e.tensor_tensor(out=fo[so][:, 0:n], in0=t2[sq][:, 0:n],
                                                                                in1=t1[sq][:, 0:n], op=ALU.add),
                                   ["t2%d" % sq, "t1%d" % sq], [fk])
                        elif CBR <= c < CGATE:
                            P.act(lambda e, sp_=sp_, so=so, n=n: e.activation(out=fo[so][:, 0:n], in_=pFM[sp_][:, 0:n],
                                                                              func=AF.Sigmoid), [pk], [fk])
                        else:
                            if c % 2 == 0:
                                P.act(lambda e, sp_=sp_, so=so, n=n: e.copy(out=fo[so][:, 0:n], in_=pFM[sp_][:, 0:n]),
                                      [pk], [fk])
                            else:
                                P.dve(lambda e, sp_=sp_, so=so, n=n: e.tensor_copy(out=fo[so][:, 0:n],
                                                                                   in_=pFM[sp_][:, 0:n]), [pk], [fk])
                        P.dma("pool", projT[c * 128:(c + 1) * 128, t0:t0 + n], fo[so][:, 0:n], r=[fk], w=["projT"])
                P.barrier()

            phase(l, 2)
            with Scope(nc) as S:
                qT = S.sb("qT", [128, 4, T], BF16)
                kT = S.sb("kT", [128, 4, T], BF16)
                va = S.sb("va", [128, NT, 2, 65], BF16)
                vtmp = S.sb("vtmp", [128, NT * 128], BF16)
                aT = S.sb("aT", [128, 4, T], BF16)
                m32 = S.sb("m32", [128, 2, 128], F32)
                mk = S.sb("mk", [128, 2, 4, 128], BF16)
                sk = S.sb("sk", [128, 8], F32)
                esk = S.sb("esk", [128, 8], F32)
                pt = [S.sb("pt%d" % i, [128, 5, 512], BF16) for i in range(2)]
                den = S.sb("den", [128, 4], F32)
                att = [S.sb("att%d" % i, [128, 512], BF16) for i in range(2)]
                pST = [S.ps("pST%d" % i, [128, 512]) for i in range(3)]
                pPV = [S.ps("pPV%d" % i, [128, 4, 128]) for i in range(2)]
                pAT = S.ps("pAT", [128, 4, 128], BF16)
                for c in range(4):
                    P.dma("sp", qT[:, c, :], projT[c * 128:(c + 1) * 128, :], r=["projT"], w=["qT"])
                for c in range(4):
                    P.dma("sp", kT[:, c, :], projT[(CK + c) * 128:(CK + 1 + c) * 128, :], r=["projT"], w=["kT"])
                P.dma("sp", vtmp[:], projM[0, :, :], r=["projM"], w=["vtmp"])
                P.dve(lambda e: e.tensor_copy(out=va[:, :, :, 0:64],
                                              in_=vtmp[:].rearrange("p (t g d) -> p t g d", g=2, d=64)), ["vtmp"], ["va"])
                P.pool(lambda e: e.memset(va[:, :, :, 64:65], 1.0), [], ["va"])
                P.dma("sp", m32[:, 0, :], mprev_in[:, :], w=["m32"])
                P.dma("sp", m32[:, 1, :], mnext_in[:, :], w=["m32"])
                for i in range(2):
                    for hh in range(4):
                        P.dve(lambda e, i=i, hh=hh: e.tensor_copy(out=mk[:, i, hh, :], in_=m32[:, i, :]), ["m32"], ["mk"])
                P.dma("sp", sk[:], sink_in[l, :].partition_broadcast(128), w=["sk"])
                P.act(lambda e: e.activation(out=esk[:], in_=sk[:], func=AF.Exp), ["sk"], ["esk"])
                ist = 0
                ipv = 0
                iat = 0
                qblocks = list(range(NT)) if not last else list(range(32))
                if KD < 1:
                    qblocks = []
                for n in qblocks:
                    sa = iat % 2
                    iat += 1
                    ak = "att%d" % sa
                    for g in range(2):
                        if n < 32:
                            kbs = [(32, None), (33, None)]
                            if n > 0:
                                kbs.append((n - 1, 0))
                            kbs.append((n, None))
                            if n < 31:
                                kbs.append((n + 1, 1))
                        else:
                            kbs = [(32, None), (33, None)]
                        spt = ipv % 2
                        ptk = "pt%d" % spt
                        pvk = "pPV%d" % spt
                        ipv += 1
                        for bi, (kb, mi) in enumerate(kbs):
                            s_ = ist % 3
                            ist += 1
                            for hh in range(4):
                                h = 4 * g + hh
                                p0 = (h % 2) * 64
                                P.pe(lambda e, s_=s_, hh=hh, h=h, kb=kb, n=n, g=g: e.matmul(
                                    pST[s_][:, hh * 128:(hh + 1) * 128],
                                    lhsT=kT[:, 2 * g + (h % 2), kb * 128:(kb + 1) * 128],
                                    rhs=qT[:, h // 2, n * 128:(n + 1) * 128], start=True, stop=True),
                                    ["qT", "kT"], ["pST%d" % s_])
                            P.act(lambda e, s_=s_, spt=spt, bi=bi: e.activation(out=pt[spt][:, bi, :], in_=pST[s_][:],
                                                                               func=AF.Exp, scale=0.125),
                                  ["pST%d" % s_], [ptk + "_%d" % bi])
                            if mi is not None:
                                P.pool(lambda e, spt=spt, bi=bi, mi=mi: e.tensor_tensor(
                                    out=pt[spt][:, bi, :], in0=pt[spt][:, bi, :],
                                    in1=mk[:, mi, :, :].rearrange("p a b -> p (a b)"), op=ALU.mult),
                                    [ptk + "_%d" % bi, "mk"], [ptk + "_%d" % bi])
                        for hh in range(4):
                            for bi, (kb, mi) in enumerate(kbs):
                                P.pe(lambda e, spt=spt, bi=bi, hh=hh, kb=kb, g=g, nb=len(kbs): e.matmul(
                                    pPV[spt][:, hh, 0:65], lhsT=pt[spt][:, bi, hh * 128:(hh + 1) * 128],
                                    rhs=va[:, kb, g, :], start=(bi == 0), stop=(bi == nb - 1)),
                                    [ptk + "_%d" % bi, "va"], [pvk])
                        if KD < 3:
                            continue
                        P.dve(lambda e, spt=spt, g=g: e.tensor_tensor(out=den[:], in0=pPV[spt][:, :, 64],
                                                                      in1=esk[:, 4 * g:4 * g + 4], op=ALU.add),
                              [pvk, "esk"], ["den"])
                        P.dve(lambda e: e.reciprocal(out=den[:], in_=den[:]), ["den"], ["den"])
                        for hh in range(4):
                            h = 4 * g + hh
                            P.dve(lambda e, spt=spt, hh=hh, h=h, sa=sa: e.tensor_scalar(
                                out=att[sa][:, h * 64:(h + 1) * 64], in0=pPV[spt][:, hh, 0:64],
                                scalar1=den[:, hh:hh + 1], scalar2=None, op0=ALU.mult), [pvk, "den"], [ak])
                    for c in (range(4) if KD >= 4 else []):
                        P.pe(lambda e, sa=sa, c=c: e.transpose(out=pAT[:, c, :], in_=att[sa][:, c * 128:(c + 1) * 128],
                                                               identity=idb[:]), [ak, "idb"], ["pAT"])
                    P.act(lambda e, n=n: e.copy(out=aT[:, :, n * 128:(n + 1) * 128], in_=pAT[:]), ["pAT"], ["aT"])
                nq = len(qblocks) * 128
                for c in (range(4) if nq else []):
                    P.dma("pool", attT[c * 128:(c + 1) * 128, 0:nq], aT[:, c, 0:nq], r=["aT"], w=["attT"])
                P.barrier()

            phase(l, 3)
            with Scope(nc) as S:
                cv = S.sb("cv", [128, 4, 34], F32)
                Y = S.sb("Y", [128, 4, 512], F32)
                zb = S.sb("zb", [128, 4, 15 + SEQ + 15], BF16)
                zcb = S.sb("zcb", [128, 4, 15 + CTX + 15], BF16)
                dg = S.sb("dg", [128, 4, 31, 128], BF16)
                vl = [S.sb("vl%d" % i, [128, T], BF16) for i in range(2)]
                gl = [S.sb("gl%d" % i, [128, T], BF16) for i in range(2)]
                sg = S.sb("sg", [128, T], F32)
                sq = [S.sb("sq%d" % i, [128, 512], F32) for i in range(2)]
                mean = S.sb("mean", [128, 512], F32)
                m2 = S.sb("m2", [128, 512], F32)
                rs = S.sb("rs", [128, 512], F32)
                tt = [S.sb("tt%d" % i, [128, 512], F32) for i in range(2)]
                co = [S.sb("co%d" % i, [128, 512], BF16) for i in range(2)]
                pY = [S.ps("pY%d" % i, [128, 512]) for i in range(2)]
                pS1 = S.ps("pS1", [128, 512])
                pS2 = S.ps("pS2", [128, 512])
                P.dma("sp", cv[:], cvec_in[l, :, :, :], w=["cv"])
                P.pool(lambda e: e.memset(zb[:], 0.0), [], ["zb"])
                P.pool(lambda e: e.memset(zcb[:], 0.0), [], ["zcb"])
                for c in range(4):
                    for j in range(31):
                        eng = P.dve if (j % 3) else P.pool
                        eng(lambda e, c=c, j=j: e.tensor_scalar(out=dg[:, c, j, :], in0=idb[:], scalar1=cv[:, c, j:j + 1],
                                                                scalar2=None, op0=ALU.mult), ["idb", "cv"], ["dg%d" % c])
                for c in range(4):
                    s_ = c % 2
                    P.dma("sp", vl[s_][:], projT[(CGLU + c) * 128:(CGLU + 1 + c) * 128, :], r=["projT"], w=["vl%d" % s_])
                    P.dma("sp", gl[s_][:], projT[(CGLU + 4 + c) * 128:(CGLU + 5 + c) * 128, :], r=["projT"],
                          w=["gl%d" % s_])
                    P.act(lambda e, s_=s_: e.activation(out=sg[:], in_=gl[s_][:], func=AF.Sigmoid),
                          ["gl%d" % s_], ["sg"])
                    P.dve(lambda e, s_=s_, c=c: e.tensor_tensor(out=zb[:, c, 15:15 + SEQ], in0=vl[s_][:, 0:SEQ],
                                                                in1=sg[:, 0:SEQ], op=ALU.mult),
                          ["vl%d" % s_, "sg", "zb"], ["zb%d" % c])
                    P.dve(lambda e, s_=s_, c=c: e.tensor_tensor(out=zcb[:, c, 15:15 + CTX], in0=vl[s_][:, SEQ:T],
                                                                in1=sg[:, SEQ:T], op=ALU.mult),
                          ["vl%d" % s_, "sg", "zcb"], ["zcb%d" % c])
                groups = GROUPS512 if not last else GROUPS512[:8]
                isq = 0
                ico = 0
                ipy = 0
                for (t0, n) in groups:
                    ykeys = ["Yc%d" % c for c in range(4)]
                    for c in range(4):
                        sp_ = ipy % 2
                        ipy += 1
                        for j in range(31):
                            if t0 < SEQ:
                                P.pe(lambda e, sp_=sp_, c=c, j=j, t0=t0, n=n: e.matmul(
                                    pY[sp_][:, 0:n], lhsT=dg[:, c, j, :], rhs=zb[:, c, t0 + j:t0 + j + n],
                                    start=(j == 0), stop=(j == 30)), ["dg%d" % c, "zb%d" % c], ["pY%d" % sp_])
                            else:
                                P.pe(lambda e, sp_=sp_, c=c, j=j, n=n: e.matmul(
                                    pY[sp_][:, 0:n], lhsT=dg[:, c, j, :], rhs=zcb[:, c, j:j + n],
                                    start=(j == 0), stop=(j == 30)), ["dg%d" % c, "zcb%d" % c], ["pY%d" % sp_])
                        P.act(lambda e, sp_=sp_, c=c, n=n: e.activation(out=Y[:, c, 0:n], in_=pY[sp_][:, 0:n],
                                                                        func=AF.Identity, bias=cv[:, c, 31:32], scale=1.0),
                              ["pY%d" % sp_, "cv"], [ykeys[c]])
                    for c in range(4):
                        P.pe(lambda e, c=c, n=n: e.matmul(pS1[:, 0:n], lhsT=ones32[:], rhs=Y[:, c, 0:n],
                                                          start=(c == 0), stop=(c == 3)),
                             ["ones32", ykeys[c]], ["pS1"])
                    for c in range(4):
                        s_ = isq % 2
                        isq += 1
                        P.act(lambda e, c=c, s_=s_, n=n: e.activation(out=sq[s_][:, 0:n], in_=Y[:, c, 0:n],
                                                                      func=AF.Square), [ykeys[c]], ["sq%d" % s_])
                        P.pe(lambda e, c=c, s_=s_, n=n: e.matmul(pS2[:, 0:n], lhsT=ones32[:], rhs=sq[s_][:, 0:n],
                                                                 start=(c == 0), stop=(c == 3)),
                             ["ones32", "sq%d" % s_], ["pS2"])
                    P.act(lambda e, n=n: e.activation(out=mean[:, 0:n], in_=pS1[:, 0:n], func=AF.Copy, scale=1.0 / 512),
                          ["pS1"], ["mean"])
                    P.dve(lambda e, n=n: e.tensor_tensor(out=m2[:, 0:n], in0=mean[:, 0:n], in1=mean[:, 0:n], op=ALU.mult),
                          ["mean"], ["m2"])
                    P.dve(lambda e, n=n: e.scalar_tensor_tensor(out=rs[:, 0:n], in0=pS2[:, 0:n], scalar=1.0 / 512,
                                                                in1=m2[:, 0:n], op0=ALU.mult, op1=ALU.subtract),
                          ["pS2", "m2"], ["rs"])
                    P.act(lambda e, n=n: e.activation(out=rs[:, 0:n], in_=rs[:, 0:n], func=AF.Sqrt, bias=EPS, scale=1.0),
                          ["rs"], ["rs"])
                    P.dve(lambda e, n=n: e.reciprocal(out=rs[:, 0:n], in_=rs[:, 0:n]), ["rs"], ["rs"])
                    for c in range(4):
                        s_ = ico % 2
                        ico += 1
                        P.dve(lambda e, c=c, s_=s_, n=n: e.tensor_tensor(out=tt[s_][:, 0:n], in0=Y[:, c, 0:n],
                                                                         in1=mean[:, 0:n], op=ALU.subtract),
                              [ykeys[c], "mean"], ["tt%d" % s_])
                        P.dve(lambda e, s_=s_, n=n: e.tensor_tensor(out=tt[s_][:, 0:n], in0=tt[s_][:, 0:n],
                                                                    in1=rs[:, 0:n], op=ALU.mult),
                              ["tt%d" % s_, "rs"], ["tt%d" % s_])
                        P.act(lambda e, c=c, s_=s_, n=n: e.activation(out=co[s_][:, 0:n], in_=tt[s_][:, 0:n], func=AF.Silu,
                                                                      scale=cv[:, c, 32:33], bias=cv[:, c, 33:34]),
                              ["tt%d" % s_, "cv"], ["co%d" % s_])
                        P.dma("pool", cbT[c * 128:(c + 1) * 128, t0:t0 + n], co[s_][:, 0:n], r=["co%d" % s_],
                              w=["cbT"])
                P.barrier()

            phase(l, 4)
            with Scope(nc) as S4:
                wkT = S4.sb("wkT", [128, NT, 8], F32)
                wkH = S4.sb("wkH", [128, NT, 2, 8], F32)
                thT = S4.sb("thT", [128, NT, 8], F32)
                decB = S4.sb("decB", [128, 8, NCH], F32)
                wmv = S4.sb("wmv", [128, 8, 3], F32)
                gmn = S4.sb("gmn", [128, 4], F32)
                bm32 = S4.sb("bm32", [128, 2, 128], F32)
                bmk = S4.sb("bmk", [128, 2, 128], BF16)
                P.dma("sp", wmv[:], wm_in[l, :, :, :], w=["wmv"])
                P.dma("sp", gmn[:], gmn_in[l, :, :], w=["gmn"])
                P.dma("sp", bm32[:, 0, :], bmf_in[:, :], w=["bm32"])
                P.dma("sp", bm32[:, 1, :], bmb_in[:, :], w=["bm32"])
                P.dve(lambda e: e.tensor_copy(out=bmk[:], in_=bm32[:]), ["bm32"], ["bmk"])
                with Scope(nc) as S:
                    gI = S.sb("gI", [64, T], F32)
                    gF = S.sb("gF", [64, T], F32)
                    csp = S.sb("csp", [64, T], F32)
                    nb = S.sb("nb", [64, T], F32)
                    msk = S.sb("msk", [64, T], F32)
                    bmt = S.sb("bmt", [64, 2], F32)
                    nbf = S.sb("nbf", [64, 1], F32)
                    tot = S.sb("tot", [64, NCH], F32)
                    umax = S.sb("umax", [64, NCH], F32)
                    R = S.sb("R", [64, NCH], F32)
                    mc = S.sb("mc", [64, NCH + 1], F32)
                    dec = S.sb("dec", [64, NCH], F32)
                    sel = S.sb("sel", [64, 8, 128], F32)
                    pG = [S.ps("pG%d" % i, [128, 4, 2, 64]) for i in range(2)]
                    pD = [S.ps("pD%d" % i, [128, 4, NCH]) for i in range(2)]
                    v3 = lambda a: a[:].rearrange("p (c l) -> p c l", l=64)
                    P.dma("sp", gI[:], gatesT[0:64, :], r=["gatesT"], w=["gI"])
                    P.dma("sp", gF[:], gatesT[64:128, :], r=["gatesT"], w=["gF"])
                    P.dma("sp", bmt[:], bm_in[l, :, :], w=["bmt"])
                    P.dma("sp", sel[:], sel_in[:, :, :], w=["sel"])
                    P.dve(lambda e: e.tensor_scalar(out=nbf[:], in0=bmt[:, 1:2], scalar1=-1.0, scalar2=None, op0=ALU.mult),
                          ["bmt"], ["nbf"])
                    P.pool(lambda e: e.memset(msk[:], 1.0), [], ["msk"])
                    P.pool(lambda e: e.memset(v3(msk)[:, :, 0:1], 0.0), [], ["msk"])
                    P.act(lambda e: e.activation(out=gI[:], in_=gI[:], func=AF.Identity, bias=bmt[:, 0:1], scale=1.0),
                          ["gI", "bmt"], ["gI"])
                    P.act(lambda e: e.activation(out=gF[:], in_=gF[:], func=AF.Exp, bias=nbf[:], scale=-1.0),
                          ["gF", "nbf"], ["gF"])
                    P.act(lambda e: e.activation(out=gF[:], in_=gF[:], func=AF.Ln, bias=1.0, scale=1.0), ["gF"], ["gF"])
                    P.dve(lambda e: e.tensor_tensor_scan(out=csp[:], data0=msk[:], data1=gF[:], initial=0.0,
                                                         op0=ALU.mult, op1=ALU.add), ["msk", "gF"], ["csp"])
                    P.dve(lambda e: e.tensor_copy(out=tot[:], in_=v3(csp)[:, :, 63]), ["csp"], ["tot"])
                    P.dve(lambda e: e.tensor_copy(out=nb[0:32, :], in_=csp[0:32, :]), ["csp"], ["nb"])
                    P.dve(lambda e: e.tensor_tensor(out=nb[32:64, :], in0=gF[32:64, :], in1=csp[32:64, :],
                                                    op=ALU.subtract), ["gF", "csp"], ["nb"])
                    P.dve(lambda e: e.tensor_tensor(out=v3(nb)[32:64], in0=v3(nb)[32:64],
                                                    in1=tot[32:64, :].unsqueeze(2).broadcast_to([32, NCH, 64]),
                                                    op=ALU.add), ["nb", "tot"], ["nb"])
                    P.dve(lambda e: e.tensor_tensor(out=gI[:], in0=gI[:], in1=nb[:], op=ALU.add), ["gI", "nb"], ["gI"])
                    P.dve(lambda e: e.tensor_reduce(out=umax[:], in_=v3(gI), axis=AX.X, op=ALU.max), ["gI"], ["umax"])
                    P.dve(lambda e: e.memset(mc[:], -1e30), [], ["mc"])
                    for (r0, order) in ((0, ORDER_F), (32, ORDER_B)):
                        for j, c in enumerate(order):
                            P.dve(lambda e, r0=r0, c=c: e.tensor_tensor(out=R[r0:r0 + 32, c:c + 1],
                                                                        in0=mc[r0:r0 + 32, c:c + 1],
                                                                        in1=umax[r0:r0 + 32, c:c + 1], op=ALU.max),
                                  ["mc", "umax"], ["R"])
                            if j + 1 < len(order):
                                c2 = order[j + 1]
                                P.dve(lambda e, r0=r0, c=c, c2=c2: e.tensor_tensor(
                                    out=mc[r0:r0 + 32, c2:c2 + 1], in0=R[r0:r0 + 32, c:c + 1],
                                    in1=tot[r0:r0 + 32, c:c + 1], op=ALU.subtract), ["R", "tot"], ["mc"])
                    Rb = lambda: R[:, :].unsqueeze(2).broadcast_to([64, NCH, 64])
                    P.dve(lambda e: e.tensor_tensor(out=v3(gI), in0=v3(gI), in1=Rb(), op=ALU.subtract), ["gI", "R"], ["gI"])
                    P.dve(lambda e: e.tensor_tensor(out=v3(nb), in0=v3(nb), in1=Rb(), op=ALU.subtract), ["nb", "R"], ["nb"])
                    P.dve(lambda e: e.tensor_tensor(out=dec[:], in0=mc[:, 0:NCH], in1=R[:], op=ALU.subtract),
                          ["mc", "R"], ["dec"])
                    P.dve(lambda e: e.tensor_scalar(out=gI[:], in0=gI[:], scalar1=LN_DK, scalar2=None, op0=ALU.add),
                          ["gI"], ["gI"])
                    P.act(lambda e: e.activation(out=gI[:], in_=gI[:], func=AF.Exp), ["gI"], ["gI"])
                    P.act(lambda e: e.activation(out=nb[:], in_=nb[:], func=AF.Exp), ["nb"], ["nb"])
                    P.act(lambda e: e.activation(out=dec[:], in_=dec[:], func=AF.Exp), ["dec"], ["dec"])
                    for t4 in range(0, NT, 4):
                        s_ = (t4 // 4) % 2
                        nt4 = min(4, NT - t4)
                        for j in range(nt4):
                            t = t4 + j
                            P.pe(lambda e, s_=s_, j=j, t=t: e.transpose(out=pG[s_][:, j, 0, :],
                                                                        in_=gI[:, t * 128:(t + 1) * 128],
                                                                        identity=id32[0:64, 0:64]),
                                 ["gI", "id32"], ["pG%d" % s_])
                            P.pe(lambda e, s_=s_, j=j, t=t: e.transpose(out=pG[s_][:, j, 1, :],
                                                                        in_=nb[:, t * 128:(t + 1) * 128],
                                                                        identity=id32[0:64, 0:64]),
                                 ["nb", "id32"], ["pG%d" % s_])
                        for d in range(2):
                            P.dve(lambda e, s_=s_, t4=t4, nt4=nt4, d=d: e.tensor_copy(
                                out=wkT[:, t4:t4 + nt4, d * 4:d * 4 + 4], in_=pG[s_][:, 0:nt4, 0, d * 32:d * 32 + 4]),
                                ["pG%d" % s_], ["wkT"])
                            P.dve(lambda e, s_=s_, t4=t4, nt4=nt4, d=d: e.tensor_copy(
                                out=thT[:, t4:t4 + nt4, d * 4:d * 4 + 4], in_=pG[s_][:, 0:nt4, 1, d * 32:d * 32 + 4]),
                                ["pG%d" % s_], ["thT"])
                    P.pool(lambda e: e.memset(wkH[:], 0.0), [], ["wkH"])
                    P.dve(lambda e: e.tensor_copy(out=wkH[0:64, :, 0, :], in_=wkT[0:64, :, :]), ["wkT", "wkH"], ["wkH"])
                    P.dve(lambda e: e.tensor_copy(out=wkH[64:128, :, 1, :], in_=wkT[64:128, :, :]), ["wkT", "wkH"], ["wkH"])
                    for d in range(2):
                        for hh in range(4):
                            P.pe(lambda e, d=d, hh=hh: e.matmul(pD[d][:, hh, :], lhsT=sel[:, d * 4 + hh, :], rhs=dec[:],
                                                                start=True, stop=True), ["sel", "dec"], ["pD%d" % d])
                        P.dve(lambda e, d=d: e.tensor_copy(out=decB[:, d * 4:d * 4 + 4, :], in_=pD[d][:]),
                              ["pD%d" % d], ["decB"])
                    P.barrier()

                for h in range(4):
                    with Scope(nc) as S:
                        qr = S.sb("qr", [128, T], BF16)
                        kr = S.sb("kr", [128, T], BF16)
                        acc = S.sb("acc", [128, T], F32)
                        qTm = S.sb("qTm", [128, T], BF16)
                        kTm = S.sb("kTm", [128, T], BF16)
                        qTp = S.sb("qTp", [128, NT, 2, 128], BF16)
                        ktm = S.sb("ktm", [128, NT, 128], BF16)
                        vau = S.sb("vau", [128, NT, 129], BF16)
                        otm = S.sb("otm", [128, NT, 128], BF16)
                        hraw = S.sb("hraw", [128, NT, 2, 129], F32)
                        dnv = S.sb("dnv", [128, 2, NT], F32)
                        C32 = [[S.sb("C32_%d_%d" % (i, p_), [128, 129], F32) for p_ in range(2)] for i in range(2)]
                        Cbf = [[S.sb("Cbf_%d_%d" % (i, p_), [128, 129], BF16) for p_ in range(2)] for i in range(2)]
                        ptl = [[S.sb("ptl%d_%d" % (d, i), [128, 128], BF16) for i in range(2)] for d in range(2)]
                        kp = [[S.sb("kp%d_%d" % (d, i), [128, 128], BF16) for i in range(4)] for d in range(2)]
                        dn = [S.sb("dn%d" % d, [128, 1], F32) for d in range(2)]
                        sgm = S.sb("sgm", [128, 128], F32)
                        junk = S.sb("junk", [128, 128], F32)
                        s1 = S.sb("s1", [128, NT], F32)
                        s2 = S.sb("s2", [128, NT], F32)
                        mu = S.sb("mu", [128, NT], F32)
                        rsd = S.sb("rsd", [128, NT], F32)
                        hn = [S.sb("hn%d" % i, [128, 128], BF16) for i in range(2)]
                        mho = S.sb("mho", [128, T], BF16)
                        pKT = S.ps("pKT", [128, 8, 128], BF16)
                        pSmt = [S.ps("pSm%d" % d, [128, 128]) for d in range(2)]
                        pSm = [pSmt[d][:, :] for d in range(2)]
                        pIN = [S.ps("pIN%d" % d, [128, 129]) for d in range(2)]
                        pKVt = [S.ps("pKV%d" % d, [128, 129]) for d in range(2)]
                        pKV = [[pKVt[d][:, :] for i in range(2)] for d in range(2)]
                        pHT = pKT

                        P.dma("sp", qr[:], projT[(CQKM + h) * 128:(CQKM + 1 + h) * 128, :], r=["projT"], w=["qr"])
                        P.dma("sp", kr[:], projT[(CQKM + 4 + h) * 128:(CQKM + 5 + h) * 128, :], r=["projT"], w=["kr"])
                        P.dma("sp", mho[:], projM[1 + h, :, :], r=["projM"], w=["mho"])
                        P.dve(lambda e: e.tensor_copy(out=vau[:, :, 0:128], in_=mho[:].rearrange("p (t d) -> p t d", d=128)),
                              ["mho"], ["vau"])
                        P.pool(lambda e: e.memset(vau[:, :, 128:129], 1.0), [], ["vau"])
                        P.dma("sp", otm[:].rearrange("p t d -> p (t d)"), projM[5 + h, :, :], r=["projM"], w=["otm"])
                        P.pool(lambda e: e.memset(qTp[:], 0.0), [], ["qTp"])
                        hsum = acc[:].rearrange("p (t d) -> p t d", d=128)
                        for d in range(2):
                            P.pool(lambda e, d=d: e.memset(C32[d][0][:], 0.0), [], ["C32_%d_0" % d])
                            P.pool(lambda e, d=d: e.memset(Cbf[d][0][:], 0.0), [], ["Cbf_%d_0" % d])
                        for (src, sk_, dst, dk_, wc) in ((qr, "qr", qTm, "qTm", h), (kr, "kr", kTm, "kTm", 4 + h)):
                            P.dve(lambda e, src=src, wc=wc: e.tensor_scalar(out=acc[:], in0=src[:], scalar1=wmv[:, wc, 1:2],
                                                                            scalar2=None, op0=ALU.mult),
                                  [sk_, "wmv"], ["acc"])
                            for (a, b) in ((0, SEQ), (SEQ, T)):
                                P.dve(lambda e, src=src, wc=wc, a=a, b=b: e.scalar_tensor_tensor(
                                    out=acc[:, a + 1:b], in0=src[:, a:b - 1], scalar=wmv[:, wc, 0:1], in1=acc[:, a + 1:b],
                                    op0=ALU.mult, op1=ALU.add), [sk_, "wmv", "acc"], ["acc"])
                                P.dve(lambda e, src=src, wc=wc, a=a, b=b: e.scalar_tensor_tensor(
                                    out=acc[:, a:b - 1], in0=src[:, a + 1:b], scalar=wmv[:, wc, 2:3], in1=acc[:, a:b - 1],
                                    op0=ALU.mult, op1=ALU.add), [sk_, "wmv", "acc"], ["acc"])
                            P.act(lambda e, dst=dst: e.activation(out=dst[:], in_=acc[:], func=AF.Silu), ["acc"], [dk_])
                        q4 = qTm[:].rearrange("p (t two l) -> p t two l", two=2, l=64)
                        P.dve(lambda e: e.tensor_copy(out=qTp[:, :, 0, 0:64], in_=q4[:, :, 0, :]), ["qTm"], ["qTp"])
                        P.dve(lambda e: e.tensor_copy(out=qTp[:, :, 1, 64:128], in_=q4[:, :, 1, :]), ["qTm"], ["qTp"])
                        for t8 in range(0, NT, 8):
                            n8 = min(8, NT - t8)
                            for j in range(n8):
                                P.pe(lambda e, j=j, t=t8 + j: e.transpose(out=pKT[:, j, :], in_=kTm[:, t * 128:(t + 1) * 128],
                                                                          identity=idb[:]), ["kTm", "idb"], ["pKT"])
                            P.act(lambda e, t8=t8, n8=n8: e.copy(out=ktm[:, t8:t8 + n8, :], in_=pKT[:, 0:n8, :]),
                                  ["pKT"], ["ktm"])
                        done_tiles = [set(), set()]
                        slot = [0, 0]
                        ptl_of = [{}, {}]
                        ORD = (ORDER_F, ORDER_B)

                        def geom(j, d):
                            c = ORD[d][j]
                            return c, c // 2, (c % 2) * 64, d * 4 + h

                        def st_kp(j, d):
                            c, t, r0, col = geom(j, d)
                            ks = j % 4
                            P.act(lambda e: e.activation(out=kp[d][ks][:, :], in_=ktm[:, t, :], func=AF.Copy,
                                                         scale=wkH[:, t, r0 // 64, col:col + 1]),
                                  ["ktm", "wkH"], ["kp%d_%d" % (d, ks)])

                        def st_pt(j, d):
                            c, t, r0, col = geom(j, d)
                            if t in done_tiles[d]:
                                return
                            done_tiles[d].add(t)
                            sl = slot[d] % 2
                            slot[d] += 1
                            ptl_of[d][t] = sl
                            P.pe(lambda e: e.matmul(pSm[d], lhsT=kTm[:, t * 128:(t + 1) * 128],
                                                    rhs=qTm[:, t * 128:(t + 1) * 128], start=True, stop=True),
                                 ["kTm", "qTm"], ["pSm%d" % d])
                            P.dve(lambda e: e.scalar_tensor_tensor(out=ptl[d][sl][:], in0=pSm[d],
                                                                   scalar=wkT[:, t, col:col + 1], in1=bmk[:, d, :],
                                                                   op0=ALU.mult, op1=ALU.mult),
                                  ["pSm%d" % d, "wkT", "bmk"], ["ptl%d_%d" % (d, sl)])

                        def st_kv(j, d):
                            c, t, r0, col = geom(j, d)
                            ks = j % 4
                            kv = 0
                            P.pe(lambda e: e.matmul(pKV[d][kv], lhsT=kp[d][ks][:, :], rhs=vau[:, t, :],
                                                    start=True, stop=True),
                                 ["kp%d_%d" % (d, ks), "vau"], ["pKV%d_%d" % (d, kv)])

                        def st_c(j, d):
                            c, t, r0, col = geom(j, d)
                            cur, nxt, kv = j % 2, (j + 1) % 2, 0
                            P.dve(lambda e: e.scalar_tensor_tensor(out=C32[d][nxt][:], in0=C32[d][cur][:],
                                                                   scalar=decB[:, col, c:c + 1], in1=pKV[d][kv],
                                                                   op0=ALU.mult, op1=ALU.add),
                                  ["C32_%d_%d" % (d, cur), "decB", "pKV%d_%d" % (d, kv)], ["C32_%d_%d" % (d, nxt)])
                            if j + 1 < NCH:
                                c2 = ORD[d][j + 1]
                                P.act(lambda e: e.activation(out=Cbf[d][nxt][:], in_=C32[d][nxt][:], func=AF.Copy,
                                                             scale=decB[:, col, c2:c2 + 1]),
                                      ["C32_%d_%d" % (d, nxt), "decB"], ["Cbf_%d_%d" % (d, nxt)])

                        def st_out(j, d):
                            c, t, r0, col = geom(j, d)
                            cur = j % 2
                            sl = ptl_of[d][t]
                            P.pe(lambda e: e.matmul(pIN[d][:], lhsT=qTp[:, t, r0 // 64, :], rhs=Cbf[d][cur][:],
                                                    start=True, stop=False),
                                 ["qTp", "Cbf_%d_%d" % (d, cur)], ["pIN%d" % d])
                            P.pe(lambda e: e.matmul(pIN[d][:], lhsT=ptl[d][sl][:, :], rhs=vau[:, t, :],
                                                    start=False, stop=True),
                                 ["ptl%d_%d" % (d, sl), "vau"], ["pIN%d" % d])
                            if d == 0:
                                P.act(lambda e: e.copy(out=hraw[r0:r0 + 64, t, d, :], in_=pIN[d][r0:r0 + 64, :]),
                                      ["pIN%d" % d], ["hraw%d" % d])
                            else:
                                P.dve(lambda e: e.tensor_copy(out=hraw[r0:r0 + 64, t, d, :], in_=pIN[d][r0:r0 + 64, :]),
                                      ["pIN%d" % d], ["hraw%d" % d])

                        for d in range(2):
                            st_kp(0, d)
                            st_kp(1, d)
                            st_pt(0, d)
                            st_pt(1, d)
                            st_kv(0, d)
                        for j in range(NCH):
                            for d in range(2):
                                if j + 2 < NCH:
                                    st_kp(j + 2, d)
                                    st_pt(j + 2, d)
                                st_c(j, d)
                                if j + 1 < NCH:
                                    st_kv(j + 1, d)
                                st_out(j, d)
                        for d in range(2):
                            col = d * 4 + h
                            P.act(lambda e, d=d: e.activation(out=dnv[:, d, :], in_=hraw[:, :, d, 128], func=AF.Abs),
                                  ["hraw%d" % d], ["dnv%d" % d])
                            P.dve(lambda e, d=d, col=col: e.tensor_tensor(out=dnv[:, d, :], in0=dnv[:, d, :],
                                                                          in1=thT[:, :, col], op=ALU.max),
                                  ["dnv%d" % d, "thT"], ["dnv%d" % d])
                            P.dve(lambda e, d=d: e.reciprocal(out=dnv[:, d, :], in_=dnv[:, d, :]), ["dnv%d" % d], ["dnv%d" % d])
                        P.dve(lambda e: e.tensor_tensor(out=hsum[:], in0=hraw[:, :, 0, 0:128],
                                                        in1=dnv[:, 0, :].unsqueeze(2).broadcast_to([128, NT, 128]),
                                                        op=ALU.mult), ["hraw0", "dnv0"], ["hsum", "acc"])
                        P.dve(lambda e: e.tensor_tensor(out=hraw[:, :, 1, 0:128], in0=hraw[:, :, 1, 0:128],
                                                        in1=dnv[:, 1, :].unsqueeze(2).broadcast_to([128, NT, 128]),
                                                        op=ALU.mult), ["hraw1", "dnv1"], ["hraw1"])
                        P.pool(lambda e: e.tensor_tensor(out=hsum[:], in0=hsum[:], in1=hraw[:, :, 1, 0:128], op=ALU.add),
                               ["hsum", "hraw1"], ["hsum"])
                        ntr = NT if not last else 32
                        P.dve(lambda e: e.memset(s1[:], 0.0), [], ["s1"])
                        P.dve(lambda e: e.memset(s2[:], 0.0), [], ["s2"])
                        for t in range(ntr):
                            P.act(lambda e, t=t: e.activation(out=sgm[:], in_=otm[:, t, :], func=AF.Sigmoid),
                                  ["otm"], ["sgm"])
                            P.dve(lambda e, t=t: e.tensor_tensor(out=hsum[:, t, :], in0=hsum[:, t, :], in1=sgm[:],
                                                                 op=ALU.mult), ["sgm", "hsum", "acc"], ["hg_%d" % t])
                            P.act(lambda e, t=t: e.activation(out=junk[:], in_=hsum[:, t, :], func=AF.Copy,
                                                              accum_out=s1[:, t:t + 1]), ["hg_%d" % t, "s1"],
                                  ["junk", "s1"])
                            P.act(lambda e, t=t: e.activation(out=junk[:], in_=hsum[:, t, :], func=AF.Square,
                                                              accum_out=s2[:, t:t + 1]), ["hg_%d" % t, "s2"],
                                  ["junk", "s2"])
                        P.dve(lambda e: e.tensor_scalar(out=mu[:], in0=s1[:], scalar1=1.0 / 128, scalar2=None, op0=ALU.mult),
                              ["s1"], ["mu"])
                        P.dve(lambda e: e.tensor_tensor(out=rsd[:], in0=mu[:], in1=mu[:], op=ALU.mult), ["mu"], ["rsd"])
                        P.dve(lambda e: e.scalar_tensor_tensor(out=rsd[:], in0=s2[:], scalar=1.0 / 128, in1=rsd[:],
                                                               op0=ALU.mult, op1=ALU.subtract), ["s2", "rsd"], ["rsd"])
                        P.act(lambda e: e.activation(out=rsd[:], in_=rsd[:], func=AF.Sqrt, bias=EPS, scale=1.0),
                              ["rsd"], ["rsd"])
                        P.dve(lambda e: e.reciprocal(out=rsd[:], in_=rsd[:]), ["rsd"], ["rsd"])
                        for t8 in range(0, ntr, 8):
                            n8 = min(8, ntr - t8)
                            for j in range(n8):
                                t = t8 + j
                                s_ = t % 2
                                P.dve(lambda e, t=t, s_=s_: e.tensor_scalar(out=hn[s_][:], in0=hsum[:, t, :],
                                                                            scalar1=mu[:, t:t + 1], scalar2=rsd[:, t:t + 1],
                                                                            op0=ALU.subtract, op1=ALU.mult),
                                      ["hg_%d" % t, "mu", "rsd"], ["hn%d" % s_])
                                P.pe(lambda e, j=j, s_=s_: e.transpose(out=pHT[:, j, :], in_=hn[s_][:], identity=idb[:]),
                                     ["hn%d" % s_, "idb"], ["pKT"])
                            P.act(lambda e, t8=t8, n8=n8: e.activation(
                                out=mho[:, t8 * 128:(t8 + n8) * 128].rearrange("p (a b) -> p a b", b=128),
                                in_=pHT[:, 0:n8, :], func=AF.Copy, scale=gmn[:, h:h + 1]), ["pKT", "gmn"], ["mho"])
                        P.dma("pool", mhT[h * 128:(h + 1) * 128, 0:ntr * 128], mho[:, 0:ntr * 128], r=["mho"], w=["mhT"])
                        P.barrier()

            phase(l, 5)
            with Scope(nc) as S:
                wA = S.sb("wA", [128, 4, D], BF16)
                wB = S.sb("wB", [128, 4, D], BF16)
                wC = S.sb("wC", [128, 4, D], BF16)
                wO = S.sb("wO", [128, 8, D], BF16)
                gt = S.sb("gt", [128, D], F32)
                aG = [S.sb("aG%d" % i, [128, 4, 512], BF16) for i in range(2)]
                bG = [S.sb("bG%d" % i, [128, 4, 512], BF16) for i in range(2)]
                cG = [S.sb("cG%d" % i, [128, 4, 512], BF16) for i in range(2)]
                br = [S.sb("br%d" % i, [128, 24, 512], BF16) for i in range(2)]
                m1 = [S.sb("m1_%d" % i, [128, 512], F32) for i in range(2)]
                m2_ = [S.sb("m2_%d" % i, [128, 512], F32) for i in range(2)]
                m3 = [S.sb("m3_%d" % i, [128, 512], F32) for i in range(2)]
                mg = S.sb("mg", [128, 8, 512], BF16)
                xt = [S.sb("xt%d" % i, [128, D], F32) for i in range(2)]
                ty = [S.sb("ty%d" % i, [128, 512], F32) for i in range(2)]
                pA = [S.ps("pA%d" % i, [128, 512]) for i in range(2)]
                pB = [S.ps("pB%d" % i, [128, 512]) for i in range(2)]
                pC = [S.ps("pC%d" % i, [128, 512]) for i in range(2)]
                pY = [S.ps("pY%d" % i, [128, 512]) for i in range(2)]
                P.dma("pool", wA[:], w_ao[l, :, :].rearrange("(c p) n -> p c n", p=128), w=["wA"])
                P.dma("pool", wB[:], w_pw[l, :, :].rearrange("(c p) n -> p c n", p=128), w=["wB"])
                P.dma("pool", wC[:], w_mo[l, :, :].rearrange("(c p) n -> p c n", p=128), w=["wC"])
                P.dma("pool", wO[:], w_out[l, :, :].rearrange("(c p) n -> p c n", p=128), w=["wO"])
                groups = GROUPS512 if not last else GROUPS512[:8]
                cur_stream = None
                ix = 0
                iy = 0
                for gi, (t0, n) in enumerate(groups):
                    stream = 0 if t0 < SEQ else 1
                    if stream != cur_stream:
                        cur_stream = stream
                        load_mod("sp", gt, l, stream, 2, "gt")
                    s_ = gi % 2
                    P.dma("sp", aG[s_][:, :, 0:n], attT[:, t0:t0 + n].rearrange("(c p) n -> p c n", p=128),
                          r=["attT"], w=["aG%d" % s_])
                    P.dma("sp", bG[s_][:, :, 0:n], cbT[:, t0:t0 + n].rearrange("(c p) n -> p c n", p=128),
                          r=["cbT"], w=["bG%d" % s_])
                    P.dma("sp", cG[s_][:, :, 0:n], mhT[:, t0:t0 + n].rearrange("(c p) n -> p c n", p=128),
                          r=["mhT"], w=["cG%d" % s_])
                    P.dma("sp", br[s_][:, :, 0:n],
                          projT[CBR * 128:CGATE * 128, t0:t0 + n].rearrange("(c p) n -> p c n", p=128),
                          r=["projT"], w=["br%d" % s_])
                    for j in range(8):
                        u = j % 2
                        for (pp, pn, ww, wn, src, sn) in ((pA, "pA", wA, "wA", aG, "aG"), (pB, "pB", wB, "wB", bG, "bG"),
                                                          (pC, "pC", wC, "wC", cG, "cG")):
                            for k in range(4):
                                P.pe(lambda e, pp=pp, ww=ww, src=src, u=u, k=k, j=j, s_=s_, n=n: e.matmul(
                                    pp[u][:, 0:n], lhsT=ww[:, k, j * 128:(j + 1) * 128], rhs=src[s_][:, k, 0:n],
                                    start=(k == 0), stop=(k == 3)), [wn, "%s%d" % (sn, s_)], ["%s%d" % (pn, u)])
                        brk = "br%d" % s_
                        P.dve(lambda e, u=u, s_=s_, j=j, n=n: e.tensor_tensor(out=m1[u][:, 0:n], in0=pA[u][:, 0:n],
                                                                              in1=br[s_][:, j, 0:n], op=ALU.mult),
                              ["pA%d" % u, brk], ["m1_%d" % u])
                        P.dve(lambda e, u=u, s_=s_, j=j, n=n: e.tensor_tensor(out=m2_[u][:, 0:n], in0=pB[u][:, 0:n],
                                                                              in1=br[s_][:, 8 + j, 0:n], op=ALU.mult),
                              ["pB%d" % u, brk], ["m2_%d" % u])
                        P.dve(lambda e, u=u, s_=s_, j=j, n=n: e.tensor_tensor(out=m3[u][:, 0:n], in0=pC[u][:, 0:n],
                                                                              in1=br[s_][:, 16 + j, 0:n], op=ALU.mult),
                              ["pC%d" % u, brk], ["m3_%d" % u])
                        P.pool(lambda e, u=u, n=n: e.tensor_tensor(out=m1[u][:, 0:n], in0=m1[u][:, 0:n], in1=m2_[u][:, 0:n],
                                                                   op=ALU.add), ["m1_%d" % u, "m2_%d" % u], ["m1_%d" % u])
                        P.pool(lambda e, u=u, j=j, n=n: e.tensor_tensor(out=mg[:, j, 0:n], in0=m1[u][:, 0:n],
                                                                        in1=m3[u][:, 0:n], op=ALU.add),
                               ["m1_%d" % u, "m3_%d" % u], ["mg%d" % j])
                    mgk = ["mg%d" % j for j in range(8)]
                    for jt in range(n // 128):
                        sx = ix % 2
                        ix += 1
                        xk = "xt%d" % sx
                        P.dma("sp", xt[sx][:], xs[t0 + jt * 128:t0 + (jt + 1) * 128, :], r=["xs"], w=[xk])
                        for cg in range(2):
                            sy = iy % 2
                            iy += 1
                            for k in range(8):
                                P.pe(lambda e, sy=sy, k=k, jt=jt, cg=cg: e.matmul(
                                    pY[sy][:], lhsT=mg[:, k, jt * 128:(jt + 1) * 128], rhs=wO[:, k, cg * 512:(cg + 1) * 512],
                                    start=(k == 0), stop=(k == 7)), [mgk[k], "wO"], ["pY%d" % sy])
                            P.dve(lambda e, sy=sy, cg=cg: e.tensor_tensor(out=ty[sy][:], in0=pY[sy][:],
                                                                          in1=gt[:, cg * 512:(cg + 1) * 512], op=ALU.mult),
                                  ["pY%d" % sy, "gt"], ["ty%d" % sy])
                            P.pool(lambda e, sy=sy, sx=sx, cg=cg: e.tensor_tensor(
                                out=xt[sx][:, cg * 512:(cg + 1) * 512], in0=xt[sx][:, cg * 512:(cg + 1) * 512],
                                in1=ty[sy][:], op=ALU.add), ["ty%d" % sy, xk], [xk])
                        P.dma("pool", xs[t0 + jt * 128:t0 + (jt + 1) * 128, :], xt[sx][:], r=[xk], w=["xs"])
                P.barrier()

            phase(l, 6)
            with Scope(nc) as S:
                wG = S.sb("wG", [128, 8, DFF], BF16)
                wU = S.sb("wU", [128, 8, DFF], BF16)
                wD = S.sb("wD", [128, 22, D], BF16)
                Gb = S.sb("Gb", [128, D], F32)
                Sb = S.sb("Sb", [128, D], F32)
                gt = S.sb("gt", [128, D], F32)
                gfin = S.sb("gfin", [128, D], F32)
                xg = S.sb("xg", [128, 2, D], F32)
                tmpf = S.sb("tmpf", [128, D], F32)
                hb = S.sb("hb", [128, D], BF16)
                hT = S.sb("hT", [128, 8, 256], BF16)
                ss = S.sb("ss", [128, 2], F32)
                rstd = S.sb("rstd", [128, 2], F32)
                aT = S.sb("aT", [128, 22, 256], BF16)
                sgl = [S.sb("sgl%d" % i, [128, 256], F32) for i in range(2)]
                ty = [S.sb("ty%d" % i, [128, 512], F32) for i in range(2)]
                pT = S.ps("pT", [128, 8, 128], BF16)
                pGa = [S.ps("pGa%d" % i, [128, 256]) for i in range(2)]
                pUa = [S.ps("pUa%d" % i, [128, 256]) for i in range(2)]
                pDn = [S.ps("pDn%d" % i, [128, 512]) for i in range(2)]
                for k in range(8):
                    P.dma("pool", wG[:, k, :], w_g[l, k * 128:(k + 1) * 128, :], w=["wG"])
                    P.dma("pool", wU[:, k, :], w_u[l, k * 128:(k + 1) * 128, :], w=["wU"])
                for k in range(0, 22, 2):
                    P.dma("pool", wD[:, k:k + 2, :], w_d[l, k * 128:(k + 2) * 128, :].rearrange("(c p) n -> p c n", p=128),
                          w=["wD"])
                groups = GROUPS256 if not last else GROUPS256[:16]
                cur_stream = None
                iu = 0
                iy = 0
                for gi, (t0, n) in enumerate(groups):
                    stream = 0 if t0 < SEQ else 1
                    if stream != cur_stream:
                        cur_stream = stream
                        load_mod("sp", Sb, l, stream, 3, "Sb")
                        load_mod("sp", tmpf, l, stream, 4, "tmpf")
                        P.dma("sp", Gb[:], g_ffn[l, :].partition_broadcast(128), w=["Gb"])
                        P.dve(lambda e: e.scalar_tensor_tensor(out=Gb[:], in0=tmpf[:], scalar=1.0, in1=Gb[:],
                                                               op0=ALU.add, op1=ALU.mult), ["tmpf", "Gb"], ["Gb"])
                        load_mod("sp", gt, l, stream, 5, "gt")
                    P.dve(lambda e: e.memset(ss[:], 0.0), [], ["ss"])
                    for j in range(2):
                        P.dma("sp", xg[:, j, :], xs[t0 + j * 128:t0 + (j + 1) * 128, :], r=["xs"], w=["xg%d" % j])
                        P.act(lambda e, j=j: e.activation(out=tmpf[:], in_=xg[:, j, :], func=AF.Square, scale=1.0 / 32,
                                                          accum_out=ss[:, j:j + 1]), ["xg%d" % j, "ss"], ["tmpf", "ss"])
                    P.act(lambda e: e.activation(out=rstd[:], in_=ss[:], func=AF.Sqrt, bias=EPS, scale=1.0),
                          ["ss"], ["rstd"])
                    P.dve(lambda e: e.reciprocal(out=rstd[:], in_=rstd[:]), ["rstd"], ["rstd"])
                    for j in range(2):
                        P.dve(lambda e, j=j: e.scalar_tensor_tensor(out=tmpf[:], in0=xg[:, j, :], scalar=rstd[:, j:j + 1],
                                                                    in1=Gb[:], op0=ALU.mult, op1=ALU.mult),
                              ["xg%d" % j, "rstd", "Gb"], ["tmpf"])
                        P.dve(lambda e: e.tensor_tensor(out=hb[:], in0=tmpf[:], in1=Sb[:], op=ALU.add),
                              ["tmpf", "Sb"], ["hb"])
                        for c in range(8):
                            P.pe(lambda e, c=c: e.transpose(out=pT[:, c, :], in_=hb[:, c * 128:(c + 1) * 128],
                                                            identity=idb[:]), ["hb", "idb"], ["pT"])
                        P.act(lambda e, j=j: e.copy(out=hT[:, :, j * 128:(j + 1) * 128], in_=pT[:]), ["pT"], ["hT"])
                    for f in range(22):
                        u = iu % 2
                        iu += 1
                        for k in range(8):
                            P.pe(lambda e, u=u, k=k, f=f: e.matmul(pGa[u][:], lhsT=wG[:, k, f * 128:(f + 1) * 128],
                                                                   rhs=hT[:, k, :], start=(k == 0), stop=(k == 7)),
                                 ["wG", "hT"], ["pGa%d" % u])
                        for k in range(8):
                            P.pe(lambda e, u=u, k=k, f=f: e.matmul(pUa[u][:], lhsT=wU[:, k, f * 128:(f + 1) * 128],
                                                                   rhs=hT[:, k, :], start=(k == 0), stop=(k == 7)),
                                 ["wU", "hT"], ["pUa%d" % u])
                        P.act(lambda e, u=u: e.activation(out=sgl[u][:], in_=pGa[u][:], func=AF.Silu),
                              ["pGa%d" % u], ["sgl%d" % u])
                        P.dve(lambda e, u=u, f=f: e.tensor_tensor(out=aT[:, f, :], in0=pUa[u][:], in1=sgl[u][:],
                                                                  op=ALU.mult), ["pUa%d" % u, "sgl%d" % u], ["aT%d" % f])
                    atk = ["aT%d" % f for f in range(22)]
                    for j in range(2):
                        xk = "xg%d" % j
                        for cg in range(2):
                            sy = iy % 2
                            iy += 1
                            for f in range(22):
                                P.pe(lambda e, sy=sy, f=f, j=j, cg=cg: e.matmul(
                                    pDn[sy][:], lhsT=aT[:, f, j * 128:(j + 1) * 128], rhs=wD[:, f, cg * 512:(cg + 1) * 512],
                                    start=(f == 0), stop=(f == 21)), [atk[f], "wD"], ["pDn%d" % sy])
                            P.dve(lambda e, sy=sy, cg=cg: e.tensor_tensor(out=ty[sy][:], in0=pDn[sy][:],
                                                                          in1=gt[:, cg * 512:(cg + 1) * 512], op=ALU.mult),
                                  ["pDn%d" % sy, "gt"], ["ty%d" % sy])
                            P.pool(lambda e, sy=sy, j=j, cg=cg: e.tensor_tensor(
                                out=xg[:, j, cg * 512:(cg + 1) * 512], in0=xg[:, j, cg * 512:(cg + 1) * 512],
                                in1=ty[sy][:], op=ALU.add), ["ty%d" % sy, xk], [xk])
                        if not last:
                            P.dma("pool", xs[t0 + j * 128:t0 + (j + 1) * 128, :], xg[:, j, :], r=[xk], w=["xs"])
                        else:
                            if gi == 0 and j == 0:
                                P.dma("sp", gfin[:], g_fin.partition_broadcast(128), r=[], w=["gfin"])
                            P.dve(lambda e: e.memset(ss[:, 0:1], 0.0), [], ["ss"])
                            P.act(lambda e, j=j: e.activation(out=tmpf[:], in_=xg[:, j, :], func=AF.Square, scale=1.0 / 32,
                                                              accum_out=ss[:, 0:1]), [xk, "ss"], ["tmpf", "ss"])
                            P.act(lambda e: e.activation(out=rstd[:, 0:1], in_=ss[:, 0:1], func=AF.Sqrt, bias=EPS,
                                                         scale=1.0), ["ss"], ["rstd"])
                            P.dve(lambda e: e.reciprocal(out=rstd[:, 0:1], in_=rstd[:, 0:1]), ["rstd"], ["rstd"])
                            P.dve(lambda e, j=j: e.scalar_tensor_tensor(out=xg[:, j, :], in0=xg[:, j, :],
                                                                        scalar=rstd[:, 0:1], in1=gfin[:], op0=ALU.mult,
                                                                        op1=ALU.mult), [xk, "rstd", "gfin"], [xk])
                            P.dma("pool", out[t0 + j * 128:t0 + (j + 1) * 128, :], xg[:, j, :], r=[xk], w=["out"])
                P.barrier()
        P.enabled = True
        if n_layers < DEPTH:
            with Scope(nc) as S:
                xo = S.sb("xo", [128, D], F32)
                for t in range(32):
                    P.dma("sp", xo[:], xs[t * 128:(t + 1) * 128, :], r=["xs"], w=["xo"])
                    P.dma("sp", out[t * 128:(t + 1) * 128, :], xo[:], r=["xo"], w=["out"])
                P.barrier()
        P.barrier()
        stats = P.emit()
    return nc, stats


def _consts():
    ident = np.eye(128, dtype=np.float32)
    rperm = np.zeros((128, 128), np.float32)
    sign = np.zeros(128, np.float32)
    for m in range(128):
        d = m % 64
        base = m - d
        hb = (d // 32) * 32
        dd = d % 32
        if dd < 16:
            pm, sg = hb + dd + 16, -1.0
        else:
            pm, sg = hb + dd - 16, 1.0
        rperm[base + pm, m] = 1.0
        sign[m] = sg
    t = np.arange(SEQ)
    row = (t // 64).astype(np.float32)
    col = (t % 64).astype(np.float32)
    inv = (10000.0 ** (-np.arange(16, dtype=np.float32) / 16)).astype(np.float32)
    ang_r = row[:, None] * inv[None, :]
    ang_c = col[:, None] * inv[None, :]
    ang = np.concatenate([ang_r, ang_r, ang_c, ang_c], axis=-1).astype(np.float32)
    cos = np.cos(ang).astype(np.float32).T
    sin = np.sin(ang).astype(np.float32).T
    cosT = np.ascontiguousarray(np.concatenate([cos, cos], axis=0))
    sinT = np.ascontiguousarray(np.concatenate([sin, sin], axis=0) * sign[:, None]).astype(np.float32)
    a = np.arange(128)
    mprev = (a[:, None] >= a[None, :]).astype(np.float32)
    mnext = (a[:, None] <= a[None, :]).astype(np.float32)
    same = (a[:, None] // 64) == (a[None, :] // 64)
    bmf = (same & (a[:, None] <= a[None, :])).astype(np.float32)
    bmb = (same & (a[:, None] >= a[None, :])).astype(np.float32)
    sel = np.zeros((64, 8, 128), np.float32)
    for d in range(2):
        for h in range(4):
            sel[d * 32 + h, d * 4 + h, :] = 1.0
    return dict(ident=ident, rperm=rperm, cosT=cosT, sinT=sinT, mprev=mprev, mnext=mnext, bmf=bmf, bmb=bmb, sel=sel)


def _prep(inp):
    f = lambda a: np.ascontiguousarray(np.asarray(a, dtype=np.float32))
    w_in = f(inp["w_in"])
    L = w_in.shape[0]
    qa = w_in[:, :, 0:512]
    ka = w_in[:, :, 512:640]
    va = w_in[:, :, 640:768]
    glu = w_in[:, :, 768:1792]
    qkm = w_in[:, :, 1792:2816]
    vm = w_in[:, :, 2816:3328]
    om = w_in[:, :, 3328:3840]
    gm = w_in[:, :, 3840:3856]
    br = w_in[:, :, 3856:6928]
    gch = np.zeros((L, D, 128), np.float32)
    gch[:, :, 0:4] = gm[:, :, 0:4]
    gch[:, :, 32:36] = gm[:, :, 4:8]
    gch[:, :, 64:68] = gm[:, :, 8:12]
    gch[:, :, 96:100] = gm[:, :, 12:16]
    z64 = np.zeros((L, D, 64), np.float32)
    w_fm = np.concatenate([qa, ka[:, :, 0:64], z64, z64, ka[:, :, 0:64], ka[:, :, 64:128], z64, z64, ka[:, :, 64:128],
                           glu, qkm, br, gch], axis=2)
    assert w_fm.shape[2] == NFM * 128
    w_tm = np.concatenate([va, vm, om], axis=2)
    bmg = f(inp["b_mgate"])
    bm = np.zeros((L, 64, 2), np.float32)
    bm[:, 0:4, 0] = bmg[:, 0:4]
    bm[:, 32:36, 0] = bmg[:, 4:8]
    bm[:, 0:4, 1] = bmg[:, 8:12]
    bm[:, 32:36, 1] = bmg[:, 12:16]
    cv = np.zeros((L, 512, 34), np.float32)
    cv[:, :, 0:31] = np.transpose(f(inp["w_conv_dw"]), (0, 2, 1))
    cv[:, :, 31] = f(inp["b_conv_dw"])
    cv[:, :, 32] = f(inp["g_conv_ln"])
    cv[:, :, 33] = f(inp["b_conv_ln"])
    cvec = np.ascontiguousarray(cv.reshape(L, 4, 128, 34).transpose(0, 2, 1, 3))
    wm = np.ascontiguousarray(np.transpose(f(inp["w_mconv"]), (0, 2, 1)).reshape(L, 8, 128, 3).transpose(0, 2, 1, 3))
    gmn = np.ascontiguousarray(f(inp["g_mlstm_norm"]).reshape(L, 4, 128).transpose(0, 2, 1))
    common = dict(
        w_ada=f(inp["w_ada"]), b_ada=f(inp["b_ada"]), g_norm_mix=f(inp["g_norm_mix"]), g_norm_ffn=f(inp["g_norm_ffn"]),
        w_fm=np.ascontiguousarray(w_fm), w_tm=np.ascontiguousarray(w_tm), bm=bm, att_sink=f(inp["att_sink"]),
        w_att_out=f(inp["w_att_out"]), cvec=cvec, w_conv_pw=f(inp["w_conv_pw"]), wm=wm, gmn=gmn,
        w_mlstm_out=f(inp["w_mlstm_out"]), w_out=f(inp["w_out"]), w_ff_gate=f(inp["w_ff_gate"]),
        w_ff_up=f(inp["w_ff_up"]), w_ff_down=f(inp["w_ff_down"]), g_final=f(inp["g_final"]))
    common.update(_consts())
    x = f(inp["x"])
    ctx = f(inp["ctx"])
    c = f(inp["c"])
    c_ctx = f(inp["c_ctx"])
    maps = []
    for core in range(8):
        b = core % 4
        cc = np.stack([c[b], c_ctx], axis=-1).reshape(8, 128, 2).transpose(1, 0, 2)
        m = dict(common)
        m["x"] = np.ascontiguousarray(x[b])
        m["ctx"] = np.ascontiguousarray(ctx[b])
        m["cc"] = np.ascontiguousarray(cc)
        maps.append(m)
    return maps


_NC_CACHE = {}


def kernel(**inputs):
    maps = _prep(inputs)
    if "nc" not in _NC_CACHE:
        _NC_CACHE["nc"] = build()[0]
    nc = _NC_CACHE["nc"]
    res = run_bass_kernel_spmd(nc, maps, core_ids=list(range(8)))
    outs = [np.asarray(res.results[b]["out"], dtype=np.float32) for b in range(4)]
    return np.stack(outs, axis=0)
```

```python
import contextlib
import os
import numpy as np
KD = int(os.environ.get('KD', '9'))
import concourse.bass as bass
import concourse.mybir as mybir
from concourse.bass_utils import run_bass_kernel_spmd

F32 = mybir.dt.float32
BF16 = mybir.dt.bfloat16
AF = mybir.ActivationFunctionType
ALU = mybir.AluOpType
AX = mybir.AxisListType

COMPUTE = ("pe", "act", "dve", "pool", "sp")
EPOCH = 6000
DMA_USES = 1800


class _Rec:
    def __getattr__(self, name):
        def f(*a, **k):
            self.call = (name, a, k)
            return self
        return f


class Prog:
    def __init__(self, nc, n_dma_sems=10):
        self.nc = nc
        self.ins = []
        self.per_eng = {e: [] for e in COMPUTE}
        self.last_writer = {}
        self.readers = {}
        self.nstreams = len(COMPUTE)
        self.stream_id = {e: i for i, e in enumerate(COMPUTE)}
        self.stream_pos = [0] * self.nstreams
        self.stream_last = {}
        self.ic = {e: [0] * self.nstreams for e in COMPUTE}
        self.vc_snap = []
        self.dma_pool = {}
        self.dma_rr = {}
        self.dma_last = {}
        self.dma_uses = {}
        self.n_dma_sems = n_dma_sems
        self.signal = set()

    def _new_stream(self):
        sid = self.nstreams
        self.nstreams += 1
        self.stream_pos.append(0)
        for e in COMPUTE:
            self.ic[e] = self.ic[e] + [0]
        return sid

    def _dma_stream(self, q):
        pool = self.dma_pool.setdefault(q, [])
        if len(pool) < self.n_dma_sems:
            sid = self._new_stream()
            pool.append(sid)
            self.dma_uses[sid] = 0
            self.dma_rr[q] = len(pool) - 1
            return sid
        k = (self.dma_rr[q] + 1) % len(pool)
        self.dma_rr[q] = k
        sid = pool[k]
        if self.dma_uses[sid] >= DMA_USES:
            sid = self._new_stream()
            pool[k] = sid
            self.dma_uses[sid] = 0
        return sid

    enabled = True

    def add(self, eng, fn, reads=(), writes=(), dma=False, force=False, extra=()):
        if not self.enabled:
            return None
        rec = _Rec()
        fn(rec)
        fn = (lambda call: (lambda e: getattr(e, call[0])(*call[1], **call[2])))(rec.call)
        iid = len(self.ins)
        deps = set(extra)
        for k in reads:
            w = self.last_writer.get(k)
            if w is not None:
                deps.add(w)
        for k in writes:
            w = self.last_writer.get(k)
            if w is not None:
                deps.add(w)
            for r in self.readers.get(k, ()):
                deps.add(r)
        if dma:
            sid = self._dma_stream(eng)
            prev = self.dma_last.get(sid)
            if prev is not None:
                deps.add(prev)
            self.dma_last[sid] = iid
            self.dma_uses[sid] += 1
        else:
            sid = self.stream_id[eng]
        self.stream_pos[sid] += 1
        pos = self.stream_pos[sid]
        self.stream_last[sid] = iid
        ic = self.ic[eng]
        waits = []
        rset = set(reads)
        for d in sorted(deps):
            deng, _, ddma, _, dsid, dpos = self.ins[d]
            if not ddma and deng == eng and not dma and not force:
                if eng == "pe":
                    continue
                raw = False
                for k in rset:
                    if self.last_writer.get(k) == d:
                        raw = True
                        break
                if not raw:
                    continue
            if len(ic) < self.nstreams:
                ic = ic + [0] * (self.nstreams - len(ic))
            if ic[dsid] >= dpos:
                continue
            waits.append(d)
            self.signal.add(d)
            snap, s2, p2 = self.vc_snap[d]
            if len(snap) < self.nstreams:
                snap = snap + [0] * (self.nstreams - len(snap))
            new = [a if a > b else b for a, b in zip(ic, snap)]
            if new[s2] < p2:
                new[s2] = p2
            ic = new
        self.ic[eng] = ic
        self.vc_snap.append((ic, sid, pos))
        self.ins.append((eng, fn, dma, waits, sid, pos))
        self.per_eng[eng].append(iid)
        for k in reads:
            self.readers.setdefault(k, []).append(iid)
        for k in writes:
            self.last_writer[k] = iid
            self.readers[k] = []
        return iid

    def pe(self, fn, r=(), w=()):
        return self.add("pe", fn, r, w)

    def act(self, fn, r=(), w=()):
        return self.add("act", fn, r, w)

    def dve(self, fn, r=(), w=()):
        return self.add("dve", fn, r, w)

    def pool(self, fn, r=(), w=()):
        return self.add("pool", fn, r, w)

    def dma(self, q, out, in_, r=(), w=(), **kw):
        return self.add(q, lambda e: e.dma_start(out=out, in_=in_, **kw), r, w, dma=True)

    def barrier(self):
        lasts = list(self.stream_last.values())
        for e in COMPUTE:
            self.add(e, lambda en: en.nop(), (), (), force=True, extra=lasts)

    def emit(self):
        nc = self.nc
        sig_count = {}
        stream_sig = [0] * self.nstreams
        for iid, (eng, fn, dma, waits, sid, pos) in enumerate(self.ins):
            if dma:
                sig_count[iid] = 16 * pos
            elif iid in self.signal:
                stream_sig[sid] += 1
                sig_count[iid] = stream_sig[sid]
        stack = contextlib.ExitStack()
        sems = {}

        def sem_for(sid, cnt):
            if sid < len(COMPUTE):
                ep = (cnt - 1) // EPOCH
                key = (sid, ep)
                val = cnt - ep * EPOCH
            else:
                key = (sid, 0)
                val = cnt
            if key not in sems:
                sems[key] = stack.enter_context(nc.semaphore("s%d_%d" % key))
            return sems[key], val

        for iid in sorted(sig_count):
            sem_for(self.ins[iid][4], sig_count[iid])
        with stack:
            with nc.Block() as block:
                def run(engname):
                    def body(e):
                        for iid in self.per_eng[engname]:
                            _, fn, dma, waits, sid, pos = self.ins[iid]
                            for d in waits:
                                s, v = sem_for(self.ins[d][4], sig_count[d])
                                e.wait_ge(s, v)
                            r = fn(e)
                            if iid in sig_count:
                                s, v = sem_for(sid, sig_count[iid])
                                r.then_inc(s, 16 if dma else 1)
                    return body
                block.sync(run("sp"))
                block.gpsimd(run("pool"))
                block.scalar(run("act"))
                block.vector(run("dve"))
                block.tensor(run("pe"))
        return len(self.ins), len(sems)


D = 1024
SEQ = 4096
CTX = 256
T = SEQ + CTX
NT = T // 128
NCH = T // 64
DEPTH = 4
DFF = 2816
NFM = 49
CK, CGLU, CQKM, CBR, CGATE = 4, 8, 16, 24, 48
NTM = 1152
LN_DK = float(np.log(128.0 ** -0.5))
EPS = 1e-6

GROUPS512 = [(g * 512, 512) for g in range(8)] + [(4096, 256)]
GROUPS256 = [(g * 256, 256) for g in range(17)]
ORDER_F = [64, 65, 66, 67] + list(range(64))
ORDER_B = [67, 66, 65, 64] + list(range(63, -1, -1))


class Scope:
    uid = 0

    def __init__(self, nc):
        self.nc = nc
        self.st = contextlib.ExitStack()
        self.n = 0

    def __enter__(self):
        self.st.__enter__()
        return self

    def __exit__(self, *a):
        return self.st.__exit__(*a)

    def sb(self, name, shape, dt):
        Scope.uid += 1
        return self.st.enter_context(self.nc.sbuf_tensor("%s_u%d" % (name, Scope.uid), list(shape), dt))

    def ps(self, name, shape, dt=F32):
        Scope.uid += 1
        return self.st.enter_context(self.nc.psum_tensor("%s_u%d" % (name, Scope.uid), list(shape), dt))


def build(n_layers=DEPTH, debug=False, stop=None):
    nc = bass.Bass("TRN2", target_bir_lowering=False)
    P = Prog(nc)

    def phase(l, k):
        P.enabled = stop is None or (l, k) <= stop

    def din(name, shape, dt=F32):
        return nc.dram_tensor(name, list(shape), dt, kind="ExternalInput").ap()

    def dscr(name, shape, dt):
        kind = "ExternalOutput" if debug else "Internal"
        return nc.dram_tensor(name, list(shape), dt, kind=kind).ap()

    x_in = din("x", [SEQ, D])
    ctx_in = din("ctx", [CTX, D])
    cc_in = din("cc", [128, 8, 2])
    w_ada = din("w_ada", [DEPTH, D, 6 * D])
    b_ada = din("b_ada", [DEPTH, 6 * D])
    g_mix = din("g_norm_mix", [DEPTH, D])
    g_ffn = din("g_norm_ffn", [DEPTH, D])
    w_fm = din("w_fm", [DEPTH, D, NFM * 128])
    w_tm = din("w_tm", [DEPTH, D, NTM])
    bm_in = din("bm", [DEPTH, 64, 2])
    sink_in = din("att_sink", [DEPTH, 8])
    w_ao = din("w_att_out", [DEPTH, 512, D])
    cvec_in = din("cvec", [DEPTH, 128, 4, 34])
    w_pw = din("w_conv_pw", [DEPTH, 512, D])
    wm_in = din("wm", [DEPTH, 128, 8, 3])
    gmn_in = din("gmn", [DEPTH, 128, 4])
    w_mo = din("w_mlstm_out", [DEPTH, 512, D])
    w_out = din("w_out", [DEPTH, D, D])
    w_g = din("w_ff_gate", [DEPTH, D, DFF])
    w_u = din("w_ff_up", [DEPTH, D, DFF])
    w_d = din("w_ff_down", [DEPTH, DFF, D])
    g_fin = din("g_final", [D])
    ident_in = din("ident", [128, 128])
    rperm_in = din("rperm", [128, 128])
    cos_in = din("cosT", [128, SEQ])
    sin_in = din("sinT", [128, SEQ])
    mprev_in = din("mprev", [128, 128])
    mnext_in = din("mnext", [128, 128])
    bmf_in = din("bmf", [128, 128])
    bmb_in = din("bmb", [128, 128])
    sel_in = din("sel", [64, 8, 128])
    out = nc.dram_tensor("out", [SEQ, D], F32, kind="ExternalOutput").ap()

    xs = dscr("xs", [T, D], F32)
    modd = dscr("modd", [DEPTH, 2, 6 * D], F32)
    projT = dscr("projT", [48 * 128, T], BF16)
    gatesT = dscr("gatesT", [128, T], F32)
    projM = dscr("projM", [9, 128, NT * 128], BF16)
    attT = dscr("attT", [512, T], BF16)
    cbT = dscr("cbT", [512, T], BF16)
    mhT = dscr("mhT", [512, T], BF16)

    with Scope(nc) as G:
        id32 = G.sb("id32", [128, 128], F32)
        idb = G.sb("idb", [128, 128], BF16)
        ones32 = G.sb("ones32", [128, 128], F32)
        P.dma("sp", id32[:], ident_in[:, :], w=["id32"])
        P.dve(lambda e: e.tensor_copy(out=idb[:], in_=id32[:]), ["id32"], ["idb"])
        P.dve(lambda e: e.memset(ones32[:], 1.0), [], ["ones32"])
        for i in range(16):
            P.dma("sp", xs[i * 256:(i + 1) * 256, :], x_in[i * 256:(i + 1) * 256, :], w=["xs_i%d" % i])
        P.dma("sp", xs[SEQ:T, :], ctx_in[:, :], w=["xs_c"])

        with Scope(nc) as S:
            cc = S.sb("cc", [128, 8, 2], F32)
            scc = S.sb("scc", [128, 8, 2], F32)
            wa = [S.sb("wa%d" % i, [128, 8, 512], F32) for i in range(2)]
            bad = S.sb("bad", [2, 6 * D], F32)
            mrow = S.sb("mrow", [2, 6 * D], F32)
            pm = [S.ps("pm%d" % i, [2, 512]) for i in range(2)]
            P.dma("sp", cc[:], cc_in[:, :, :], w=["cc"])
            P.act(lambda e: e.activation(out=scc[:], in_=cc[:], func=AF.Silu), ["cc"], ["scc"])
            it = 0
            for l in range(n_layers):
                P.dma("sp", bad[:], b_ada[l, :].partition_broadcast(2), w=["bad"])
                for cg in range(12):
                    s = it % 2
                    it += 1
                    P.dma("sp", wa[s][:], w_ada[l, :, cg * 512:(cg + 1) * 512].rearrange("(c p) n -> p c n", p=128),
                          w=["wa%d" % s])
                    for k in range(8):
                        P.pe(lambda e, s=s, k=k: e.matmul(pm[s][:], lhsT=scc[:, k, :], rhs=wa[s][:, k, :],
                                                         start=(k == 0), stop=(k == 7)),
                             ["scc", "wa%d" % s], ["pm%d" % s])
                    P.dve(lambda e, s=s, cg=cg: e.tensor_tensor(out=mrow[:, cg * 512:(cg + 1) * 512], in0=pm[s][:],
                                                               in1=bad[:, cg * 512:(cg + 1) * 512], op=ALU.add),
                          ["pm%d" % s, "bad"], ["mrow"])
                P.dma("pool", modd[l, :, :], mrow[:], r=["mrow"], w=["modd"])
            P.barrier()

        def load_mod(eng_q, tile, l, stream, k, key):
            P.dma(eng_q, tile[:], modd[l, stream, k * D:(k + 1) * D].partition_broadcast(128), r=["modd"], w=[key])

        def make_gain(gt, gkey, sct, sckey, gvec_dram, tmp, tmpkey):
            P.dma("sp", tmp[:], gvec_dram.partition_broadcast(128), w=[tmpkey])
            P.dve(lambda e: e.scalar_tensor_tensor(out=gt[:], in0=sct[:], scalar=1.0, in1=tmp[:],
                                                   op0=ALU.add, op1=ALU.mult),
                  [sckey, tmpkey], [gkey])

        for l in range(n_layers):
            last = (l == DEPTH - 1)
            phase(l, 1)
            with Scope(nc) as S:
                wfm = S.sb("wfm", [128, 8, NFM * 128], BF16)
                wtm = S.sb("wtm", [128, 8, NTM], BF16)
                cosT = S.sb("cosT", [128, SEQ], F32)
                sinT = S.sb("sinT", [128, SEQ], F32)
                rp32 = S.sb("rp32", [128, 128], F32)
                rpb = S.sb("rpb", [128, 128], BF16)
                Gb = S.sb("Gb", [128, D], F32)
                Sb = S.sb("Sb", [128, D], F32)
                xg = S.sb("xg", [128, 4, D], F32)
                tmpf = S.sb("tmpf", [128, D], F32)
                hb = S.sb("hb", [128, D], BF16)
                hT = S.sb("hT", [128, 8, 512], BF16)
                ss = S.sb("ss", [128, 4], F32)
                rstd = S.sb("rstd", [128, 4], F32)
                fo = [S.sb("fo%d" % i, [128, 512], BF16) for i in range(3)]
                qraw = [S.sb("qraw%d" % i, [128, 512], BF16) for i in range(2)]
                t1 = [S.sb("t1%d" % i, [128, 512], F32) for i in range(2)]
                t2 = [S.sb("t2%d" % i, [128, 512], F32) for i in range(2)]
                go = S.sb("go", [128, 512], F32)
                tmo = [S.sb("tmo%d" % i, [128, NTM], BF16) for i in range(2)]
                pT = S.ps("pT", [128, 8, 128], BF16)
                pTM = [S.ps("pTM%d" % i, [128, 512]) for i in range(2)]
                pFM = [S.ps("pFM%d" % i, [128, 512]) for i in range(3)]
                pR = S.ps("pR", [128, 512])

                for k in range(8):
                    P.dma("pool", wfm[:, k, :], w_fm[l, k * 128:(k + 1) * 128, :], w=["wfm%d" % k])
                P.dma("pool", wtm[:], w_tm[l, :, :].rearrange("(c p) n -> p c n", p=128), w=["wtm"])
                wfm_keys = ["wfm%d" % k for k in range(8)]
                P.dma("sp", cosT[:], cos_in[:, :], w=["cosT"])
                P.dma("sp", sinT[:], sin_in[:, :], w=["sinT"])
                P.dma("sp", rp32[:], rperm_in[:, :], w=["rp32"])
                P.dve(lambda e: e.tensor_copy(out=rpb[:], in_=rp32[:]), ["rp32"], ["rpb"])
                ifm = 0
                ifo = 0
                iq = 0
                itm = 0
                cur_stream = None
                for gi, (t0, n) in enumerate(GROUPS512):
                    stream = 0 if t0 < SEQ else 1
                    if stream != cur_stream:
                        cur_stream = stream
                        load_mod("sp", Sb, l, stream, 0, "Sb")
                        load_mod("sp", tmpf, l, stream, 1, "tmpf")
                        P.dma("sp", Gb[:], g_mix[l, :].partition_broadcast(128), w=["Gb"])
                        P.dve(lambda e: e.scalar_tensor_tensor(out=Gb[:], in0=tmpf[:], scalar=1.0, in1=Gb[:],
                                                               op0=ALU.add, op1=ALU.mult),
                              ["tmpf", "Gb"], ["Gb"])
                    ntl = n // 128
                    P.dve(lambda e: e.memset(ss[:], 0.0), [], ["ss"])
                    for j in range(ntl):
                        P.dma("sp", xg[:, j, :], xs[t0 + j * 128:t0 + (j + 1) * 128, :], r=["xs"], w=["xg%d" % j])
                        P.act(lambda e, j=j: e.activation(out=tmpf[:], in_=xg[:, j, :], func=AF.Square, scale=1.0 / 32,
                                                          accum_out=ss[:, j:j + 1]),
                              ["xg%d" % j, "ss"], ["tmpf", "ss"])
                    P.act(lambda e: e.activation(out=rstd[:], in_=ss[:], func=AF.Sqrt, bias=EPS, scale=1.0),
                          ["ss"], ["rstd"])
                    P.dve(lambda e: e.reciprocal(out=rstd[:], in_=rstd[:]), ["rstd"], ["rstd"])
                    for j in range(ntl):
                        P.dve(lambda e, j=j: e.scalar_tensor_tensor(out=tmpf[:], in0=xg[:, j, :], scalar=rstd[:, j:j + 1],
                                                                    in1=Gb[:], op0=ALU.mult, op1=ALU.mult),
                              ["xg%d" % j, "rstd", "Gb"], ["tmpf"])
                        P.dve(lambda e: e.tensor_tensor(out=hb[:], in0=tmpf[:], in1=Sb[:], op=ALU.add),
                              ["tmpf", "Sb"], ["hb"])
                        for c in range(8):
                            P.pe(lambda e, c=c: e.transpose(out=pT[:, c, :], in_=hb[:, c * 128:(c + 1) * 128],
                                                            identity=idb[:]),
                                 ["hb", "idb"], ["pT"])
                        P.act(lambda e, j=j: e.copy(out=hT[:, :, j * 128:(j + 1) * 128], in_=pT[:]), ["pT"], ["hT"])
                    for j in range(ntl):
                        so = itm % 2
                        itm += 1
                        for (c0, cn) in ((0, 512), (512, 512), (1024, 128)):
                            sp_ = ifm % 2
                            ifm += 1
                            for k in range(8):
                                P.pe(lambda e, sp_=sp_, k=k, j=j, c0=c0, cn=cn: e.matmul(
                                    pTM[sp_][:, 0:cn], lhsT=hT[:, k, j * 128:(j + 1) * 128], rhs=wtm[:, k, c0:c0 + cn],
                                    start=(k == 0), stop=(k == 7)), ["hT", "wtm"], ["pTM%d" % sp_])
                            P.dve(lambda e, sp_=sp_, so=so, c0=c0, cn=cn: e.tensor_copy(
                                out=tmo[so][:, c0:c0 + cn], in_=pTM[sp_][:, 0:cn]), ["pTM%d" % sp_], ["tmo%d" % so])
                        tt_ = (t0 // 128) + j
                        P.dma("pool", projM[:, :, tt_ * 128:(tt_ + 1) * 128].rearrange("b p d -> p b d"),
                              tmo[so][:].rearrange("p (b d) -> p b d", d=128), r=["tmo%d" % so], w=["projM"])
                    for c in range(NFM):
                        sp_ = ifm % 3
                        ifm += 1
                        for k in range(8):
                            P.pe(lambda e, sp_=sp_, k=k, c=c, n=n: e.matmul(
                                pFM[sp_][:, 0:n], lhsT=wfm[:, k, c * 128:(c + 1) * 128], rhs=hT[:, k, 0:n],
                                start=(k == 0), stop=(k == 7)), ["hT", "wfm%d" % k], ["pFM%d" % sp_])
                        pk = "pFM%d" % sp_
                        if c == CGATE:
                            P.act(lambda e, sp_=sp_, n=n: e.copy(out=go[:, 0:n], in_=pFM[sp_][:, 0:n]), [pk], ["go"])
                            P.dma("pool", gatesT[:, t0:t0 + n], go[:, 0:n], r=["go"], w=["gatesT"])
                            continue
                        so = ifo % 3
                        ifo += 1
                        fk = "fo%d" % so
                        if c < CGLU and stream == 0:
                            sq = iq % 2
                            iq += 1
                            qk_ = "qraw%d" % sq
                            P.act(lambda e, sp_=sp_, sq=sq, n=n: e.copy(out=qraw[sq][:, 0:n], in_=pFM[sp_][:, 0:n]),
                                  [pk], [qk_])
                            P.pe(lambda e, sq=sq, n=n: e.matmul(pR[:, 0:n], lhsT=rpb[:], rhs=qraw[sq][:, 0:n],
                                                                start=True, stop=True), [qk_, "rpb"], ["pR"])
                            P.dve(lambda e, sq=sq, n=n, t0=t0: e.tensor_tensor(out=t1[sq][:, 0:n], in0=qraw[sq][:, 0:n],
                                                                               in1=cosT[:, t0:t0 + n], op=ALU.mult),
                                  [qk_, "cosT"], ["t1%d" % sq])
                            P.dve(lambda e, sq=sq, n=n, t0=t0: e.tensor_tensor(out=t2[sq][:, 0:n], in0=pR[:, 0:n],
                                                                               in1=sinT[:, t0:t0 + n], op=ALU.mult),
                                  ["pR", "sinT"], ["t2%d" % sq])
                            P.pool(lambda e, sq=sq, so=so, n=n: e.tensor_tensor(out=fo[so][:, 0:n], in0=t2[sq][:, 0:n],
                                                                                in1=t1[sq][:, 0:n], op=ALU.add),
                                   ["t2%d" % sq, "t1%d" % sq], [fk])
                        elif CBR <= c < CGATE:
                            P.act(lambda e, sp_=sp_, so=so, n=n: e.activation(out=fo[so][:, 0:n], in_=pFM[sp_][:, 0:n],
                                                                              func=AF.Sigmoid), [pk], [fk])
                        else:
                            if c % 2 == 0:
                                P.act(lambda e, sp_=sp_, so=so, n=n: e.copy(out=fo[so][:, 0:n], in_=pFM[sp_][:, 0:n]),
                                      [pk], [fk])
                            else:
                                P.dve(lambda e, sp_=sp_, so=so, n=n: e.tensor_copy(out=fo[so][:, 0:n],
                                                                                   in_=pFM[sp_][:, 0:n]), [pk], [fk])
                        P.dma("pool", projT[c * 128:(c + 1) * 128, t0:t0 + n], fo[so][:, 0:n], r=[fk], w=["projT"])
                P.barrier()

            phase(l, 2)
            with Scope(nc) as S:
                qT = S.sb("qT", [128, 4, T], BF16)
                kT = S.sb("kT", [128, 4, T], BF16)
                va = S.sb("va", [128, NT, 2, 65], BF16)
                vtmp = S.sb("vtmp", [128, NT * 128], BF16)
                aT = S.sb("aT", [128, 4, T], BF16)
                m32 = S.sb("m32", [128, 2, 128], F32)
                mk = S.sb("mk", [128, 2, 4, 128], BF16)
                sk = S.sb("sk", [128, 8], F32)
                esk = S.sb("esk", [128, 8], F32)
                pt = [S.sb("pt%d" % i, [128, 5, 512], BF16) for i in range(2)]
                den = S.sb("den", [128, 4], F32)
                att = [S.sb("att%d" % i, [128, 512], BF16) for i in range(2)]
                pST = [S.ps("pST%d" % i, [128, 512]) for i in range(3)]
                pPV = [S.ps("pPV%d" % i, [128, 4, 128]) for i in range(2)]
                pAT = S.ps("pAT", [128, 4, 128], BF16)
                for c in range(4):
                    P.dma("sp", qT[:, c, :], projT[c * 128:(c + 1) * 128, :], r=["projT"], w=["qT"])
                for c in range(4):
                    P.dma("sp", kT[:, c, :], projT[(CK + c) * 128:(CK + 1 + c) * 128, :], r=["projT"], w=["kT"])
                P.dma("sp", vtmp[:], projM[0, :, :], r=["projM"], w=["vtmp"])
                P.dve(lambda e: e.tensor_copy(out=va[:, :, :, 0:64],
                                              in_=vtmp[:].rearrange("p (t g d) -> p t g d", g=2, d=64)), ["vtmp"], ["va"])
                P.pool(lambda e: e.memset(va[:, :, :, 64:65], 1.0), [], ["va"])
                P.dma("sp", m32[:, 0, :], mprev_in[:, :], w=["m32"])
                P.dma("sp", m32[:, 1, :], mnext_in[:, :], w=["m32"])
                for i in range(2):
                    for hh in range(4):
                        P.dve(lambda e, i=i, hh=hh: e.tensor_copy(out=mk[:, i, hh, :], in_=m32[:, i, :]), ["m32"], ["mk"])
                P.dma("sp", sk[:], sink_in[l, :].partition_broadcast(128), w=["sk"])
                P.act(lambda e: e.activation(out=esk[:], in_=sk[:], func=AF.Exp), ["sk"], ["esk"])
                ist = 0
                ipv = 0
                iat = 0
                qblocks = list(range(NT)) if not last else list(range(32))
                if KD < 1:
                    qblocks = []
                for n in qblocks:
                    sa = iat % 2
                    iat += 1
                    ak = "att%d" % sa
                    for g in range(2):
                        if n < 32:
                            kbs = [(32, None), (33, None)]
                            if n > 0:
                                kbs.append((n - 1, 0))
                            kbs.append((n, None))
                            if n < 31:
                                kbs.append((n + 1, 1))
                        else:
                            kbs = [(32, None), (33, None)]
                        spt = ipv % 2
                        ptk = "pt%d" % spt
                        pvk = "pPV%d" % spt
                        ipv += 1
                        for bi, (kb, mi) in enumerate(kbs):
                            s_ = ist % 3
                            ist += 1
                            for hh in range(4):
                                h = 4 * g + hh
                                p0 = (h % 2) * 64
                                P.pe(lambda e, s_=s_, hh=hh, h=h, kb=kb, n=n, g=g: e.matmul(
                                    pST[s_][:, hh * 128:(hh + 1) * 128],
                                    lhsT=kT[:, 2 * g + (h % 2), kb * 128:(kb + 1) * 128],
                                    rhs=qT[:, h // 2, n * 128:(n + 1) * 128], start=True, stop=True),
                                    ["qT", "kT"], ["pST%d" % s_])
                            P.act(lambda e, s_=s_, spt=spt, bi=bi: e.activation(out=pt[spt][:, bi, :], in_=pST[s_][:],
                                                                               func=AF.Exp, scale=0.125),
                                  ["pST%d" % s_], [ptk + "_%d" % bi])
                            if mi is not None:
                                P.dve(lambda e, spt=spt, bi=bi, mi=mi: e.tensor_tensor(
                                    out=pt[spt][:, bi, :], in0=pt[spt][:, bi, :],
                                    in1=mk[:, mi, :, :].rearrange("p a b -> p (a b)"), op=ALU.mult),
                                    [ptk + "_%d" % bi, "mk"], [ptk + "_%d" % bi])
                        for hh in range(4):
                            for bi, (kb, mi) in enumerate(kbs):
                                P.pe(lambda e, spt=spt, bi=bi, hh=hh, kb=kb, g=g, nb=len(kbs): e.matmul(
                                    pPV[spt][:, hh, 0:65], lhsT=pt[spt][:, bi, hh * 128:(hh + 1) * 128],
                                    rhs=va[:, kb, g, :], start=(bi == 0), stop=(bi == nb - 1)),
                                    [ptk + "_%d" % bi, "va"], [pvk])
                        if KD < 3:
                            continue
                        P.dve(lambda e, spt=spt, g=g: e.tensor_tensor(out=den[:], in0=pPV[spt][:, :, 64],
                                                                      in1=esk[:, 4 * g:4 * g + 4], op=ALU.add),
                              [pvk, "esk"], ["den"])
                        P.dve(lambda e: e.reciprocal(out=den[:], in_=den[:]), ["den"], ["den"])
                        for hh in range(4):
                            h = 4 * g + hh
                            P.dve(lambda e, spt=spt, hh=hh, h=h, sa=sa: e.tensor_scalar(
                                out=att[sa][:, h * 64:(h + 1) * 64], in0=pPV[spt][:, hh, 0:64],
                                scalar1=den[:, hh:hh + 1], scalar2=None, op0=ALU.mult), [pvk, "den"], [ak])
                    for c in (range(4) if KD >= 4 else []):
                        P.pe(lambda e, sa=sa, c=c: e.transpose(out=pAT[:, c, :], in_=att[sa][:, c * 128:(c + 1) * 128],
                                                               identity=idb[:]), [ak, "idb"], ["pAT"])
                    P.act(lambda e, n=n: e.copy(out=aT[:, :, n * 128:(n + 1) * 128], in_=pAT[:]), ["pAT"], ["aT"])
                nq = len(qblocks) * 128
                for c in (range(4) if nq else []):
                    P.dma("pool", attT[c * 128:(c + 1) * 128, 0:nq], aT[:, c, 0:nq], r=["aT"], w=["attT"])
                P.barrier()

            phase(l, 3)
            with Scope(nc) as S:
                cv = S.sb("cv", [128, 4, 34], F32)
                Y = S.sb("Y", [128, 4, 512], F32)
                zb = S.sb("zb", [128, 4, 15 + SEQ + 15], BF16)
                zcb = S.sb("zcb", [128, 4, 15 + CTX + 15], BF16)
                dg = S.sb("dg", [128, 4, 31, 128], BF16)
                vl = [S.sb("vl%d" % i, [128, T], BF16) for i in range(2)]
                gl = [S.sb("gl%d" % i, [128, T], BF16) for i in range(2)]
                sg = S.sb("sg", [128, T], F32)
                sq = [S.sb("sq%d" % i, [128, 512], F32) for i in range(2)]
                mean = S.sb("mean", [128, 512], F32)
                m2 = S.sb("m2", [128, 512], F32)
                rs = S.sb("rs", [128, 512], F32)
                tt = [S.sb("tt%d" % i, [128, 512], F32) for i in range(2)]
                co = [S.sb("co%d" % i, [128, 512], BF16) for i in range(2)]
                pY = [S.ps("pY%d" % i, [128, 512]) for i in range(2)]
                pS1 = S.ps("pS1", [128, 512])
                pS2 = S.ps("pS2", [128, 512])
                P.dma("sp", cv[:], cvec_in[l, :, :, :], w=["cv"])
                P.pool(lambda e: e.memset(zb[:], 0.0), [], ["zb"])
                P.pool(lambda e: e.memset(zcb[:], 0.0), [], ["zcb"])
                for c in range(4):
                    for j in range(31):
                        eng = P.dve if (j % 3) else P.pool
                        eng(lambda e, c=c, j=j: e.tensor_scalar(out=dg[:, c, j, :], in0=idb[:], scalar1=cv[:, c, j:j + 1],
                                                                scalar2=None, op0=ALU.mult), ["idb", "cv"], ["dg%d" % c])
                for c in range(4):
                    s_ = c % 2
                    P.dma("sp", vl[s_][:], projT[(CGLU + c) * 128:(CGLU + 1 + c) * 128, :], r=["projT"], w=["vl%d" % s_])
                    P.dma("sp", gl[s_][:], projT[(CGLU + 4 + c) * 128:(CGLU + 5 + c) * 128, :], r=["projT"],
                          w=["gl%d" % s_])
                    P.act(lambda e, s_=s_: e.activation(out=sg[:], in_=gl[s_][:], func=AF.Sigmoid),
                          ["gl%d" % s_], ["sg"])
                    P.dve(lambda e, s_=s_, c=c: e.tensor_tensor(out=zb[:, c, 15:15 + SEQ], in0=vl[s_][:, 0:SEQ],
                                                                in1=sg[:, 0:SEQ], op=ALU.mult),
                          ["vl%d" % s_, "sg", "zb"], ["zb%d" % c])
                    P.dve(lambda e, s_=s_, c=c: e.tensor_tensor(out=zcb[:, c, 15:15 + CTX], in0=vl[s_][:, SEQ:T],
                                                                in1=sg[:, SEQ:T], op=ALU.mult),
                          ["vl%d" % s_, "sg", "zcb"], ["zcb%d" % c])
                groups = GROUPS512 if not last else GROUPS512[:8]
                isq = 0
                ico = 0
                ipy = 0
                for (t0, n) in groups:
                    ykeys = ["Yc%d" % c for c in range(4)]
                    for c in range(4):
                        sp_ = ipy % 2
                        ipy += 1
                        for j in range(31):
                            if t0 < SEQ:
                                P.pe(lambda e, sp_=sp_, c=c, j=j, t0=t0, n=n: e.matmul(
                                    pY[sp_][:, 0:n], lhsT=dg[:, c, j, :], rhs=zb[:, c, t0 + j:t0 + j + n],
                                    start=(j == 0), stop=(j == 30)), ["dg%d" % c, "zb%d" % c], ["pY%d" % sp_])
                            else:
                                P.pe(lambda e, sp_=sp_, c=c, j=j, n=n: e.matmul(
                                    pY[sp_][:, 0:n], lhsT=dg[:, c, j, :], rhs=zcb[:, c, j:j + n],
                                    start=(j == 0), stop=(j == 30)), ["dg%d" % c, "zcb%d" % c], ["pY%d" % sp_])
                        P.act(lambda e, sp_=sp_, c=c, n=n: e.activation(out=Y[:, c, 0:n], in_=pY[sp_][:, 0:n],
                                                                        func=AF.Identity, bias=cv[:, c, 31:32], scale=1.0),
                              ["pY%d" % sp_, "cv"], [ykeys[c]])
                    for c in range(4):
                        P.pe(lambda e, c=c, n=n: e.matmul(pS1[:, 0:n], lhsT=ones32[:], rhs=Y[:, c, 0:n],
                                                          start=(c == 0), stop=(c == 3)),
                             ["ones32", ykeys[c]], ["pS1"])
                    for c in range(4):
                        s_ = isq % 2
                        isq += 1
                        P.act(lambda e, c=c, s_=s_, n=n: e.activation(out=sq[s_][:, 0:n], in_=Y[:, c, 0:n],
                                                                      func=AF.Square), [ykeys[c]], ["sq%d" % s_])
                        P.pe(lambda e, c=c, s_=s_, n=n: e.matmul(pS2[:, 0:n], lhsT=ones32[:], rhs=sq[s_][:, 0:n],
                                                                 start=(c == 0), stop=(c == 3)),
                             ["ones32", "sq%d" % s_], ["pS2"])
                    P.act(lambda e, n=n: e.activation(out=mean[:, 0:n], in_=pS1[:, 0:n], func=AF.Copy, scale=1.0 / 512),
                          ["pS1"], ["mean"])
                    P.dve(lambda e, n=n: e.tensor_tensor(out=m2[:, 0:n], in0=mean[:, 0:n], in1=mean[:, 0:n], op=ALU.mult),
                          ["mean"], ["m2"])
                    P.dve(lambda e, n=n: e.scalar_tensor_tensor(out=rs[:, 0:n], in0=pS2[:, 0:n], scalar=1.0 / 512,
                                                                in1=m2[:, 0:n], op0=ALU.mult, op1=ALU.subtract),
                          ["pS2", "m2"], ["rs"])
                    P.act(lambda e, n=n: e.activation(out=rs[:, 0:n], in_=rs[:, 0:n], func=AF.Sqrt, bias=EPS, scale=1.0),
                          ["rs"], ["rs"])
                    P.dve(lambda e, n=n: e.reciprocal(out=rs[:, 0:n], in_=rs[:, 0:n]), ["rs"], ["rs"])
                    for c in range(4):
                        s_ = ico % 2
                        ico += 1
                        P.dve(lambda e, c=c, s_=s_, n=n: e.tensor_tensor(out=tt[s_][:, 0:n], in0=Y[:, c, 0:n],
                                                                         in1=mean[:, 0:n], op=ALU.subtract),
                              [ykeys[c], "mean"], ["tt%d" % s_])
                        P.dve(lambda e, s_=s_, n=n: e.tensor_tensor(out=tt[s_][:, 0:n], in0=tt[s_][:, 0:n],
                                                                    in1=rs[:, 0:n], op=ALU.mult),
                              ["tt%d" % s_, "rs"], ["tt%d" % s_])
                        P.act(lambda e, c=c, s_=s_, n=n: e.activation(out=co[s_][:, 0:n], in_=tt[s_][:, 0:n], func=AF.Silu,
                                                                      scale=cv[:, c, 32:33], bias=cv[:, c, 33:34]),
                              ["tt%d" % s_, "cv"], ["co%d" % s_])
                        P.dma("pool", cbT[c * 128:(c + 1) * 128, t0:t0 + n], co[s_][:, 0:n], r=["co%d" % s_],
                              w=["cbT"])
                P.barrier()

            phase(l, 4)
            with Scope(nc) as S4:
                wkT = S4.sb("wkT", [128, NT, 8], F32)
                wkH = S4.sb("wkH", [128, NT, 2, 8], F32)
                thT = S4.sb("thT", [128, NT, 8], F32)
                decB = S4.sb("decB", [128, 8, NCH], F32)
                wmv = S4.sb("wmv", [128, 8, 3], F32)
                gmn = S4.sb("gmn", [128, 4], F32)
                bm32 = S4.sb("bm32", [128, 2, 128], F32)
                bmk = S4.sb("bmk", [128, 2, 128], BF16)
                P.dma("sp", wmv[:], wm_in[l, :, :, :], w=["wmv"])
                P.dma("sp", gmn[:], gmn_in[l, :, :], w=["gmn"])
                P.dma("sp", bm32[:, 0, :], bmf_in[:, :], w=["bm32"])
                P.dma("sp", bm32[:, 1, :], bmb_in[:, :], w=["bm32"])
                P.dve(lambda e: e.tensor_copy(out=bmk[:], in_=bm32[:]), ["bm32"], ["bmk"])
                with Scope(nc) as S:
                    gI = S.sb("gI", [64, T], F32)
                    gF = S.sb("gF", [64, T], F32)
                    csp = S.sb("csp", [64, T], F32)
                    nb = S.sb("nb", [64, T], F32)
                    msk = S.sb("msk", [64, T], F32)
                    bmt = S.sb("bmt", [64, 2], F32)
                    nbf = S.sb("nbf", [64, 1], F32)
                    tot = S.sb("tot", [64, NCH], F32)
                    umax = S.sb("umax", [64, NCH], F32)
                    R = S.sb("R", [64, NCH], F32)
                    mc = S.sb("mc", [64, NCH + 1], F32)
                    dec = S.sb("dec", [64, NCH], F32)
                    sel = S.sb("sel", [64, 8, 128], F32)
                    pG = [S.ps("pG%d" % i, [128, 4, 2, 64]) for i in range(2)]
                    pD = [S.ps("pD%d" % i, [128, 4, NCH]) for i in range(2)]
                    v3 = lambda a: a[:].rearrange("p (c l) -> p c l", l=64)
                    P.dma("sp", gI[:], gatesT[0:64, :], r=["gatesT"], w=["gI"])
                    P.dma("sp", gF[:], gatesT[64:128, :], r=["gatesT"], w=["gF"])
                    P.dma("sp", bmt[:], bm_in[l, :, :], w=["bmt"])
                    P.dma("sp", sel[:], sel_in[:, :, :], w=["sel"])
                    P.dve(lambda e: e.tensor_scalar(out=nbf[:], in0=bmt[:, 1:2], scalar1=-1.0, scalar2=None, op0=ALU.mult),
                          ["bmt"], ["nbf"])
                    P.pool(lambda e: e.memset(msk[:], 1.0), [], ["msk"])
                    P.pool(lambda e: e.memset(v3(msk)[:, :, 0:1], 0.0), [], ["msk"])
                    P.act(lambda e: e.activation(out=gI[:], in_=gI[:], func=AF.Identity, bias=bmt[:, 0:1], scale=1.0),
                          ["gI", "bmt"], ["gI"])
                    P.act(lambda e: e.activation(out=gF[:], in_=gF[:], func=AF.Exp, bias=nbf[:], scale=-1.0),
                          ["gF", "nbf"], ["gF"])
                    P.act(lambda e: e.activation(out=gF[:], in_=gF[:], func=AF.Ln, bias=1.0, scale=1.0), ["gF"], ["gF"])
                    P.dve(lambda e: e.tensor_tensor_scan(out=csp[:], data0=msk[:], data1=gF[:], initial=0.0,
                                                         op0=ALU.mult, op1=ALU.add), ["msk", "gF"], ["csp"])
                    P.dve(lambda e: e.tensor_copy(out=tot[:], in_=v3(csp)[:, :, 63]), ["csp"], ["tot"])
                    P.dve(lambda e: e.tensor_copy(out=nb[0:32, :], in_=csp[0:32, :]), ["csp"], ["nb"])
                    P.dve(lambda e: e.tensor_tensor(out=nb[32:64, :], in0=gF[32:64, :], in1=csp[32:64, :],
                                                    op=ALU.subtract), ["gF", "csp"], ["nb"])
                    P.dve(lambda e: e.tensor_tensor(out=v3(nb)[32:64], in0=v3(nb)[32:64],
                                                    in1=tot[32:64, :].unsqueeze(2).broadcast_to([32, NCH, 64]),
                                                    op=ALU.add), ["nb", "tot"], ["nb"])
                    P.dve(lambda e: e.tensor_tensor(out=gI[:], in0=gI[:], in1=nb[:], op=ALU.add), ["gI", "nb"], ["gI"])
                    P.dve(lambda e: e.tensor_reduce(out=umax[:], in_=v3(gI), axis=AX.X, op=ALU.max), ["gI"], ["umax"])
                    P.dve(lambda e: e.memset(mc[:], -1e30), [], ["mc"])
                    for (r0, order) in ((0, ORDER_F), (32, ORDER_B)):
                        for j, c in enumerate(order):
                            P.dve(lambda e, r0=r0, c=c: e.tensor_tensor(out=R[r0:r0 + 32, c:c + 1],
                                                                        in0=mc[r0:r0 + 32, c:c + 1],
                                                                        in1=umax[r0:r0 + 32, c:c + 1], op=ALU.max),
                                  ["mc", "umax"], ["R"])
                            if j + 1 < len(order):
                                c2 = order[j + 1]
                                P.dve(lambda e, r0=r0, c=c, c2=c2: e.tensor_tensor(
                                    out=mc[r0:r0 + 32, c2:c2 + 1], in0=R[r0:r0 + 32, c:c + 1],
                                    in1=tot[r0:r0 + 32, c:c + 1], op=ALU.subtract), ["R", "tot"], ["mc"])
                    Rb = lambda: R[:, :].unsqueeze(2).broadcast_to([64, NCH, 64])
                    P.dve(lambda e: e.tensor_tensor(out=v3(gI), in0=v3(gI), in1=Rb(), op=ALU.subtract), ["gI", "R"], ["gI"])
                    P.dve(lambda e: e.tensor_tensor(out=v3(nb), in0=v3(nb), in1=Rb(), op=ALU.subtract), ["nb", "R"], ["nb"])
                    P.dve(lambda e: e.tensor_tensor(out=dec[:], in0=mc[:, 0:NCH], in1=R[:], op=ALU.subtract),
                          ["mc", "R"], ["dec"])
                    P.dve(lambda e: e.tensor_scalar(out=gI[:], in0=gI[:], scalar1=LN_DK, scalar2=None, op0=ALU.add),
                          ["gI"], ["gI"])
                    P.act(lambda e: e.activation(out=gI[:], in_=gI[:], func=AF.Exp), ["gI"], ["gI"])
                    P.act(lambda e: e.activation(out=nb[:], in_=nb[:], func=AF.Exp), ["nb"], ["nb"])
                    P.act(lambda e: e.activation(out=dec[:], in_=dec[:], func=AF.Exp), ["dec"], ["dec"])
                    for t4 in range(0, NT, 4):
                        s_ = (t4 // 4) % 2
                        nt4 = min(4, NT - t4)
                        for j in range(nt4):
                            t = t4 + j
                            P.pe(lambda e, s_=s_, j=j, t=t: e.transpose(out=pG[s_][:, j, 0, :],
                                                                        in_=gI[:, t * 128:(t + 1) * 128],
                                                                        identity=id32[0:64, 0:64]),
                                 ["gI", "id32"], ["pG%d" % s_])
                            P.pe(lambda e, s_=s_, j=j, t=t: e.transpose(out=pG[s_][:, j, 1, :],
                                                                        in_=nb[:, t * 128:(t + 1) * 128],
                                                                        identity=id32[0:64, 0:64]),
                                 ["nb", "id32"], ["pG%d" % s_])
                        for d in range(2):
                            P.dve(lambda e, s_=s_, t4=t4, nt4=nt4, d=d: e.tensor_copy(
                                out=wkT[:, t4:t4 + nt4, d * 4:d * 4 + 4], in_=pG[s_][:, 0:nt4, 0, d * 32:d * 32 + 4]),
                                ["pG%d" % s_], ["wkT"])
                            P.dve(lambda e, s_=s_, t4=t4, nt4=nt4, d=d: e.tensor_copy(
                                out=thT[:, t4:t4 + nt4, d * 4:d * 4 + 4], in_=pG[s_][:, 0:nt4, 1, d * 32:d * 32 + 4]),
                                ["pG%d" % s_], ["thT"])
                    P.pool(lambda e: e.memset(wkH[:], 0.0), [], ["wkH"])
                    P.dve(lambda e: e.tensor_copy(out=wkH[0:64, :, 0, :], in_=wkT[0:64, :, :]), ["wkT", "wkH"], ["wkH"])
                    P.dve(lambda e: e.tensor_copy(out=wkH[64:128, :, 1, :], in_=wkT[64:128, :, :]), ["wkT", "wkH"], ["wkH"])
                    for d in range(2):
                        for hh in range(4):
                            P.pe(lambda e, d=d, hh=hh: e.matmul(pD[d][:, hh, :], lhsT=sel[:, d * 4 + hh, :], rhs=dec[:],
                                                                start=True, stop=True), ["sel", "dec"], ["pD%d" % d])
                        P.dve(lambda e, d=d: e.tensor_copy(out=decB[:, d * 4:d * 4 + 4, :], in_=pD[d][:]),
                              ["pD%d" % d], ["decB"])
                    P.barrier()

                for h in range(4):
                    with Scope(nc) as S:
                        qr = S.sb("qr", [128, T], BF16)
                        kr = S.sb("kr", [128, T], BF16)
                        acc = S.sb("acc", [128, T], F32)
                        qTm = S.sb("qTm", [128, T], BF16)
                        kTm = S.sb("kTm", [128, T], BF16)
                        qTp = S.sb("qTp", [128, NT, 2, 128], BF16)
                        ktm = S.sb("ktm", [128, NT, 128], BF16)
                        vau = S.sb("vau", [128, NT, 129], BF16)
                        otm = S.sb("otm", [128, NT, 128], BF16)
                        hraw = S.sb("hraw", [128, NT, 2, 129], F32)
                        dnv = S.sb("dnv", [128, 2, NT], F32)
                        C32 = [[S.sb("C32_%d_%d" % (i, p_), [128, 129], F32) for p_ in range(2)] for i in range(2)]
                        Cbf = [[S.sb("Cbf_%d_%d" % (i, p_), [128, 129], BF16) for p_ in range(2)] for i in range(2)]
                        ptl = [[S.sb("ptl%d_%d" % (d, i), [128, 128], BF16) for i in range(2)] for d in range(2)]
                        kp = [[S.sb("kp%d_%d" % (d, i), [128, 128], BF16) for i in range(4)] for d in range(2)]
                        dn = [S.sb("dn%d" % d, [128, 1], F32) for d in range(2)]
                        sgm = S.sb("sgm", [128, 128], F32)
                        junk = S.sb("junk", [128, 128], F32)
                        s1 = S.sb("s1", [128, NT], F32)
                        s2 = S.sb("s2", [128, NT], F32)
                        mu = S.sb("mu", [128, NT], F32)
                        rsd = S.sb("rsd", [128, NT], F32)
                        hn = [S.sb("hn%d" % i, [128, 128], BF16) for i in range(2)]
                        mho = S.sb("mho", [128, T], BF16)
                        pKT = S.ps("pKT", [128, 8, 128], BF16)
                        pSmt = [S.ps("pSm%d" % d, [128, 128]) for d in range(2)]
                        pSm = [pSmt[d][:, :] for d in range(2)]
                        pIN = [S.ps("pIN%d" % d, [128, 129]) for d in range(2)]
                        pKVt = [S.ps("pKV%d" % d, [128, 129]) for d in range(2)]
                        pKV = [[pKVt[d][:, :] for i in range(2)] for d in range(2)]
                        pHT = pKT

                        P.dma("sp", qr[:], projT[(CQKM + h) * 128:(CQKM + 1 + h) * 128, :], r=["projT"], w=["qr"])
                        P.dma("sp", kr[:], projT[(CQKM + 4 + h) * 128:(CQKM + 5 + h) * 128, :], r=["projT"], w=["kr"])
                        P.dma("sp", mho[:], projM[1 + h, :, :], r=["projM"], w=["mho"])
                        P.dve(lambda e: e.tensor_copy(out=vau[:, :, 0:128], in_=mho[:].rearrange("p (t d) -> p t d", d=128)),
                              ["mho"], ["vau"])
                        P.pool(lambda e: e.memset(vau[:, :, 128:129], 1.0), [], ["vau"])
                        P.dma("sp", otm[:].rearrange("p t d -> p (t d)"), projM[5 + h, :, :], r=["projM"], w=["otm"])
                        P.pool(lambda e: e.memset(qTp[:], 0.0), [], ["qTp"])
                        hsum = acc[:].rearrange("p (t d) -> p t d", d=128)
                        for d in range(2):
                            P.pool(lambda e, d=d: e.memset(C32[d][0][:], 0.0), [], ["C32_%d_0" % d])
                            P.pool(lambda e, d=d: e.memset(Cbf[d][0][:], 0.0), [], ["Cbf_%d_0" % d])
                        for (src, sk_, dst, dk_, wc) in ((qr, "qr", qTm, "qTm", h), (kr, "kr", kTm, "kTm", 4 + h)):
                            P.dve(lambda e, src=src, wc=wc: e.tensor_scalar(out=acc[:], in0=src[:], scalar1=wmv[:, wc, 1:2],
                                                                            scalar2=None, op0=ALU.mult),
                                  [sk_, "wmv"], ["acc"])
                            for (a, b) in ((0, SEQ), (SEQ, T)):
                                P.dve(lambda e, src=src, wc=wc, a=a, b=b: e.scalar_tensor_tensor(
                                    out=acc[:, a + 1:b], in0=src[:, a:b - 1], scalar=wmv[:, wc, 0:1], in1=acc[:, a + 1:b],
                                    op0=ALU.mult, op1=ALU.add), [sk_, "wmv", "acc"], ["acc"])
                                P.dve(lambda e, src=src, wc=wc, a=a, b=b: e.scalar_tensor_tensor(
                                    out=acc[:, a:b - 1], in0=src[:, a + 1:b], scalar=wmv[:, wc, 2:3], in1=acc[:, a:b - 1],
                                    op0=ALU.mult, op1=ALU.add), [sk_, "wmv", "acc"], ["acc"])
                            P.act(lambda e, dst=dst: e.activation(out=dst[:], in_=acc[:], func=AF.Silu), ["acc"], [dk_])
                        q4 = qTm[:].rearrange("p (t two l) -> p t two l", two=2, l=64)
                        P.dve(lambda e: e.tensor_copy(out=qTp[:, :, 0, 0:64], in_=q4[:, :, 0, :]), ["qTm"], ["qTp"])
                        P.dve(lambda e: e.tensor_copy(out=qTp[:, :, 1, 64:128], in_=q4[:, :, 1, :]), ["qTm"], ["qTp"])
                        for t8 in range(0, NT, 8):
                            n8 = min(8, NT - t8)
                            for j in range(n8):
                                P.pe(lambda e, j=j, t=t8 + j: e.transpose(out=pKT[:, j, :], in_=kTm[:, t * 128:(t + 1) * 128],
                                                                          identity=idb[:]), ["kTm", "idb"], ["pKT"])
                            P.act(lambda e, t8=t8, n8=n8: e.copy(out=ktm[:, t8:t8 + n8, :], in_=pKT[:, 0:n8, :]),
                                  ["pKT"], ["ktm"])
                        done_tiles = [set(), set()]
                        slot = [0, 0]
                        ptl_of = [{}, {}]
                        ORD = (ORDER_F, ORDER_B)

                        def geom(j, d):
                            c = ORD[d][j]
                            return c, c // 2, (c % 2) * 64, d * 4 + h

                        def st_kp(j, d):
                            c, t, r0, col = geom(j, d)
                            ks = j % 4
                            P.act(lambda e: e.activation(out=kp[d][ks][:, :], in_=ktm[:, t, :], func=AF.Copy,
                                                         scale=wkH[:, t, r0 // 64, col:col + 1]),
                                  ["ktm", "wkH"], ["kp%d_%d" % (d, ks)])

                        def st_pt(j, d):
                            c, t, r0, col = geom(j, d)
                            if t in done_tiles[d]:
                                return
                            done_tiles[d].add(t)
                            sl = slot[d] % 2
                            slot[d] += 1
                            ptl_of[d][t] = sl
                            P.pe(lambda e: e.matmul(pSm[d], lhsT=kTm[:, t * 128:(t + 1) * 128],
                                                    rhs=qTm[:, t * 128:(t + 1) * 128], start=True, stop=True),
                                 ["kTm", "qTm"], ["pSm%d" % d])
                            P.dve(lambda e: e.scalar_tensor_tensor(out=ptl[d][sl][:], in0=pSm[d],
                                                                   scalar=wkT[:, t, col:col + 1], in1=bmk[:, d, :],
                                                                   op0=ALU.mult, op1=ALU.mult),
                                  ["pSm%d" % d, "wkT", "bmk"], ["ptl%d_%d" % (d, sl)])

                        def st_kv(j, d):
                            c, t, r0, col = geom(j, d)
                            ks = j % 4
                            kv = 0
                            P.pe(lambda e: e.matmul(pKV[d][kv], lhsT=kp[d][ks][:, :], rhs=vau[:, t, :],
                                                    start=True, stop=True),
                                 ["kp%d_%d" % (d, ks), "vau"], ["pKV%d_%d" % (d, kv)])

                        def st_c(j, d):
                            c, t, r0, col = geom(j, d)
                            cur, nxt, kv = j % 2, (j + 1) % 2, 0
                            P.dve(lambda e: e.scalar_tensor_tensor(out=C32[d][nxt][:], in0=C32[d][cur][:],
                                                                   scalar=decB[:, col, c:c + 1], in1=pKV[d][kv],
                                                                   op0=ALU.mult, op1=ALU.add),
                                  ["C32_%d_%d" % (d, cur), "decB", "pKV%d_%d" % (d, kv)], ["C32_%d_%d" % (d, nxt)])
                            if j + 1 < NCH:
                                c2 = ORD[d][j + 1]
                                P.act(lambda e: e.activation(out=Cbf[d][nxt][:], in_=C32[d][nxt][:], func=AF.Copy,
                                                             scale=decB[:, col, c2:c2 + 1]),
                                      ["C32_%d_%d" % (d, nxt), "decB"], ["Cbf_%d_%d" % (d, nxt)])

                        def st_out(j, d):
                            c, t, r0, col = geom(j, d)
                            cur = j % 2
                            sl = ptl_of[d][t]
                            P.pe(lambda e: e.matmul(pIN[d][:], lhsT=qTp[:, t, r0 // 64, :], rhs=Cbf[d][cur][:],
                                                    start=True, stop=False),
                                 ["qTp", "Cbf_%d_%d" % (d, cur)], ["pIN%d" % d])
                            P.pe(lambda e: e.matmul(pIN[d][:], lhsT=ptl[d][sl][:, :], rhs=vau[:, t, :],
                                                    start=False, stop=True),
                                 ["ptl%d_%d" % (d, sl), "vau"], ["pIN%d" % d])
                            if d == 0:
                                P.act(lambda e: e.copy(out=hraw[r0:r0 + 64, t, d, :], in_=pIN[d][r0:r0 + 64, :]),
                                      ["pIN%d" % d], ["hraw%d" % d])
                            else:
                                P.dve(lambda e: e.tensor_copy(out=hraw[r0:r0 + 64, t, d, :], in_=pIN[d][r0:r0 + 64, :]),
                                      ["pIN%d" % d], ["hraw%d" % d])

                        for d in range(2):
                            st_kp(0, d)
                            st_kp(1, d)
                            st_pt(0, d)
                            st_pt(1, d)
                            st_kv(0, d)
                        for j in range(NCH):
                            for d in range(2):
                                if j + 2 < NCH:
                                    st_kp(j + 2, d)
                                    st_pt(j + 2, d)
                                st_c(j, d)
                                if j + 1 < NCH:
                                    st_kv(j + 1, d)
                                st_out(j, d)
                        for d in range(2):
                            col = d * 4 + h
                            P.act(lambda e, d=d: e.activation(out=dnv[:, d, :], in_=hraw[:, :, d, 128], func=AF.Abs),
                                  ["hraw%d" % d], ["dnv%d" % d])
                            P.dve(lambda e, d=d, col=col: e.tensor_tensor(out=dnv[:, d, :], in0=dnv[:, d, :],
                                                                          in1=thT[:, :, col], op=ALU.max),
                                  ["dnv%d" % d, "thT"], ["dnv%d" % d])
                            P.dve(lambda e, d=d: e.reciprocal(out=dnv[:, d, :], in_=dnv[:, d, :]), ["dnv%d" % d], ["dnv%d" % d])
                        P.dve(lambda e: e.tensor_tensor(out=hsum[:], in0=hraw[:, :, 0, 0:128],
                                                        in1=dnv[:, 0, :].unsqueeze(2).broadcast_to([128, NT, 128]),
                                                        op=ALU.mult), ["hraw0", "dnv0"], ["hsum", "acc"])
                        P.dve(lambda e: e.tensor_tensor(out=hraw[:, :, 1, 0:128], in0=hraw[:, :, 1, 0:128],
                                                        in1=dnv[:, 1, :].unsqueeze(2).broadcast_to([128, NT, 128]),
                                                        op=ALU.mult), ["hraw1", "dnv1"], ["hraw1"])
                        P.pool(lambda e: e.tensor_tensor(out=hsum[:], in0=hsum[:], in1=hraw[:, :, 1, 0:128], op=ALU.add),
                               ["hsum", "hraw1"], ["hsum"])
                        ntr = NT if not last else 32
                        P.dve(lambda e: e.memset(s1[:], 0.0), [], ["s1"])
                        P.dve(lambda e: e.memset(s2[:], 0.0), [], ["s2"])
                        for t in range(ntr):
                            P.act(lambda e, t=t: e.activation(out=sgm[:], in_=otm[:, t, :], func=AF.Sigmoid),
                                  ["otm"], ["sgm"])
                            P.dve(lambda e, t=t: e.tensor_tensor(out=hsum[:, t, :], in0=hsum[:, t, :], in1=sgm[:],
                                                                 op=ALU.mult), ["sgm", "hsum", "acc"], ["hg_%d" % t])
                            P.act(lambda e, t=t: e.activation(out=junk[:], in_=hsum[:, t, :], func=AF.Copy,
                                                              accum_out=s1[:, t:t + 1]), ["hg_%d" % t, "s1"],
                                  ["junk", "s1"])
                            P.act(lambda e, t=t: e.activation(out=junk[:], in_=hsum[:, t, :], func=AF.Square,
                                                              accum_out=s2[:, t:t + 1]), ["hg_%d" % t, "s2"],
                                  ["junk", "s2"])
                        P.dve(lambda e: e.tensor_scalar(out=mu[:], in0=s1[:], scalar1=1.0 / 128, scalar2=None, op0=ALU.mult),
                              ["s1"], ["mu"])
                        P.dve(lambda e: e.tensor_tensor(out=rsd[:], in0=mu[:], in1=mu[:], op=ALU.mult), ["mu"], ["rsd"])
                        P.dve(lambda e: e.scalar_tensor_tensor(out=rsd[:], in0=s2[:], scalar=1.0 / 128, in1=rsd[:],
                                                               op0=ALU.mult, op1=ALU.subtract), ["s2", "rsd"], ["rsd"])
                        P.act(lambda e: e.activation(out=rsd[:], in_=rsd[:], func=AF.Sqrt, bias=EPS, scale=1.0),
                              ["rsd"], ["rsd"])
                        P.dve(lambda e: e.reciprocal(out=rsd[:], in_=rsd[:]), ["rsd"], ["rsd"])
                        for t8 in range(0, ntr, 8):
                            n8 = min(8, ntr - t8)
                            for j in range(n8):
                                t = t8 + j
                                s_ = t % 2
                                P.dve(lambda e, t=t, s_=s_: e.tensor_scalar(out=hn[s_][:], in0=hsum[:, t, :],
                                                                            scalar1=mu[:, t:t + 1], scalar2=rsd[:, t:t + 1],
                                                                            op0=ALU.subtract, op1=ALU.mult),
                                      ["hg_%d" % t, "mu", "rsd"], ["hn%d" % s_])
                                P.pe(lambda e, j=j, s_=s_: e.transpose(out=pHT[:, j, :], in_=hn[s_][:], identity=idb[:]),
                                     ["hn%d" % s_, "idb"], ["pKT"])
                            P.act(lambda e, t8=t8, n8=n8: e.activation(
                                out=mho[:, t8 * 128:(t8 + n8) * 128].rearrange("p (a b) -> p a b", b=128),
                                in_=pHT[:, 0:n8, :], func=AF.Copy, scale=gmn[:, h:h + 1]), ["pKT", "gmn"], ["mho"])
                        P.dma("pool", mhT[h * 128:(h + 1) * 128, 0:ntr * 128], mho[:, 0:ntr * 128], r=["mho"], w=["mhT"])
                        P.barrier()

            phase(l, 5)
            with Scope(nc) as S:
                wA = S.sb("wA", [128, 4, D], BF16)
                wB = S.sb("wB", [128, 4, D], BF16)
                wC = S.sb("wC", [128, 4, D], BF16)
                wO = S.sb("wO", [128, 8, D], BF16)
                gt = S.sb("gt", [128, D], F32)
                aG = [S.sb("aG%d" % i, [128, 4, 512], BF16) for i in range(2)]
                bG = [S.sb("bG%d" % i, [128, 4, 512], BF16) for i in range(2)]
                cG = [S.sb("cG%d" % i, [128, 4, 512], BF16) for i in range(2)]
                br = [S.sb("br%d" % i, [128, 24, 512], BF16) for i in range(2)]
                m1 = [S.sb("m1_%d" % i, [128, 512], F32) for i in range(2)]
                m2_ = [S.sb("m2_%d" % i, [128, 512], F32) for i in range(2)]
                m3 = [S.sb("m3_%d" % i, [128, 512], F32) for i in range(2)]
                mg = S.sb("mg", [128, 8, 512], BF16)
                xt = [S.sb("xt%d" % i, [128, D], F32) for i in range(2)]
                ty = [S.sb("ty%d" % i, [128, 512], F32) for i in range(2)]
                pA = [S.ps("pA%d" % i, [128, 512]) for i in range(2)]
                pB = [S.ps("pB%d" % i, [128, 512]) for i in range(2)]
                pC = [S.ps("pC%d" % i, [128, 512]) for i in range(2)]
                pY = [S.ps("pY%d" % i, [128, 512]) for i in range(2)]
                P.dma("pool", wA[:], w_ao[l, :, :].rearrange("(c p) n -> p c n", p=128), w=["wA"])
                P.dma("pool", wB[:], w_pw[l, :, :].rearrange("(c p) n -> p c n", p=128), w=["wB"])
                P.dma("pool", wC[:], w_mo[l, :, :].rearrange("(c p) n -> p c n", p=128), w=["wC"])
                P.dma("pool", wO[:], w_out[l, :, :].rearrange("(c p) n -> p c n", p=128), w=["wO"])
                groups = GROUPS512 if not last else GROUPS512[:8]
                cur_stream = None
                ix = 0
                iy = 0
                for gi, (t0, n) in enumerate(groups):
                    stream = 0 if t0 < SEQ else 1
                    if stream != cur_stream:
                        cur_stream = stream
                        load_mod("sp", gt, l, stream, 2, "gt")
                    s_ = gi % 2
                    P.dma("sp", aG[s_][:, :, 0:n], attT[:, t0:t0 + n].rearrange("(c p) n -> p c n", p=128),
                          r=["attT"], w=["aG%d" % s_])
                    P.dma("sp", bG[s_][:, :, 0:n], cbT[:, t0:t0 + n].rearrange("(c p) n -> p c n", p=128),
                          r=["cbT"], w=["bG%d" % s_])
                    P.dma("sp", cG[s_][:, :, 0:n], mhT[:, t0:t0 + n].rearrange("(c p) n -> p c n", p=128),
                          r=["mhT"], w=["cG%d" % s_])
                    P.dma("sp", br[s_][:, :, 0:n],
                          projT[CBR * 128:CGATE * 128, t0:t0 + n].rearrange("(c p) n -> p c n", p=128),
                          r=["projT"], w=["br%d" % s_])
                    for j in range(8):
                        u = j % 2
                        for (pp, pn, ww, wn, src, sn) in ((pA, "pA", wA, "wA", aG, "aG"), (pB, "pB", wB, "wB", bG, "bG"),
                                                          (pC, "pC", wC, "wC", cG, "cG")):
                            for k in range(4):
                                P.pe(lambda e, pp=pp, ww=ww, src=src, u=u, k=k, j=j, s_=s_, n=n: e.matmul(
                                    pp[u][:, 0:n], lhsT=ww[:, k, j * 128:(j + 1) * 128], rhs=src[s_][:, k, 0:n],
                                    start=(k == 0), stop=(k == 3)), [wn, "%s%d" % (sn, s_)], ["%s%d" % (pn, u)])
                        brk = "br%d" % s_
                        P.dve(lambda e, u=u, s_=s_, j=j, n=n: e.tensor_tensor(out=m1[u][:, 0:n], in0=pA[u][:, 0:n],
                                                                              in1=br[s_][:, j, 0:n], op=ALU.mult),
                              ["pA%d" % u, brk], ["m1_%d" % u])
                        P.dve(lambda e, u=u, s_=s_, j=j, n=n: e.tensor_tensor(out=m2_[u][:, 0:n], in0=pB[u][:, 0:n],
                                                                              in1=br[s_][:, 8 + j, 0:n], op=ALU.mult),
                              ["pB%d" % u, brk], ["m2_%d" % u])
                        P.dve(lambda e, u=u, s_=s_, j=j, n=n: e.tensor_tensor(out=m3[u][:, 0:n], in0=pC[u][:, 0:n],
                                                                              in1=br[s_][:, 16 + j, 0:n], op=ALU.mult),
                              ["pC%d" % u, brk], ["m3_%d" % u])
                        P.dve(lambda e, u=u, n=n: e.tensor_tensor(out=m1[u][:, 0:n], in0=m1[u][:, 0:n], in1=m2_[u][:, 0:n],
                                                                   op=ALU.add), ["m1_%d" % u, "m2_%d" % u], ["m1_%d" % u])
                        P.dve(lambda e, u=u, j=j, n=n: e.tensor_tensor(out=mg[:, j, 0:n], in0=m1[u][:, 0:n],
                                                                        in1=m3[u][:, 0:n], op=ALU.add),
                               ["m1_%d" % u, "m3_%d" % u], ["mg%d" % j])
                    mgk = ["mg%d" % j for j in range(8)]
                    for jt in range(n // 128):
                        sx = ix % 2
                        ix += 1
                        xk = "xt%d" % sx
                        P.dma("sp", xt[sx][:], xs[t0 + jt * 128:t0 + (jt + 1) * 128, :], r=["xs"], w=[xk])
                        for cg in range(2):
                            sy = iy % 2
                            iy += 1
                            for k in range(8):
                                P.pe(lambda e, sy=sy, k=k, jt=jt, cg=cg: e.matmul(
                                    pY[sy][:], lhsT=mg[:, k, jt * 128:(jt + 1) * 128], rhs=wO[:, k, cg * 512:(cg + 1) * 512],
                                    start=(k == 0), stop=(k == 7)), [mgk[k], "wO"], ["pY%d" % sy])
                            P.dve(lambda e, sy=sy, cg=cg: e.tensor_tensor(out=ty[sy][:], in0=pY[sy][:],
                                                                          in1=gt[:, cg * 512:(cg + 1) * 512], op=ALU.mult),
                                  ["pY%d" % sy, "gt"], ["ty%d" % sy])
                            P.dve(lambda e, sy=sy, sx=sx, cg=cg: e.tensor_tensor(
                                out=xt[sx][:, cg * 512:(cg + 1) * 512], in0=xt[sx][:, cg * 512:(cg + 1) * 512],
                                in1=ty[sy][:], op=ALU.add), ["ty%d" % sy, xk], [xk])
                        P.dma("pool", xs[t0 + jt * 128:t0 + (jt + 1) * 128, :], xt[sx][:], r=[xk], w=["xs"])
                P.barrier()

            phase(l, 6)
            with Scope(nc) as S:
                wG = S.sb("wG", [128, 8, DFF], BF16)
                wU = S.sb("wU", [128, 8, DFF], BF16)
                wD = S.sb("wD", [128, 22, D], BF16)
                Gb = S.sb("Gb", [128, D], F32)
                Sb = S.sb("Sb", [128, D], F32)
                gt = S.sb("gt", [128, D], F32)
                gfin = S.sb("gfin", [128, D], F32)
                xg = S.sb("xg", [128, 2, D], F32)
                tmpf = S.sb("tmpf", [128, D], F32)
                hb = S.sb("hb", [128, D], BF16)
                hT = S.sb("hT", [128, 8, 256], BF16)
                ss = S.sb("ss", [128, 2], F32)
                rstd = S.sb("rstd", [128, 2], F32)
                aT = S.sb("aT", [128, 22, 256], BF16)
                sgl = [S.sb("sgl%d" % i, [128, 256], F32) for i in range(2)]
                ty = [S.sb("ty%d" % i, [128, 512], F32) for i in range(2)]
                pT = S.ps("pT", [128, 8, 128], BF16)
                pGa = [S.ps("pGa%d" % i, [128, 256]) for i in range(2)]
                pUa = [S.ps("pUa%d" % i, [128, 256]) for i in range(2)]
                pDn = [S.ps("pDn%d" % i, [128, 512]) for i in range(2)]
                for k in range(8):
                    P.dma("pool", wG[:, k, :], w_g[l, k * 128:(k + 1) * 128, :], w=["wG"])
                    P.dma("pool", wU[:, k, :], w_u[l, k * 128:(k + 1) * 128, :], w=["wU"])
                for k in range(0, 22, 2):
                    P.dma("pool", wD[:, k:k + 2, :], w_d[l, k * 128:(k + 2) * 128, :].rearrange("(c p) n -> p c n", p=128),
                          w=["wD"])
                groups = GROUPS256 if not last else GROUPS256[:16]
                cur_stream = None
                iu = 0
                iy = 0
                for gi, (t0, n) in enumerate(groups):
                    stream = 0 if t0 < SEQ else 1
                    if stream != cur_stream:
                        cur_stream = stream
                        load_mod("sp", Sb, l, stream, 3, "Sb")
                        load_mod("sp", tmpf, l, stream, 4, "tmpf")
                        P.dma("sp", Gb[:], g_ffn[l, :].partition_broadcast(128), w=["Gb"])
                        P.dve(lambda e: e.scalar_tensor_tensor(out=Gb[:], in0=tmpf[:], scalar=1.0, in1=Gb[:],
                                                               op0=ALU.add, op1=ALU.mult), ["tmpf", "Gb"], ["Gb"])
                        load_mod("sp", gt, l, stream, 5, "gt")
                    P.dve(lambda e: e.memset(ss[:], 0.0), [], ["ss"])
                    for j in range(2):
                        P.dma("sp", xg[:, j, :], xs[t0 + j * 128:t0 + (j + 1) * 128, :], r=["xs"], w=["xg%d" % j])
                        P.act(lambda e, j=j: e.activation(out=tmpf[:], in_=xg[:, j, :], func=AF.Square, scale=1.0 / 32,
                                                          accum_out=ss[:, j:j + 1]), ["xg%d" % j, "ss"], ["tmpf", "ss"])
                    P.act(lambda e: e.activation(out=rstd[:], in_=ss[:], func=AF.Sqrt, bias=EPS, scale=1.0),
                          ["ss"], ["rstd"])
                    P.dve(lambda e: e.reciprocal(out=rstd[:], in_=rstd[:]), ["rstd"], ["rstd"])
                    for j in range(2):
                        P.dve(lambda e, j=j: e.scalar_tensor_tensor(out=tmpf[:], in0=xg[:, j, :], scalar=rstd[:, j:j + 1],
                                                                    in1=Gb[:], op0=ALU.mult, op1=ALU.mult),
                              ["xg%d" % j, "rstd", "Gb"], ["tmpf"])
                        P.dve(lambda e: e.tensor_tensor(out=hb[:], in0=tmpf[:], in1=Sb[:], op=ALU.add),
                              ["tmpf", "Sb"], ["hb"])
                        for c in range(8):
                            P.pe(lambda e, c=c: e.transpose(out=pT[:, c, :], in_=hb[:, c * 128:(c + 1) * 128],
                                                            identity=idb[:]), ["hb", "idb"], ["pT"])
                        P.act(lambda e, j=j: e.copy(out=hT[:, :, j * 128:(j + 1) * 128], in_=pT[:]), ["pT"], ["hT"])
                    for f in range(22):
                        u = iu % 2
                        iu += 1
                        for k in range(8):
                            P.pe(lambda e, u=u, k=k, f=f: e.matmul(pGa[u][:], lhsT=wG[:, k, f * 128:(f + 1) * 128],
                                                                   rhs=hT[:, k, :], start=(k == 0), stop=(k == 7)),
                                 ["wG", "hT"], ["pGa%d" % u])
                        for k in range(8):
                            P.pe(lambda e, u=u, k=k, f=f: e.matmul(pUa[u][:], lhsT=wU[:, k, f * 128:(f + 1) * 128],
                                                                   rhs=hT[:, k, :], start=(k == 0), stop=(k == 7)),
                                 ["wU", "hT"], ["pUa%d" % u])
                        P.act(lambda e, u=u: e.activation(out=sgl[u][:], in_=pGa[u][:], func=AF.Silu),
                              ["pGa%d" % u], ["sgl%d" % u])
                        P.dve(lambda e, u=u, f=f: e.tensor_tensor(out=aT[:, f, :], in0=pUa[u][:], in1=sgl[u][:],
                                                                  op=ALU.mult), ["pUa%d" % u, "sgl%d" % u], ["aT%d" % f])
                    atk = ["aT%d" % f for f in range(22)]
                    for j in range(2):
                        xk = "xg%d" % j
                        for cg in range(2):
                            sy = iy % 2
                            iy += 1
                            for f in range(22):
                                P.pe(lambda e, sy=sy, f=f, j=j, cg=cg: e.matmul(
                                    pDn[sy][:], lhsT=aT[:, f, j * 128:(j + 1) * 128], rhs=wD[:, f, cg * 512:(cg + 1) * 512],
                                    start=(f == 0), stop=(f == 21)), [atk[f], "wD"], ["pDn%d" % sy])
                            P.dve(lambda e, sy=sy, cg=cg: e.tensor_tensor(out=ty[sy][:], in0=pDn[sy][:],
                                                                          in1=gt[:, cg * 512:(cg + 1) * 512], op=ALU.mult),
                                  ["pDn%d" % sy, "gt"], ["ty%d" % sy])
                            P.dve(lambda e, sy=sy, j=j, cg=cg: e.tensor_tensor(
                                out=xg[:, j, cg * 512:(cg + 1) * 512], in0=xg[:, j, cg * 512:(cg + 1) * 512],
                                in1=ty[sy][:], op=ALU.add), ["ty%d" % sy, xk], [xk])
                        if not last:
                            P.dma("pool", xs[t0 + j * 128:t0 + (j + 1) * 128, :], xg[:, j, :], r=[xk], w=["xs"])
                        else:
                            if gi == 0 and j == 0:
                                P.dma("sp", gfin[:], g_fin.partition_broadcast(128), r=[], w=["gfin"])
                            P.dve(lambda e: e.memset(ss[:, 0:1], 0.0), [], ["ss"])
                            P.act(lambda e, j=j: e.activation(out=tmpf[:], in_=xg[:, j, :], func=AF.Square, scale=1.0 / 32,
                                                              accum_out=ss[:, 0:1]), [xk, "ss"], ["tmpf", "ss"])
                            P.act(lambda e: e.activation(out=rstd[:, 0:1], in_=ss[:, 0:1], func=AF.Sqrt, bias=EPS,
                                                         scale=1.0), ["ss"], ["rstd"])
                            P.dve(lambda e: e.reciprocal(out=rstd[:, 0:1], in_=rstd[:, 0:1]), ["rstd"], ["rstd"])
                            P.dve(lambda e, j=j: e.scalar_tensor_tensor(out=xg[:, j, :], in0=xg[:, j, :],
                                                                        scalar=rstd[:, 0:1], in1=gfin[:], op0=ALU.mult,
                                                                        op1=ALU.mult), [xk, "rstd", "gfin"], [xk])
                            P.dma("pool", out[t0 + j * 128:t0 + (j + 1) * 128, :], xg[:, j, :], r=[xk], w=["out"])
                P.barrier()
        P.enabled = True
        if n_layers < DEPTH:
            with Scope(nc) as S:
                xo = S.sb("xo", [128, D], F32)
                for t in range(32):
                    P.dma("sp", xo[:], xs[t * 128:(t + 1) * 128, :], r=["xs"], w=["xo"])
                    P.dma("sp", out[t * 128:(t + 1) * 128, :], xo[:], r=["xo"], w=["out"])
                P.barrier()
        P.barrier()
        stats = P.emit()
    return nc, stats


def _consts():
    ident = np.eye(128, dtype=np.float32)
    rperm = np.zeros((128, 128), np.float32)
    sign = np.zeros(128, np.float32)
    for m in range(128):
        d = m % 64
        base = m - d
        hb = (d // 32) * 32
        dd = d % 32
        if dd < 16:
            pm, sg = hb + dd + 16, -1.0
        else:
            pm, sg = hb + dd - 16, 1.0
        rperm[base + pm, m] = 1.0
        sign[m] = sg
    t = np.arange(SEQ)
    row = (t // 64).astype(np.float32)
    col = (t % 64).astype(np.float32)
    inv = (10000.0 ** (-np.arange(16, dtype=np.float32) / 16)).astype(np.float32)
    ang_r = row[:, None] * inv[None, :]
    ang_c = col[:, None] * inv[None, :]
    ang = np.concatenate([ang_r, ang_r, ang_c, ang_c], axis=-1).astype(np.float32)
    cos = np.cos(ang).astype(np.float32).T
    sin = np.sin(ang).astype(np.float32).T
    cosT = np.ascontiguousarray(np.concatenate([cos, cos], axis=0))
    sinT = np.ascontiguousarray(np.concatenate([sin, sin], axis=0) * sign[:, None]).astype(np.float32)
    a = np.arange(128)
    mprev = (a[:, None] >= a[None, :]).astype(np.float32)
    mnext = (a[:, None] <= a[None, :]).astype(np.float32)
    same = (a[:, None] // 64) == (a[None, :] // 64)
    bmf = (same & (a[:, None] <= a[None, :])).astype(np.float32)
    bmb = (same & (a[:, None] >= a[None, :])).astype(np.float32)
    sel = np.zeros((64, 8, 128), np.float32)
    for d in range(2):
        for h in range(4):
            sel[d * 32 + h, d * 4 + h, :] = 1.0
    return dict(ident=ident, rperm=rperm, cosT=cosT, sinT=sinT, mprev=mprev, mnext=mnext, bmf=bmf, bmb=bmb, sel=sel)


def _prep(inp):
    f = lambda a: np.ascontiguousarray(np.asarray(a, dtype=np.float32))
    w_in = f(inp["w_in"])
    L = w_in.shape[0]
    qa = w_in[:, :, 0:512]
    ka = w_in[:, :, 512:640]
    va = w_in[:, :, 640:768]
    glu = w_in[:, :, 768:1792]
    qkm = w_in[:, :, 1792:2816]
    vm = w_in[:, :, 2816:3328]
    om = w_in[:, :, 3328:3840]
    gm = w_in[:, :, 3840:3856]
    br = w_in[:, :, 3856:6928]
    gch = np.zeros((L, D, 128), np.float32)
    gch[:, :, 0:4] = gm[:, :, 0:4]
    gch[:, :, 32:36] = gm[:, :, 4:8]
    gch[:, :, 64:68] = gm[:, :, 8:12]
    gch[:, :, 96:100] = gm[:, :, 12:16]
    z64 = np.zeros((L, D, 64), np.float32)
    w_fm = np.concatenate([qa, ka[:, :, 0:64], z64, z64, ka[:, :, 0:64], ka[:, :, 64:128], z64, z64, ka[:, :, 64:128],
                           glu, qkm, br, gch], axis=2)
    assert w_fm.shape[2] == NFM * 128
    w_tm = np.concatenate([va, vm, om], axis=2)
    bmg = f(inp["b_mgate"])
    bm = np.zeros((L, 64, 2), np.float32)
    bm[:, 0:4, 0] = bmg[:, 0:4]
    bm[:, 32:36, 0] = bmg[:, 4:8]
    bm[:, 0:4, 1] = bmg[:, 8:12]
    bm[:, 32:36, 1] = bmg[:, 12:16]
    cv = np.zeros((L, 512, 34), np.float32)
    cv[:, :, 0:31] = np.transpose(f(inp["w_conv_dw"]), (0, 2, 1))
    cv[:, :, 31] = f(inp["b_conv_dw"])
    cv[:, :, 32] = f(inp["g_conv_ln"])
    cv[:, :, 33] = f(inp["b_conv_ln"])
    cvec = np.ascontiguousarray(cv.reshape(L, 4, 128, 34).transpose(0, 2, 1, 3))
    wm = np.ascontiguousarray(np.transpose(f(inp["w_mconv"]), (0, 2, 1)).reshape(L, 8, 128, 3).transpose(0, 2, 1, 3))
    gmn = np.ascontiguousarray(f(inp["g_mlstm_norm"]).reshape(L, 4, 128).transpose(0, 2, 1))
    common = dict(
        w_ada=f(inp["w_ada"]), b_ada=f(inp["b_ada"]), g_norm_mix=f(inp["g_norm_mix"]), g_norm_ffn=f(inp["g_norm_ffn"]),
        w_fm=np.ascontiguousarray(w_fm), w_tm=np.ascontiguousarray(w_tm), bm=bm, att_sink=f(inp["att_sink"]),
        w_att_out=f(inp["w_att_out"]), cvec=cvec, w_conv_pw=f(inp["w_conv_pw"]), wm=wm, gmn=gmn,
        w_mlstm_out=f(inp["w_mlstm_out"]), w_out=f(inp["w_out"]), w_ff_gate=f(inp["w_ff_gate"]),
        w_ff_up=f(inp["w_ff_up"]), w_ff_down=f(inp["w_ff_down"]), g_final=f(inp["g_final"]))
    common.update(_consts())
    x = f(inp["x"])
    ctx = f(inp["ctx"])
    c = f(inp["c"])
    c_ctx = f(inp["c_ctx"])
    maps = []
    for core in range(8):
        b = core % 4
        cc = np.stack([c[b], c_ctx], axis=-1).reshape(8, 128, 2).transpose(1, 0, 2)
        m = dict(common)
        m["x"] = np.ascontiguousarray(x[b])
        m["ctx"] = np.ascontiguousarray(ctx[b])
        m["cc"] = np.ascontiguousarray(cc)
        maps.append(m)
    return maps


_NC_CACHE = {}


def kernel(**inputs):
    maps = _prep(inputs)
    if "nc" not in _NC_CACHE:
        _NC_CACHE["nc"] = build()[0]
    nc = _NC_CACHE["nc"]
    res = run_bass_kernel_spmd(nc, maps, core_ids=list(range(8)))
    outs = [np.asarray(res.results[b]["out"], dtype=np.float32) for b in range(4)]
    return np.stack(outs, axis=0)
```
